# Optimizing a Trainium2 kernel written in Bass

```python
import math
import jax, jax.numpy as jnp
from jax import lax
import numpy as np

D_MODEL = 4096
BATCH = 2
SEQ = 8192
DEPTH = 2

HEAD_DIM = 128
BLOCK = 128
NORM_EPS = 1e-6
DIL_GROUPS = ((128, 1), (512, 4), (2048, 16))
A_HEADS_PER_GROUP = 4
A_HEADS = A_HEADS_PER_GROUP * len(DIL_GROUPS)
A_OUT = A_HEADS_PER_GROUP * HEAD_DIM
B_HEADS = 8
B_DK = 64
B_DV = 128
B_GATE_RANK = 16
B_GATE_TAU = 16.0
B_CHUNK = 64
B_OUT = B_HEADS * B_DV
C_Q_HEADS = 12
C_KV_HEADS = 3
C_WINDOW = 128
C_OUT = C_Q_HEADS * HEAD_DIM
REL_BUCKETS = 32
REL_MAX_DIST = 2048
REL_HEADS = A_HEADS + C_Q_HEADS
D_FF = 11008
CONV_W = 3
IN_WIDTHS = (A_HEADS * HEAD_DIM, A_HEADS * HEAD_DIM, A_HEADS * HEAD_DIM,
             B_HEADS * B_DK, B_HEADS * B_DK, B_HEADS * B_DV, B_HEADS * B_DV, B_GATE_RANK,
             C_Q_HEADS * HEAD_DIM, C_KV_HEADS * HEAD_DIM, C_KV_HEADS * HEAD_DIM,
             D_MODEL, D_MODEL, D_MODEL)
N_IN = sum(IN_WIDTHS)

kernel_name = "hybrid_gated_dilated_gla_swa_convffn"


def _rmsnorm(x, g):
    x32 = x.astype(jnp.float32)
    y = x32 * lax.rsqrt(jnp.mean(x32 * x32, axis=-1, keepdims=True) + NORM_EPS)
    return (y * g.astype(jnp.float32)).astype(x.dtype)


def _t5_bucket(dist):
    dist = jnp.maximum(dist, 0)
    max_exact = REL_BUCKETS // 2
    far = max_exact + (jnp.log(jnp.maximum(dist, 1).astype(jnp.float32) / max_exact)
                       / math.log(REL_MAX_DIST / max_exact)
                       * (REL_BUCKETS - max_exact)).astype(jnp.int32)
    return jnp.where(dist < max_exact, dist, jnp.minimum(far, REL_BUCKETS - 1))


def _band_mask(nb, max_steps):
    qi = jnp.arange(BLOCK)[:, None]
    kj = jnp.arange(2 * BLOCK)[None, :]
    rel = qi + BLOCK - kj
    band = (rel >= 0) & (rel <= max_steps)
    valid = (jnp.arange(nb)[:, None, None] > 0) | (kj >= BLOCK)[None]
    return rel, band[None] & valid


def _with_prev_block(t, axis):
    pad = [(0, 0)] * t.ndim
    pad[axis] = (1, 0)
    prev = lax.slice_in_dim(jnp.pad(t, pad), 0, t.shape[axis], axis=axis)
    return jnp.concatenate([prev, t], axis=axis + 1)


def _dilated_group(q, k, v, bias_tab, window, dilation):
    b, s, h, e = q.shape
    span = window // dilation
    L = s // dilation
    nb = -(-L // BLOCK)
    Lp = nb * BLOCK

    def to_stream(t):
        t = t.reshape(b, L, dilation, h, e).transpose(0, 2, 1, 3, 4)
        t = jnp.pad(t, ((0, 0), (0, 0), (0, Lp - L), (0, 0), (0, 0)))
        return t.reshape(b, dilation, nb, BLOCK, h, e)

    qs = to_stream(q)
    kk = _with_prev_block(to_stream(k), 2)
    vv = _with_prev_block(to_stream(v), 2)
    rel, mask = _band_mask(nb, span)
    bias = bias_tab.astype(jnp.float32)[_t5_bucket(rel * dilation)].transpose(2, 0, 1)
    logits = jnp.einsum('bdnqhe,bdnkhe->bdnhqk', qs, kk,
                        preferred_element_type=jnp.float32) * (e ** -0.5) + bias
    logits = jnp.where(mask[:, None], logits, -jnp.inf)
    lse = jax.nn.logsumexp(logits, axis=-1)
    p = jnp.exp(logits - lse[..., None])
    o = jnp.einsum('bdnhqk,bdnkhe->bdnqhe', p.astype(v.dtype), vv)
    o = o.reshape(b, dilation, Lp, h, e)[:, :, :L].transpose(0, 2, 1, 3, 4).reshape(b, s, h, e)
    lse = lse.transpose(0, 1, 2, 4, 3).reshape(b, dilation, Lp, h)[:, :, :L]
    lse = lse.transpose(0, 2, 1, 3).reshape(b, s, h)
    return o, lse


def _swa_sinks(q, k, v, bias_tab, sinks):
    b, s, hq, e = q.shape
    hkv = k.shape[2]
    g = hq // hkv
    nb = s // BLOCK
    qb = q.reshape(b, nb, BLOCK, hkv, g, e)
    kk = _with_prev_block(k.reshape(b, nb, BLOCK, hkv, e), 1)
    vv = _with_prev_block(v.reshape(b, nb, BLOCK, hkv, e), 1)
    rel, mask = _band_mask(nb, C_WINDOW - 1)
    bias = bias_tab.astype(jnp.float32)[_t5_bucket(rel)].transpose(2, 0, 1).reshape(hkv, g, BLOCK, 2 * BLOCK)
    logits = jnp.einsum('bnqhge,bnkhe->bnhgqk', qb, kk,
                        preferred_element_type=jnp.float32) * (e ** -0.5) + bias
    logits = jnp.where(mask[:, None, None], logits, -jnp.inf)
    sink = sinks.astype(jnp.float32).reshape(hkv, g, 1, 1)
    m = jnp.maximum(jnp.max(logits, axis=-1, keepdims=True), sink)
    w = jnp.exp(logits - m)
    p = w / (jnp.sum(w, axis=-1, keepdims=True) + jnp.exp(sink - m))
    o = jnp.einsum('bnhgqk,bnkhe->bnqhge', p.astype(v.dtype), vv)
    return o.reshape(b, s, hq * e)


def _gla(q, k, v, log_a):
    b, s, h, dk = q.shape
    dv = v.shape[-1]
    nc = s // B_CHUNK
    q = q.reshape(b, nc, B_CHUNK, h, dk) * (dk ** -0.5)
    k = k.reshape(b, nc, B_CHUNK, h, dk)
    v = v.reshape(b, nc, B_CHUNK, h, dv)
    cum = lax.cumsum(log_a.reshape(b, nc, B_CHUNK, h, dk), axis=2)
    last = cum[:, :, -1:]
    q_dec = q * jnp.exp(cum)
    k_inv = k * jnp.exp(-cum)
    k_out = k * jnp.exp(last - cum)
    causal = jnp.tril(jnp.ones((B_CHUNK, B_CHUNK), dtype=bool))
    att = jnp.where(causal, jnp.einsum('bnihk,bnjhk->bnhij', q_dec, k_inv), 0.0)
    o_intra = jnp.einsum('bnhij,bnjhv->bnihv', att, v)
    kv_new = jnp.einsum('bnjhk,bnjhv->nbhkv', k_out, v)
    decay = jnp.exp(last[:, :, 0]).transpose(1, 0, 2, 3)

    def step(state, inp):
        dec, upd = inp
        return state * dec[..., None] + upd, state

    _, prev = lax.scan(step, jnp.zeros((b, h, dk, dv), jnp.float32), (decay, kv_new))
    o_inter = jnp.einsum('bnihk,nbhkv->bnihv', q_dec, prev)
    return (o_intra + o_inter).reshape(b, s, h, dv)


def _causal_dwconv(u, w, bias):
    s = u.shape[1]
    up = jnp.pad(u, ((0, 0), (CONV_W - 1, 0), (0, 0)))
    y = bias
    for kk in range(CONV_W):
        y = y + w[kk] * up[:, kk:kk + s]
    return y


def _layer(x, rel_bias, w_in, w_gla_gate, b_gla_gate, gla_norm, attn_sinks,
           w_br_a, w_br_b, w_br_c, w_out, g_pre_mix, g_post_mix, g_pre_ffn, g_post_ffn,
           w_up, conv_w, conv_b, w_down):
    b, s, _ = x.shape
    xn = _rmsnorm(x, g_pre_mix)
    proj = xn @ w_in
    splits = np.cumsum(IN_WIDTHS)[:-1].tolist()
    (aq, ak, av, bq, bk, bv, br, bg, cq, ck, cv, ga, gb, gc) = jnp.split(proj, splits, axis=-1)

    aq = aq.reshape(b, s, A_HEADS, HEAD_DIM)
    ak = ak.reshape(b, s, A_HEADS, HEAD_DIM)
    av = av.reshape(b, s, A_HEADS, HEAD_DIM)
    outs, lses = [], []
    for gi, (window, dilation) in enumerate(DIL_GROUPS):
        hs = slice(gi * A_HEADS_PER_GROUP, (gi + 1) * A_HEADS_PER_GROUP)
        o, l = _dilated_group(aq[:, :, hs], ak[:, :, hs], av[:, :, hs], rel_bias[:, hs], window, dilation)
        outs.append(o)
        lses.append(l)
    alpha = jax.nn.softmax(jnp.stack(lses), axis=0)
    o_a = jnp.sum(alpha[..., None] * jnp.stack(outs), axis=0).astype(x.dtype).reshape(b, s, A_OUT)

    log_a = jax.nn.log_sigmoid((bg @ w_gla_gate + b_gla_gate).astype(jnp.float32)) / B_GATE_TAU
    o_b = _gla(bq.reshape(b, s, B_HEADS, B_DK).astype(jnp.float32),
               bk.reshape(b, s, B_HEADS, B_DK).astype(jnp.float32),
               bv.reshape(b, s, B_HEADS, B_DV).astype(jnp.float32),
               log_a.reshape(b, s, B_HEADS, B_DK))
    o_b = _rmsnorm(o_b, gla_norm) * jax.nn.silu(br.astype(jnp.float32)).reshape(b, s, B_HEADS, B_DV)
    o_b = o_b.astype(x.dtype).reshape(b, s, B_OUT)

    o_c = _swa_sinks(cq.reshape(b, s, C_Q_HEADS, HEAD_DIM),
                     ck.reshape(b, s, C_KV_HEADS, HEAD_DIM),
                     cv.reshape(b, s, C_KV_HEADS, HEAD_DIM),
                     rel_bias[:, A_HEADS:], attn_sinks).astype(x.dtype)

    merged = (jax.nn.sigmoid(ga) * (o_a @ w_br_a)
              + jax.nn.sigmoid(gb) * (o_b @ w_br_b)
              + jax.nn.sigmoid(gc) * (o_c @ w_br_c))
    x = x + _rmsnorm(merged @ w_out, g_post_mix)

    hn = _rmsnorm(x, g_pre_ffn)
    u = _causal_dwconv(hn @ w_up, conv_w, conv_b)
    gate, up = jnp.split(u, 2, axis=-1)
    f = (jax.nn.silu(gate) * up) @ w_down
    return x + _rmsnorm(f, g_post_ffn)


def setup_inputs(seed: int = 0) -> dict:
    key = jax.random.key(seed)
    ks = jax.random.split(key, 20)
    d = D_MODEL

    def nrm(k, shape, scale):
        return jax.random.normal(k, shape, jnp.float32) * scale

    return {
        "x": nrm(ks[0], (BATCH, SEQ, d), 1.0),
        "rel_bias": nrm(ks[1], (REL_BUCKETS, REL_HEADS), 0.5),
        "w_in": nrm(ks[2], (DEPTH, d, N_IN), d ** -0.5),
        "w_gla_gate": nrm(ks[3], (DEPTH, B_GATE_RANK, B_HEADS * B_DK), B_GATE_RANK ** -0.5),
        "b_gla_gate": nrm(ks[4], (DEPTH, B_HEADS * B_DK), 0.1),
        "gla_norm": 1.0 + nrm(ks[5], (DEPTH, B_DV), 0.02),
        "attn_sinks": nrm(ks[6], (DEPTH, C_Q_HEADS), 0.5),
        "w_br_a": nrm(ks[7], (DEPTH, A_OUT, d), A_OUT ** -0.5),
        "w_br_b": nrm(ks[8], (DEPTH, B_OUT, d), B_OUT ** -0.5),
        "w_br_c": nrm(ks[9], (DEPTH, C_OUT, d), C_OUT ** -0.5),
        "w_out": nrm(ks[10], (DEPTH, d, d), d ** -0.5),
        "g_pre_mix": 1.0 + nrm(ks[11], (DEPTH, d), 0.02),
        "g_post_mix": 1.0 + nrm(ks[12], (DEPTH, d), 0.02),
        "g_pre_ffn": 1.0 + nrm(ks[13], (DEPTH, d), 0.02),
        "g_post_ffn": 1.0 + nrm(ks[14], (DEPTH, d), 0.02),
        "w_up": nrm(ks[15], (DEPTH, d, 2 * D_FF), d ** -0.5),
        "conv_w": nrm(ks[16], (DEPTH, CONV_W, 2 * D_FF), CONV_W ** -0.5),
        "conv_b": nrm(ks[17], (DEPTH, 2 * D_FF), 0.01),
        "w_down": nrm(ks[18], (DEPTH, D_FF, d), D_FF ** -0.5),
    }


def reference(x, rel_bias, w_in, w_gla_gate, b_gla_gate, gla_norm, attn_sinks,
              w_br_a, w_br_b, w_br_c, w_out, g_pre_mix, g_post_mix, g_pre_ffn, g_post_ffn,
              w_up, conv_w, conv_b, w_down):
    h = x
    for l in range(DEPTH):
        h = _layer(h, rel_bias, w_in[l], w_gla_gate[l], b_gla_gate[l], gla_norm[l], attn_sinks[l],
                   w_br_a[l], w_br_b[l], w_br_c[l], w_out[l], g_pre_mix[l], g_post_mix[l],
                   g_pre_ffn[l], g_post_ffn[l], w_up[l], conv_w[l], conv_b[l], w_down[l])
    return h
```

```python
import numpy as np
import ml_dtypes
from contextlib import ExitStack
import concourse.bass as bass
import concourse.mybir as mybir
from concourse.bass_utils import run_bass_kernel_spmd

F32 = mybir.dt.float32
BF16 = mybir.dt.bfloat16
ALU = mybir.AluOpType
AF = mybir.ActivationFunctionType

HD = 128
A_HEADS = 12
A_SLOTS = 4
DIL = ((128, 1), (512, 4), (2048, 16))
B_HEADS = 8
B_DK = 64
B_DV = 128
B_RANK = 16
B_TAU = 16.0
CH = 64
C_QH = 12
C_KVH = 3
REL_BUCKETS = 32
REL_MAX_DIST = 2048
EPS = 1e-6
NEG = -30000.0


class Cfg:
    def __init__(self, D=4096, S=8192, DFF=11008, NL=2, NB=2, debug=False):
        self.D, self.S, self.DFF, self.NL, self.NB, self.debug = D, S, DFF, NL, NB, debug
        self.KC = D // 128
        self.widths = (1536, 1536, 1536, 512, 512, 1024, 1024, 16, 1536, 384, 384, D, D, D)
        self.offs = np.concatenate([[0], np.cumsum(self.widths)]).tolist()
        self.NIN = self.offs[-1]
        assert D % 512 == 0 and S % 2048 == 0 and DFF % 256 == 0


class Sched:
    ENG = ("pe", "dve", "act", "pool", "sp")
    NDMA = 12

    def __init__(self, nc, ctx):
        self.nc = nc
        self.sem = {e: ctx.enter_context(nc.semaphore("c_" + e)) for e in self.ENG}
        self.count = {e: 0 for e in self.ENG}
        self.dsem = {q: [ctx.enter_context(nc.semaphore("d_%s%d" % (q, i))) for i in range(self.NDMA)]
                     for q in ("sp", "pool", "act")}
        self.dcnt = {q: 0 for q in self.dsem}
        self.waited = {e: {} for e in self.ENG}
        self.lastw = {}
        self.readers = {}
        self.streams = {e: [] for e in self.ENG}
        self.bar = []
        self.ninst = 0

    def _deps(self, eng, reads, writes):
        deps = list(self.bar)
        for k in reads:
            t = self.lastw.get(k)
            if t is not None:
                deps.append(t)
        for k in writes:
            t = self.lastw.get(k)
            if t is not None:
                deps.append(t)
            deps.extend(self.readers.get(k, ()))
        return deps

    def _reduce(self, eng, deps):
        best = {}
        w = self.waited[eng]
        for (s, v, src) in deps:
            if src == "pe" and eng == "pe":
                continue
            if w.get(id(s), 0) >= v:
                continue
            if best.get(id(s), (None, 0))[1] < v:
                best[id(s)] = (s, v)
        out = []
        for sid, (s, v) in best.items():
            w[sid] = v
            out.append((s, v))
        return out

    def _record(self, tok, reads, writes):
        for k in writes:
            self.lastw[k] = tok
            self.readers[k] = []
        for k in reads:
            if k not in writes:
                self.readers.setdefault(k, []).append(tok)

    def op(self, eng, fn, reads=(), writes=()):
        waits = self._reduce(eng, self._deps(eng, reads, writes))
        self.count[eng] += 1
        tok = (self.sem[eng], self.count[eng], eng)
        self.streams[eng].append((waits, fn, self.sem[eng], 1))
        self._record(tok, reads, writes)
        return tok

    def dma(self, q, out, in_, reads=(), writes=(), slow=False):
        i = self.dcnt[q]
        self.dcnt[q] += 1
        s = self.dsem[q][i % self.NDMA]
        rnd = i // self.NDMA
        deps = self._deps(q, reads, writes)
        if rnd > 0:
            deps.append((s, 16 * rnd, "dma"))
        waits = self._reduce(q, deps)
        tok = (s, 16 * (rnd + 1), "dma")
        if slow:
            self.streams[q].append((waits, lambda e, out=out, in_=in_: e.dma_start(out=out, in_=in_, allow_slow_non_contiguous=True), s, 16))
        else:
            self.streams[q].append((waits, lambda e, out=out, in_=in_: e.dma_start(out=out, in_=in_), s, 16))
        self._record(tok, reads, writes)
        return tok

    def barrier(self):
        toks = [(self.sem[e], self.count[e], e) for e in self.ENG if self.count[e] > 0]
        for q in self.dsem:
            n = self.dcnt[q]
            for j in range(self.NDMA):
                cnt = (n - j + self.NDMA - 1) // self.NDMA if n > j else 0
                if cnt > 0:
                    toks.append((self.dsem[q][j], 16 * cnt, "dma"))
        self.bar = toks
        self.lastw = {}
        self.readers = {}

    def emit(self, final=False):
        if final:
            self.barrier()
            for e in self.ENG:
                waits = self._reduce(e, list(self.bar))
                if waits:
                    self.streams[e].append((waits, None, None, 0))
        nc = self.nc
        streams = self.streams
        total = sum(len(v) for v in streams.values())
        self.ninst += total

        def replay(name, e):
            for (waits, fn, s, inc) in streams[name]:
                for (ws, wv) in waits:
                    e.wait_ge(ws, wv)
                if fn is not None:
                    ins = fn(e)
                    ins.then_inc(s, inc)

        with nc.Block() as block:
            if streams["pe"]:
                @block.tensor
                def _(e):
                    replay("pe", e)
            if streams["dve"]:
                @block.vector
                def _(e):
                    replay("dve", e)
            if streams["act"]:
                @block.scalar
                def _(e):
                    replay("act", e)
            if streams["pool"]:
                @block.gpsimd
                def _(e):
                    replay("pool", e)
            if streams["sp"]:
                @block.sync
                def _(e):
                    replay("sp", e)
        self.streams = {e: [] for e in self.ENG}


def mm_group(sc, out_ap, pairs, reads, writes):
    def fn(e, pairs=pairs, out_ap=out_ap):
        n = len(pairs)
        ins = None
        for i, (l, r) in enumerate(pairs):
            ins = e.matmul(out_ap, l, r, start=(i == 0), stop=(i == n - 1))
        return ins
    return sc.op("pe", fn, reads, writes)


def _t5_bucket(dist):
    dist = np.maximum(dist, 0)
    max_exact = REL_BUCKETS // 2
    far = max_exact + (np.log(np.maximum(dist, 1).astype(np.float32) / max_exact)
                       / np.float32(np.log(REL_MAX_DIST / max_exact))
                       * (REL_BUCKETS - max_exact)).astype(np.int32)
    return np.where(dist < max_exact, dist, np.minimum(far, REL_BUCKETS - 1))


def _rel_kq():
    k = np.arange(128)[:, None]
    q = np.arange(128)[None, :]
    rel_prev = q + 128 - k
    rel_cur = q - k
    return np.concatenate([rel_prev, rel_cur], axis=1)


def host_tables():
    rel = _rel_kq()
    kinds = [(128, d) for (_, d) in DIL] + [(127, 1)]
    bucket = []
    maskc = []
    for (span, d) in kinds:
        bucket.append(_t5_bucket(rel * d))
        ok = (rel >= 0) & (rel <= span)
        maskc.append(np.where(ok, 0.0, NEG).astype(np.float32))
    return bucket, np.stack(maskc)


def gla_masks():
    j = np.arange(128)[:, None]
    i = np.arange(128)[None, :]
    same = (j // CH) == (i // CH)
    tri = (same & (j <= i)).astype(np.float32)
    upp = (same & (j > i)).astype(np.float32)
    jj = np.arange(CH)[:, None]
    ii = np.arange(CH)[None, :]
    m64 = (jj <= ii).astype(np.float32)
    return tri, upp, np.tile(m64, (1, B_HEADS))


class Prog:
    def __init__(self, cfg):
        self.cfg = cfg
        self.nc = bass.Bass(target_bir_lowering=False)
        self.ctx = ExitStack()
        self.sc = Sched(self.nc, self.ctx)
        self.dbg = {}

    def din(self, name, shape, dt=F32):
        return self.nc.dram_tensor(name, list(shape), dt, kind="ExternalInput").ap()

    def dout(self, name, shape, dt=F32):
        return self.nc.dram_tensor(name, list(shape), dt, kind="ExternalOutput").ap()

    def dscr(self, name, shape, dt=BF16, dbg=False):
        if dbg and self.cfg.debug:
            t = self.nc.dram_tensor(name, list(shape), dt, kind="ExternalOutput").ap()
            self.dbg[name] = t
            return t
        return self.nc.dram_tensor(name, list(shape), dt).ap()


def build(cfg):
    P = Prog(cfg)
    nc, sc = P.nc, P.sc
    D, S, DFF, NL, KC = cfg.D, cfg.S, cfg.DFF, cfg.NL, cfg.KC
    KF = DFF // 128
    NT = S // 512
    offs = cfg.offs

    xT = P.din("xT", [D, S])
    outT = P.dout("outT", [D, S])
    w_in = P.din("w_in", [NL, D, cfg.NIN])
    w_gate = P.din("w_gla_gate", [NL, B_RANK, 512])
    b_gate = P.din("b_gla_gate", [NL, 512])
    gla_norm = P.din("gla_norm", [NL, 128])
    sinks = P.din("attn_sinks", [NL, C_QH])
    w_bra = P.din("w_br_a", [NL, 512, D])
    w_brb = P.din("w_br_b", [NL, 1024, D])
    w_brc = P.din("w_br_c", [NL, 1536, D])
    w_out = P.din("w_out", [NL, D, D])
    g_in = {n: P.din(n, [NL, D]) for n in ("g_pre_mix", "g_post_mix", "g_pre_ffn", "g_post_ffn")}
    w_up = P.din("w_up", [NL, D, 2 * DFF])
    conv_w = P.din("conv_w", [NL, 3, 2 * DFF])
    conv_b = P.din("conv_b", [NL, 2 * DFF])
    w_down = P.din("w_down", [NL, DFF, D])
    biasg = P.din("biasg", [24, 128, 256])
    maskc = P.din("maskc", [4, 128, 256])
    tri_in = P.din("gla_tri", [128, 128])
    upp_in = P.din("gla_upp", [128, 128])
    m64_in = P.din("gla_m64", [CH, CH * B_HEADS])

    o = offs
    in_tiles = []
    for i in range(3):
        in_tiles.append(("F", o[0] + 512 * i, 512, "aqT", 512 * i, None))
    for i in range(3):
        in_tiles.append(("F", o[1] + 512 * i, 512, "akT", 512 * i, None))
    for i in range(3):
        in_tiles.append(("T", o[2] + 512 * i, 512, "av", 512 * i, None))
    in_tiles.append(("F", o[3], 512, "bqT", 0, None))
    in_tiles.append(("F", o[4], 512, "bkT", 0, None))
    in_tiles.append(("T", o[4], 512, "bk", 0, None))
    for i in range(2):
        in_tiles.append(("T", o[5] + 512 * i, 512, "bv", 512 * i, None))
    for i in range(2):
        in_tiles.append(("F", o[6] + 512 * i, 512, "brT", 512 * i, AF.Silu))
    in_tiles.append(("F", o[7], 16, "bgT", 0, None))
    for i in range(3):
        in_tiles.append(("F", o[8] + 512 * i, 512, "cqT", 512 * i, None))
    in_tiles.append(("F", o[9], 384, "ckT", 0, None))
    in_tiles.append(("T", o[10], 384, "cv", 0, None))
    for gi, gname in enumerate(("gaT", "gbT", "gcT")):
        for i in range(D // 512):
            in_tiles.append(("F", o[11 + gi] + 512 * i, 512, gname, 512 * i, AF.Sigmoid))
    NIT = len(in_tiles)

    NUT = DFF // 256
    Wb_in = [P.dscr("Wb_in%d" % l, [NIT, 128, KC * 512]) for l in range(NL)]
    Wb_br = [P.dscr("Wb_br%d" % l, [D // 512, 128, 24 * 512]) for l in range(NL)]
    Wb_out = [P.dscr("Wb_out%d" % l, [D // 512, 128, KC * 512]) for l in range(NL)]
    Wb_up = [P.dscr("Wb_up%d" % l, [NUT, 128, KC * 512]) for l in range(NL)]
    Wb_dn = [P.dscr("Wb_dn%d" % l, [D // 128, 128, KF * 128]) for l in range(NL)]

    XN = P.dscr("XN", [D, S], BF16, dbg=True)
    XA = P.dscr("XA", [D, S], F32, dbg=True)
    XB = P.dscr("XB", [D, S], F32)
    YT = P.dscr("YT", [D, S], F32, dbg=True)
    MT = P.dscr("MT", [D, S], BF16, dbg=True)
    HT = P.dscr("HT", [DFF, S], BF16, dbg=True)
    pr = {
        "aqT": P.dscr("aqT", [1536, S], BF16, dbg=True), "akT": P.dscr("akT", [1536, S], BF16),
        "av": P.dscr("av", [S, 1536], BF16, dbg=True),
        "bqT": P.dscr("bqT", [512, S]), "bkT": P.dscr("bkT", [512, S]), "bk": P.dscr("bk", [S, 512]),
        "bv": P.dscr("bv", [S, 1024]), "brT": P.dscr("brT", [1024, S], BF16, dbg=True),
        "bgT": P.dscr("bgT", [16, S]),
        "cqT": P.dscr("cqT", [1536, S]), "ckT": P.dscr("ckT", [384, S]), "cv": P.dscr("cv", [S, 384]),
        "gaT": P.dscr("gaT", [D, S], BF16, dbg=True), "gbT": P.dscr("gbT", [D, S]), "gcT": P.dscr("gcT", [D, S]),
    }
    OA = P.dscr("OA", [512, S], BF16, dbg=True)
    OB = P.dscr("OB", [1024, S], BF16, dbg=True)
    OC = P.dscr("OC", [1536, S], BF16, dbg=True)

    ctx = P.ctx

    uid = [0]

    def sb(name, shape, dt, c=None):
        uid[0] += 1
        return (c or ctx).enter_context(nc.sbuf_tensor("%s_%d" % (name, uid[0]), list(shape), dt))

    def psum_banks(c, n=8):
        uid[0] += 1
        return [c.enter_context(nc.psum_tensor("ps%d_%d" % (i, uid[0]), [128, 512], F32)) for i in range(n)]

    ones_bf = sb("ones_bf", [128, 128], BF16)
    ones_f = sb("ones_f", [128, 128], F32)
    sc.op("pool", lambda e: e.memset(ones_bf[:], 1.0), writes=["ones_bf"])
    sc.op("pool", lambda e: e.memset(ones_f[:], 1.0), writes=["ones_f"])

    def phase_norm(x_src, y_src, gpost, x_dst, gnext, xn_dst):
        TN = 128
        with ExitStack() as c:
            ps = psum_banks(c, 4)
            gp = sb("gp", [128, KC], F32, c)
            gn = sb("gn", [128, KC], F32, c)
            if gpost is not None:
                sc.dma("sp", gp[:], gpost.rearrange("(kc p) -> p kc", p=128), writes=["gp"], slow=True)
            if gnext is not None:
                sc.dma("sp", gn[:], gnext.rearrange("(kc p) -> p kc", p=128), writes=["gn"], slow=True)
            NS = 3
            xt = [sb("nx%d" % i, [128, KC, TN], F32, c) for i in range(NS)]
            yt = [sb("ny%d" % i, [128, KC, TN], F32, c) for i in range(NS)]
            xo = [sb("no%d" % i, [128, KC, TN], BF16, c) for i in range(NS)]
            sq = [sb("nsq%d" % i, [128, KC, TN], F32, c) for i in range(2)]
            red = [sb("nrd%d" % i, [128, TN], F32, c) for i in range(4)]
            rs = [sb("nrs%d" % i, [128, TN], F32, c) for i in range(4)]
            cnt = {"s": 0, "m": 0}

            def stats(src_tile, src_key):
                i = cnt["s"]
                cnt["s"] += 1
                q, r = i % 2, i % 4
                sc.op("act", lambda e, o_=sq[q][:], i_=src_tile[:]: e.activation(out=o_, in_=i_, func=AF.Square),
                      reads=[src_key], writes=[("nsq", q)])
                sc.op("dve", lambda e, o_=red[r][:], i_=sq[q][:].rearrange("p k t -> p t k"):
                      e.reduce_sum(out=o_, in_=i_, axis=mybir.AxisListType.X),
                      reads=[("nsq", q)], writes=[("nrd", r)])
                sc.op("pe", lambda e, o_=ps[r][:, 0:TN], r_=red[r][:]: e.matmul(o_, ones_f[:], r_, start=True, stop=True),
                      reads=[("nrd", r), "ones_f"], writes=[("nps", r)])
                sc.op("act", lambda e, o_=rs[r][:], i_=ps[r][:, 0:TN]:
                      e.activation(out=o_, in_=i_, func=AF.Ln, scale=1.0 / D, bias=EPS),
                      reads=[("nps", r)], writes=[("nrs", r)])
                sc.op("act", lambda e, o_=rs[r][:]: e.activation(out=o_, in_=o_, func=AF.Exp, scale=-0.5),
                      reads=[("nrs", r)], writes=[("nrs", r)])
                return r

            def scale_rows(dst_tile, dst_key, src_tile, src_key, gt, gkey, r):
                for kc in range(KC):
                    eng = "dve"
                    cnt["m"] += 1
                    wk = (dst_key, eng, kc % 2)
                    sc.op(eng, lambda e, o_=dst_tile[:, kc, :], i_=src_tile[:, kc, :], g_=gt[:, kc:kc + 1], r_=rs[r][:]:
                          e.scalar_tensor_tensor(out=o_, in0=i_, scalar=g_, in1=r_, op0=ALU.mult, op1=ALU.mult),
                          reads=[gkey, ("nrs", r)] + ([src_key] if src_key != dst_key else []), writes=[wk])
                return [(dst_key, en, u) for en in ("dve", "pool") for u in (0, 1)]

            for t in range(S // TN):
                b = t % NS
                ts = slice(t * TN, (t + 1) * TN)
                sc.dma("sp", xt[b][:], x_src.rearrange("(kc p) s -> p kc s", p=128)[:, :, ts], writes=[("nx", b), ("nxa", b)])
                if y_src is not None:
                    yk = ("ny", b)
                    ysub = [(yk, en, u) for en in ("dve", "pool") for u in (0, 1)]
                    sc.dma("sp", yt[b][:], y_src.rearrange("(kc p) s -> p kc s", p=128)[:, :, ts], writes=[yk] + ysub)
                    r = stats(yt[b], yk)
                    for kc in range(KC):
                        eng = "dve"
                        cnt["m"] += 1
                        sc.op(eng, lambda e, o_=yt[b][:, kc, :], g_=gp[:, kc:kc + 1], r_=rs[r][:]:
                              e.scalar_tensor_tensor(out=o_, in0=o_, scalar=g_, in1=r_, op0=ALU.mult, op1=ALU.mult),
                              reads=["gp", ("nrs", r), yk], writes=[(yk, eng, kc % 2)])
                    sc.op("dve", lambda e, o_=xt[b][:], y_=yt[b][:]: e.tensor_tensor(out=o_, in0=o_, in1=y_, op=ALU.add),
                          reads=ysub + [("nx", b)], writes=[("nx", b), ("nxa", b), yk])
                    sc.dma("pool", x_dst.rearrange("(kc p) s -> p kc s", p=128)[:, :, ts], xt[b][:], reads=[("nx", b)])
                if gnext is not None:
                    r = stats(xt[b], ("nx", b))
                    ok = ("no", b)
                    osub = [(ok, en, u) for en in ("dve", "pool") for u in (0, 1)]
                    for kc in range(KC):
                        eng = "dve"
                        cnt["m"] += 1
                        sc.op(eng, lambda e, o_=xo[b][:, kc, :], i_=xt[b][:, kc, :], g_=gn[:, kc:kc + 1], r_=rs[r][:]:
                              e.scalar_tensor_tensor(out=o_, in0=i_, scalar=g_, in1=r_, op0=ALU.mult, op1=ALU.mult),
                              reads=["gn", ("nrs", r), ("nxa", b)], writes=[(ok, eng, kc % 2)])
                    sc.dma("pool", xn_dst.rearrange("(kc p) s -> p kc s", p=128)[:, :, ts], xo[b][:], reads=osub)
            sc.emit()
        sc.barrier()

    def make_wfetch(c, cw_buf, kcn, nstg=4):
        stg = [sb("wst%d" % i, [128, 2048], F32, c) for i in range(nstg)]
        st = {"n": 0}
        engs = ("pool", "dve", "pool", "act")

        def fetch(first, wbuf, wkey, wtile_dram, pieces, dkey):
            wkeys = [(wkey, en) for en in ("pool", "dve", "act")]
            if not first:
                sc.dma("sp", wbuf[:], wtile_dram, reads=[dkey], writes=wkeys)
                return
            wv = wbuf[:].rearrange("p (k c) -> p k c", c=cw_buf)
            for (src, k0, kn, c0, cwp) in pieces:
                i = st["n"] % nstg
                st["n"] += 1
                sview = stg[i][:, 0:kn * cwp].rearrange("p (k c) -> p k c", c=cwp)
                sc.dma("sp", sview, src, writes=[("wst", i)])
                eng = engs[st["n"] % 4]
                dstv = wv[:, k0:k0 + kn, c0:c0 + cwp]
                if eng == "act":
                    sc.op("act", lambda e, o_=dstv, i_=sview: e.copy(out=o_, in_=i_), reads=[("wst", i)], writes=[(wkey, eng)])
                else:
                    sc.op(eng, lambda e, o_=dstv, i_=sview: e.tensor_copy(out=o_, in_=i_), reads=[("wst", i)], writes=[(wkey, eng)])
            sc.dma("pool", wtile_dram, wbuf[:], reads=wkeys, writes=[dkey])
        return fetch

    def pieces_of(src2d, col0, ncol, kcn, k_dst0=0, c_dst0=0):
        v = src2d.rearrange("(kc p) n -> p kc n", p=128)
        g = max(1, 2048 // ncol)
        out = []
        k0 = 0
        while k0 < kcn:
            kn = min(g, kcn - k0)
            out.append((v[:, k0:k0 + kn, col0:col0 + ncol], k_dst0 + k0, kn, c_dst0, ncol))
            k0 += kn
        return out

    def phase_dense(inp, kcn, wtiles, ntiles, cw, handler, wsrc, nbuf_in=2):
        with ExitStack() as c:
            ps = psum_banks(c, 8)
            xin = [sb("din%d" % i, [128, kcn, 512], BF16, c) for i in range(nbuf_in)]
            wb = [sb("dw%d" % i, [128, kcn * cw], BF16, c) for i in range(2)]
            fetch = make_wfetch(c, cw, kcn)
            f = handler(c, ps)
            nw = 0
            for tt in range(NT):
                ib = tt % nbuf_in
                sc.dma("sp", xin[ib][:], inp.rearrange("(kc p) s -> p kc s", p=128)[:, :, tt * 512:(tt + 1) * 512],
                       writes=[("din", ib)])
                for j in range(ntiles):
                    b = nw % 2
                    nw += 1
                    fetch(tt == 0, wb[b], ("dw", b), wtiles[j], wsrc(j) if tt == 0 else None, ("wdram", j))
                    f(tt, j, wb[b][:].rearrange("p (k c) -> p k c", c=cw), xin[ib], ("din", ib),
                      [(("dw", b), en) for en in ("pool", "dve", "act")])
            sc.emit()
        sc.barrier()

    def h_inproj(l):
        def mk(c, ps):
            stg = [sb("ist%d" % i, [128, 4, 512], BF16, c) for i in range(3)]
            state = {"n": 0, "pb": 0, "e": 0}

            def f(tt, j, w, xin, kin, kw):
                kind, c0, ncol, dname, doff, act = in_tiles[j]
                q = state["n"] % 3
                state["n"] += 1
                dst = pr[dname]
                if kind == "F":
                    nb = (ncol + 127) // 128
                    for cb in range(nb):
                        m = min(128, ncol - cb * 128)
                        pb = state["pb"] % 8
                        state["pb"] += 1
                        mm_group(sc, ps[pb][0:m, :], [(w[:, kc, cb * 128:cb * 128 + m], xin[:, kc, :]) for kc in range(KC)],
                                 reads=[kin] + kw, writes=[("ps", pb)])
                        if act is not None:
                            sc.op("act", lambda e, o_=stg[q][0:m, cb, :], i_=ps[pb][0:m, :], a_=act: e.activation(out=o_, in_=i_, func=a_),
                                  reads=[("ps", pb)], writes=[("ist", q)])
                        elif state["e"] % 2 == 0:
                            sc.op("act", lambda e, o_=stg[q][0:m, cb, :], i_=ps[pb][0:m, :]: e.copy(out=o_, in_=i_),
                                  reads=[("ps", pb)], writes=[("ist", q)])
                        else:
                            sc.op("dve", lambda e, o_=stg[q][0:m, cb, :], i_=ps[pb][0:m, :]: e.tensor_copy(out=o_, in_=i_),
                                  reads=[("ps", pb)], writes=[("ist", q)])
                        state["e"] += 1
                    if ncol % 128 == 0:
                        sc.dma("pool", dst[doff:doff + ncol, tt * 512:(tt + 1) * 512].rearrange("(cb p) s -> p cb s", p=128),
                               stg[q][:, 0:nb, :], reads=[("ist", q)])
                    else:
                        sc.dma("pool", dst[doff:doff + ncol, tt * 512:(tt + 1) * 512], stg[q][0:ncol, 0, :], reads=[("ist", q)])
                else:
                    for tb in range(4):
                        pb = state["pb"] % 8
                        state["pb"] += 1
                        mm_group(sc, ps[pb][:, 0:ncol], [(xin[:, kc, tb * 128:(tb + 1) * 128], w[:, kc, 0:ncol]) for kc in range(KC)],
                                 reads=[kin] + kw, writes=[("ps", pb)])
                        if state["e"] % 2 == 0:
                            sc.op("act", lambda e, o_=stg[q][:, tb, 0:ncol], i_=ps[pb][:, 0:ncol]: e.copy(out=o_, in_=i_),
                                  reads=[("ps", pb)], writes=[("ist", q)])
                        else:
                            sc.op("dve", lambda e, o_=stg[q][:, tb, 0:ncol], i_=ps[pb][:, 0:ncol]: e.tensor_copy(out=o_, in_=i_),
                                  reads=[("ps", pb)], writes=[("ist", q)])
                        state["e"] += 1
                    sc.dma("pool", dst[tt * 512:(tt + 1) * 512, doff:doff + ncol].rearrange("(tb p) n -> p tb n", p=128),
                           stg[q][:, :, 0:ncol], reads=[("ist", q)])
            return f
        return mk

    def h_f32out(dst, cw):
        def mk(c, ps):
            nb = cw // 128
            stg = [sb("fst%d" % i, [128, nb, 512], F32, c) for i in range(3)]
            state = {"n": 0, "pb": 0}

            def f(tt, j, w, xin, kin, kw):
                kcn = w.shape[1]
                q = state["n"] % 3
                state["n"] += 1
                for cb in range(nb):
                    pb = state["pb"] % 8
                    state["pb"] += 1
                    mm_group(sc, ps[pb][:, :], [(w[:, kc, cb * 128:(cb + 1) * 128], xin[:, kc, :]) for kc in range(kcn)],
                             reads=[kin] + kw, writes=[("ps", pb)])
                    if state["pb"] % 2 == 0:
                        sc.op("act", lambda e, o_=stg[q][:, cb, :], i_=ps[pb][:, :]: e.copy(out=o_, in_=i_),
                              reads=[("ps", pb)], writes=[("fst", q)])
                    else:
                        sc.op("dve", lambda e, o_=stg[q][:, cb, :], i_=ps[pb][:, :]: e.tensor_copy(out=o_, in_=i_),
                              reads=[("ps", pb)], writes=[("fst", q)])
                sc.dma("pool", dst[j * cw:(j + 1) * cw, tt * 512:(tt + 1) * 512].rearrange("(cb p) s -> p cb s", p=128),
                       stg[q][:], reads=[("fst", q)])
            return f
        return mk

    def h_ffn_up(l):
        def mk(c, ps):
            cwt = sb("cwt", [128, 3, 2 * KF], F32, c)
            cbt = sb("cbt", [128, 2 * KF], F32, c)
            sc.dma("sp", cwt[:], conv_w[l].rearrange("t (b p) -> p t b", p=128), writes=["cwt"], slow=True)
            sc.dma("sp", cbt[:], conv_b[l].rearrange("(b p) -> p b", p=128), writes=["cbt"], slow=True)
            carry = sb("carry", [128, 2 * KF, 2], F32, c)
            sc.op("pool", lambda e: e.memset(carry[:], 0.0), writes=["carry"])
            yb = [sb("yb%d" % i, [128, 514], F32, c) for i in range(4)]
            ub = [sb("ub%d" % i, [128, 512], F32, c) for i in range(4)]
            hst = [sb("hst%d" % i, [128, 2, 512], BF16, c) for i in range(3)]
            state = {"n": 0, "pb": 0, "y": 0}

            def f(tt, j, w, xin, kin, kw):
                q = state["n"] % 3
                state["n"] += 1
                for half in range(2):
                    res = []
                    for which in range(2):
                        blk = (KF * which) + 2 * j + half
                        wc = which * 256 + half * 128
                        pb = state["pb"] % 8
                        state["pb"] += 1
                        mm_group(sc, ps[pb][:, :], [(w[:, kc, wc:wc + 128], xin[:, kc, :]) for kc in range(KC)],
                                 reads=[kin] + kw, writes=[("ps", pb)])
                        yi = state["y"] % 4
                        state["y"] += 1
                        ck = ("carry", blk)
                        sc.op("pool", lambda e, o_=yb[yi][:, 0:2], i_=carry[:, blk, :]: e.tensor_copy(out=o_, in_=i_),
                              reads=[ck, "carry"], writes=[("yb", yi)])
                        sc.op("act", lambda e, o_=yb[yi][:, 2:514], i_=ps[pb][:, :]: e.copy(out=o_, in_=i_),
                              reads=[("ps", pb)], writes=[("yb", yi)])
                        sc.op("pool", lambda e, o_=carry[:, blk, :], i_=yb[yi][:, 512:514]: e.tensor_copy(out=o_, in_=i_),
                              reads=[("yb", yi)], writes=[ck])
                        sc.op("act", lambda e, o_=ub[yi][:], i_=yb[yi][:, 2:514], s_=cwt[:, 2, blk:blk + 1], b_=cbt[:, blk:blk + 1]:
                              e.activation(out=o_, in_=i_, func=AF.Identity, scale=s_, bias=b_),
                              reads=[("yb", yi), "cwt", "cbt"], writes=[("ub", yi)])
                        sc.op("dve", lambda e, o_=ub[yi][:], i_=yb[yi][:, 1:513], s_=cwt[:, 1, blk:blk + 1]:
                              e.scalar_tensor_tensor(out=o_, in0=i_, scalar=s_, in1=o_, op0=ALU.mult, op1=ALU.add),
                              reads=[("yb", yi), "cwt", ("ub", yi)], writes=[("ub", yi)])
                        sc.op("dve", lambda e, o_=ub[yi][:], i_=yb[yi][:, 0:512], s_=cwt[:, 0, blk:blk + 1]:
                              e.scalar_tensor_tensor(out=o_, in0=i_, scalar=s_, in1=o_, op0=ALU.mult, op1=ALU.add),
                              reads=[("yb", yi), "cwt", ("ub", yi)], writes=[("ub", yi)])
                        res.append(yi)
                    gi, ui = res
                    sc.op("act", lambda e, o_=yb[gi][:, 0:512], i_=ub[gi][:]: e.activation(out=o_, in_=i_, func=AF.Silu),
                          reads=[("ub", gi)], writes=[("yb", gi)])
                    sc.op("dve", lambda e, o_=hst[q][:, half, :], a_=yb[gi][:, 0:512], b_=ub[ui][:]:
                          e.tensor_tensor(out=o_, in0=a_, in1=b_, op=ALU.mult),
                          reads=[("yb", gi), ("ub", ui)], writes=[("hst", q)])
                sc.dma("pool", HT[256 * j:256 * j + 256, tt * 512:(tt + 1) * 512].rearrange("(cb p) s -> p cb s", p=128),
                       hst[q][:], reads=[("hst", q)])
            return f
        return mk

    def phase_merge(l):
        with ExitStack() as c:
            ps = psum_banks(c, 8)
            xin = [sb("min%d" % i, [128, 24, 512], BF16, c) for i in range(2)]
            wb = [sb("mw%d" % i, [128, 24 * 512], BF16, c) for i in range(2)]
            gt = [sb("mg%d" % i, [128, 3, 4, 512], BF16, c) for i in range(2)]
            t1 = [sb("mt1_%d" % i, [128, 512], F32, c) for i in range(2)]
            t2 = [sb("mt2_%d" % i, [128, 512], F32, c) for i in range(2)]
            t3 = [sb("mt3_%d" % i, [128, 512], F32, c) for i in range(2)]
            stg = [sb("mst%d" % i, [128, 4, 512], BF16, c) for i in range(3)]
            fetch = make_wfetch(c, 512, 24, nstg=2)
            nw = 0
            npb = 0
            nq = 0
            gsrc = (pr["gaT"], pr["gbT"], pr["gcT"])
            for tt in range(NT):
                ib = tt % 2
                tsl = slice(tt * 512, (tt + 1) * 512)
                sc.dma("sp", xin[ib][:, 0:4, :], OA.rearrange("(kc p) s -> p kc s", p=128)[:, :, tsl], writes=[("min", ib)])
                sc.dma("sp", xin[ib][:, 4:12, :], OB.rearrange("(kc p) s -> p kc s", p=128)[:, :, tsl], writes=[("min", ib)])
                sc.dma("sp", xin[ib][:, 12:24, :], OC.rearrange("(kc p) s -> p kc s", p=128)[:, :, tsl], writes=[("min", ib)])
                for j in range(D // 512):
                    b = nw % 2
                    nw += 1
                    pcs = None
                    if tt == 0:
                        pcs = (pieces_of(w_bra[l], 512 * j, 512, 4, 0) + pieces_of(w_brb[l], 512 * j, 512, 8, 4)
                               + pieces_of(w_brc[l], 512 * j, 512, 12, 12))
                    fetch(tt == 0, wb[b], ("mw", b), Wb_br[l][j], pcs, ("wdram", j))
                    mwk = [(("mw", b), en) for en in ("pool", "dve", "act")]
                    for gi in range(3):
                        sc.dma("sp", gt[b][:, gi, :, :],
                               gsrc[gi][512 * j:512 * j + 512, tsl].rearrange("(cb p) s -> p cb s", p=128), writes=[("mg", b)])
                    w = wb[b][:].rearrange("p (k c) -> p k c", c=512)
                    q = nq % 3
                    nq += 1
                    for cb in range(4):
                        pbs = []
                        for (k0, k1) in ((0, 4), (4, 12), (12, 24)):
                            pb = npb % 8
                            npb += 1
                            mm_group(sc, ps[pb][:, :], [(w[:, kc, cb * 128:(cb + 1) * 128], xin[ib][:, kc, :]) for kc in range(k0, k1)],
                                     reads=[("min", ib)] + mwk, writes=[("ps", pb)])
                            pbs.append(pb)
                        u = cb % 2
                        for tbuf, nm, pb, gi in ((t1, "mt1", pbs[0], 0), (t2, "mt2", pbs[1], 1), (t3, "mt3", pbs[2], 2)):
                            sc.op("dve", lambda e, o_=tbuf[u][:], p_=ps[pb][:, :], g_=gt[b][:, gi, cb, :]:
                                  e.tensor_tensor(out=o_, in0=p_, in1=g_, op=ALU.mult),
                                  reads=[("ps", pb), ("mg", b)], writes=[(nm, u)])
                        sc.op("pool", lambda e, o_=t1[u][:], i_=t2[u][:]: e.tensor_tensor(out=o_, in0=o_, in1=i_, op=ALU.add),
                              reads=[("mt1", u), ("mt2", u)], writes=[("mt1", u)])
                        sc.op("pool", lambda e, o_=stg[q][:, cb, :], a_=t1[u][:], i_=t3[u][:]: e.tensor_tensor(out=o_, in0=a_, in1=i_, op=ALU.add),
                              reads=[("mt1", u), ("mt3", u)], writes=[("mst", q)])
                    sc.dma("pool", MT[512 * j:512 * j + 512, tsl].rearrange("(cb p) s -> p cb s", p=128), stg[q][:], reads=[("mst", q)])
            sc.emit()
        sc.barrier()

    def phase_attn_A(l):
        NSUP = S // 2048
        scale = float(HD) ** -0.5
        with ExitStack() as c:
            ps = psum_banks(c, 8)
            bm = sb("bmA", [128, 12, 256], F32, c)
            mk_ = sb("mkA", [128, 3, 256], F32, c)
            sc.dma("sp", bm[:], biasg[0:12].rearrange("h k q -> k h q"), writes=["bm"])
            sc.dma("sp", mk_[:], maskc[0:3].rearrange("g k q -> k g q"), writes=["mk"])
            for h in range(12):
                sc.op("dve", lambda e, o_=bm[:, h, :], m_=mk_[:, h // 4, :]: e.tensor_tensor(out=o_, in0=o_, in1=m_, op=ALU.add),
                      reads=["bm", "mk"], writes=["bm"])
            qt = [sb("aq%d" % i, [128, 2048], BF16, c) for i in range(2)]
            kt = [sb("ak%d" % i, [128, 4096], BF16, c) for i in range(2)]
            vc = [sb("avc%d" % i, [128, 16, 128], BF16, c) for i in range(2)]
            vp = [sb("avp%d" % i, [128, 16, 128], BF16, c) for i in range(2)]
            num = sb("anum", [128, 2048], F32, c)
            den = sb("aden", [128, 2048], F32, c)
            lg = [sb("alg%d" % i, [128, 512], F32, c) for i in range(4)]
            pt = [sb("apt%d" % i, [128, 512], BF16, c) for i in range(4)]
            ost = [sb("aost%d" % i, [128, 2048], BF16, c) for i in range(2)]
            nld = 0
            npb = 0
            nlg = 0
            nun = 0
            for j in range(A_SLOTS):
                for u in range(NSUP):
                    t0 = u * 2048
                    for g, (win, d) in enumerate(DIL):
                        h = 4 * g + j
                        b = nld % 2
                        nld += 1
                        hr = slice(h * 128, (h + 1) * 128)
                        sc.dma("sp", qt[b][:], pr["aqT"][hr, t0:t0 + 2048], writes=[("aq", b)])
                        sc.dma("sp", kt[b][:, 2048:4096], pr["akT"][hr, t0:t0 + 2048], writes=[("ak", b)])
                        back = {1: 128, 4: 512, 16: 2048}[d]
                        if u > 0:
                            sc.dma("sp", kt[b][:, 4096 - 2048 - back:2048], pr["akT"][hr, t0 - back:t0], writes=[("ak", b)])
                        vsrc = pr["av"][t0:t0 + 2048, hr]
                        if d == 1:
                            sc.dma("sp", vc[b][:], vsrc.rearrange("(n i) e -> i n e", i=128), writes=[("avc", b)])
                        elif d == 4:
                            for m in range(4):
                                sc.dma("sp", vc[b][:, 4 * m:4 * m + 4, :],
                                       vsrc[512 * m:512 * m + 512, :].rearrange("(i r) e -> i r e", r=4), writes=[("avc", b)])
                        else:
                            sc.dma("sp", vc[b][:], vsrc.rearrange("(i r) e -> i r e", r=16), writes=[("avc", b)])
                        if u > 0:
                            vps = pr["av"][t0 - back:t0, hr]
                            if d == 1:
                                sc.dma("sp", vp[b][:, 0, :], vps, writes=[("avp", b)])
                            elif d == 4:
                                sc.dma("sp", vp[b][:, 0:4, :], vps.rearrange("(i r) e -> i r e", r=4), writes=[("avp", b)])
                            else:
                                sc.dma("sp", vp[b][:], vps.rearrange("(i r) e -> i r e", r=16), writes=[("avp", b)])
                        blocks = []
                        if d == 1:
                            for n in range(16):
                                blocks.append((n * 128, n, (n - 1) if n > 0 else None, 0))
                        elif d == 4:
                            for m in range(4):
                                for r in range(4):
                                    blocks.append((m * 512 + r, 4 * m + r, (4 * (m - 1) + r) if m > 0 else None, r))
                        else:
                            for r in range(16):
                                blocks.append((r, r, None, r))
                        for bi in range(0, 16, 2):
                            pb_s = npb % 8
                            npb += 1
                            li = nlg % 4
                            nlg += 1
                            info = []
                            for x in range(2):
                                f, cur_idx, prev_in_cur, r = blocks[bi + x]
                                qv = qt[b][:, f:f + 127 * d + 1:d] if d > 1 else qt[b][:, f:f + 128]
                                kc_ = kt[b][:, 2048 + f:2048 + f + 127 * d + 1:d] if d > 1 else kt[b][:, 2048 + f:2048 + f + 128]
                                fp = 2048 + f - 128 * d
                                has_prev = (u > 0) or (f - 128 * d >= 0)
                                col = x * 256
                                if has_prev:
                                    kp_ = kt[b][:, fp:fp + 127 * d + 1:d] if d > 1 else kt[b][:, fp:fp + 128]
                                    mm_group(sc, ps[pb_s][:, col:col + 128], [(kp_, qv)], reads=[("ak", b), ("aq", b)], writes=[("ps", pb_s)])
                                mm_group(sc, ps[pb_s][:, col + 128:col + 256], [(kc_, qv)], reads=[("ak", b), ("aq", b)], writes=[("ps", pb_s)])
                                if prev_in_cur is not None:
                                    vprev = vc[b][:, prev_in_cur, :]
                                elif has_prev:
                                    vprev = vp[b][:, r if d > 1 else 0, :]
                                else:
                                    vprev = None
                                info.append((f, has_prev, vprev, vc[b][:, cur_idx, :], col))
                            for x in range(2):
                                f, has_prev, vprev, vcur, col = info[x]
                                c0 = col if has_prev else col + 128
                                bc0 = 0 if has_prev else 128
                                sc.op("dve", lambda e, o_=lg[li][:, c0:col + 256], p_=ps[pb_s][:, c0:col + 256], b_=bm[:, h, bc0:256]:
                                      e.scalar_tensor_tensor(out=o_, in0=p_, scalar=scale, in1=b_, op0=ALU.mult, op1=ALU.add),
                                      reads=[("ps", pb_s), "bm"], writes=[("alg", li)])
                                sc.op("act", lambda e, o_=pt[li][:, c0:col + 256], i_=lg[li][:, c0:col + 256]: e.activation(out=o_, in_=i_, func=AF.Exp),
                                      reads=[("alg", li)], writes=[("apt", li)])
                            pb_n = npb % 8
                            npb += 1
                            pb_d = npb % 8
                            npb += 1
                            for x in range(2):
                                f, has_prev, vprev, vcur, col = info[x]
                                prs_n = []
                                prs_d = []
                                if has_prev:
                                    prs_n.append((vprev, pt[li][:, col:col + 128]))
                                    prs_d.append((ones_bf[:], pt[li][:, col:col + 128]))
                                prs_n.append((vcur, pt[li][:, col + 128:col + 256]))
                                prs_d.append((ones_bf[:], pt[li][:, col + 128:col + 256]))
                                mm_group(sc, ps[pb_n][:, x * 128:(x + 1) * 128], prs_n, reads=[("apt", li), ("avc", b), ("avp", b)], writes=[("ps", pb_n)])
                                mm_group(sc, ps[pb_d][:, x * 128:(x + 1) * 128], prs_d, reads=[("apt", li), "ones_bf"], writes=[("ps", pb_d)])
                            for x in range(2):
                                f = info[x][0]
                                nv = num[:, f:f + 127 * d + 1:d] if d > 1 else num[:, f:f + 128]
                                dv = den[:, f:f + 127 * d + 1:d] if d > 1 else den[:, f:f + 128]
                                if g == 0:
                                    sc.op("act", lambda e, o_=nv, i_=ps[pb_n][:, x * 128:(x + 1) * 128]: e.copy(out=o_, in_=i_),
                                          reads=[("ps", pb_n)], writes=["anum"])
                                    sc.op("dve", lambda e, o_=dv, i_=ps[pb_d][:, x * 128:(x + 1) * 128]: e.tensor_copy(out=o_, in_=i_),
                                          reads=[("ps", pb_d)], writes=["aden"])
                                else:
                                    sc.op("dve", lambda e, o_=nv, i_=ps[pb_n][:, x * 128:(x + 1) * 128]: e.tensor_tensor(out=o_, in0=o_, in1=i_, op=ALU.add),
                                          reads=[("ps", pb_n), "anum"], writes=["anum"])
                                    sc.op("dve", lambda e, o_=dv, i_=ps[pb_d][:, x * 128:(x + 1) * 128]: e.tensor_tensor(out=o_, in0=o_, in1=i_, op=ALU.add),
                                          reads=[("ps", pb_d), "aden"], writes=["aden"])
                    ob = nun % 2
                    nun += 1
                    sc.op("dve", lambda e: e.reciprocal(out=den[:], in_=den[:]), reads=["aden"], writes=["aden"])
                    sc.op("dve", lambda e, o_=ost[ob][:]: e.tensor_tensor(out=o_, in0=num[:], in1=den[:], op=ALU.mult),
                          reads=["anum", "aden"], writes=[("aost", ob)])
                    sc.dma("pool", OA[j * 128:(j + 1) * 128, t0:t0 + 2048], ost[ob][:], reads=[("aost", ob)])
            sc.emit()
        sc.barrier()

    def phase_attn_C(l):
        NSUP = S // 2048
        scale = float(HD) ** -0.5
        with ExitStack() as c:
            ps = psum_banks(c, 8)
            bm = sb("bmC", [128, 12, 256], F32, c)
            mk_ = sb("mkC", [128, 256], F32, c)
            es = sb("esC", [128, 12], F32, c)
            sc.dma("sp", bm[:], biasg[12:24].rearrange("h k q -> k h q"), writes=["bm"])
            sc.dma("sp", mk_[:], maskc[3], writes=["mk"])
            sc.dma("sp", es[:], sinks[l].partition_broadcast(128), writes=["es"])
            for h in range(12):
                sc.op("dve", lambda e, o_=bm[:, h, :]: e.tensor_tensor(out=o_, in0=o_, in1=mk_[:], op=ALU.add),
                      reads=["bm", "mk"], writes=["bm"])
            sc.op("act", lambda e: e.activation(out=es[:], in_=es[:], func=AF.Exp), reads=["es"], writes=["es"])
            qt = [sb("cq%d" % i, [128, 2048], BF16, c) for i in range(2)]
            kt = [sb("ck%d" % i, [128, 2048 + 128], BF16, c) for i in range(2)]
            vt = [sb("cvt%d" % i, [128, 17, 128], BF16, c) for i in range(2)]
            lg = [sb("clg%d" % i, [128, 512], F32, c) for i in range(4)]
            pt = [sb("cpt%d" % i, [128, 512], BF16, c) for i in range(4)]
            dn = [sb("cdn%d" % i, [128, 256], F32, c) for i in range(4)]
            ost = [sb("cost%d" % i, [128, 2048], BF16, c) for i in range(2)]
            nld = 0
            nkv = 0
            npb = 0
            nlg = 0
            for kvh in range(C_KVH):
                for u in range(NSUP):
                    t0 = u * 2048
                    kb = nkv % 2
                    nkv += 1
                    kr = slice(kvh * 128, (kvh + 1) * 128)
                    sc.dma("sp", kt[kb][:, 128:], pr["ckT"][kr, t0:t0 + 2048], writes=[("ck", kb)])
                    sc.dma("sp", vt[kb][:, 1:17, :], pr["cv"][t0:t0 + 2048, kr].rearrange("(n i) e -> i n e", i=128), writes=[("cvt", kb)])
                    if u > 0:
                        sc.dma("sp", kt[kb][:, 0:128], pr["ckT"][kr, t0 - 128:t0], writes=[("ck", kb)])
                        sc.dma("sp", vt[kb][:, 0, :], pr["cv"][t0 - 128:t0, kr], writes=[("cvt", kb)])
                    for gq in range(4):
                        h = kvh * 4 + gq
                        b = nld % 2
                        nld += 1
                        sc.dma("sp", qt[b][:], pr["cqT"][h * 128:(h + 1) * 128, t0:t0 + 2048], writes=[("cq", b)])
                        for bi in range(0, 16, 2):
                            pb_s = npb % 8
                            npb += 1
                            li = nlg % 4
                            nlg += 1
                            info = []
                            for x in range(2):
                                n = bi + x
                                has_prev = (u > 0) or (n > 0)
                                col = x * 256
                                qv = qt[b][:, n * 128:(n + 1) * 128]
                                if has_prev:
                                    mm_group(sc, ps[pb_s][:, col:col + 128], [(kt[kb][:, n * 128:(n + 1) * 128], qv)],
                                             reads=[("ck", kb), ("cq", b)], writes=[("ps", pb_s)])
                                mm_group(sc, ps[pb_s][:, col + 128:col + 256], [(kt[kb][:, (n + 1) * 128:(n + 2) * 128], qv)],
                                         reads=[("ck", kb), ("cq", b)], writes=[("ps", pb_s)])
                                info.append((n, has_prev, col))
                            for x in range(2):
                                n, has_prev, col = info[x]
                                c0 = col if has_prev else col + 128
                                bc0 = 0 if has_prev else 128
                                sc.op("dve", lambda e, o_=lg[li][:, c0:col + 256], p_=ps[pb_s][:, c0:col + 256], b_=bm[:, h, bc0:256]:
                                      e.scalar_tensor_tensor(out=o_, in0=p_, scalar=scale, in1=b_, op0=ALU.mult, op1=ALU.add),
                                      reads=[("ps", pb_s), "bm"], writes=[("clg", li)])
                                sc.op("act", lambda e, o_=pt[li][:, c0:col + 256], i_=lg[li][:, c0:col + 256]: e.activation(out=o_, in_=i_, func=AF.Exp),
                                      reads=[("clg", li)], writes=[("cpt", li)])
                            pb_n = npb % 8
                            npb += 1
                            pb_d = npb % 8
                            npb += 1
                            for x in range(2):
                                n, has_prev, col = info[x]
                                prs_n = []
                                prs_d = []
                                if has_prev:
                                    prs_n.append((vt[kb][:, n, :], pt[li][:, col:col + 128]))
                                    prs_d.append((ones_bf[:], pt[li][:, col:col + 128]))
                                prs_n.append((vt[kb][:, n + 1, :], pt[li][:, col + 128:col + 256]))
                                prs_d.append((ones_bf[:], pt[li][:, col + 128:col + 256]))
                                mm_group(sc, ps[pb_n][:, x * 128:(x + 1) * 128], prs_n, reads=[("cpt", li), ("cvt", kb)], writes=[("ps", pb_n)])
                                mm_group(sc, ps[pb_d][:, x * 128:(x + 1) * 128], prs_d, reads=[("cpt", li), "ones_bf"], writes=[("ps", pb_d)])
                            sc.op("dve", lambda e, o_=dn[li][:], i_=ps[pb_d][:, 0:256], s_=es[:, h:h + 1]:
                                  e.tensor_scalar(out=o_, in0=i_, scalar1=s_, scalar2=None, op0=ALU.add),
                                  reads=[("ps", pb_d), "es"], writes=[("cdn", li)])
                            sc.op("dve", lambda e, o_=dn[li][:]: e.reciprocal(out=o_, in_=o_), reads=[("cdn", li)], writes=[("cdn", li)])
                            sc.op("dve", lambda e, o_=ost[b][:, bi * 128:(bi + 2) * 128], n_=ps[pb_n][:, 0:256], d_=dn[li][:]:
                                  e.tensor_tensor(out=o_, in0=n_, in1=d_, op=ALU.mult),
                                  reads=[("ps", pb_n), ("cdn", li)], writes=[("cost", b)])
                        sc.dma("pool", OC[h * 128:(h + 1) * 128, t0:t0 + 2048], ost[b][:], reads=[("cost", b)])
            sc.emit()
        sc.barrier()

    def phase_gla(l):
        NBLK = S // 128
        with ExitStack() as c:
            ps = psum_banks(c, 8)
            wg_f = sb("wg_f", [16, 512], F32, c)
            wg = sb("wg", [16, 512], BF16, c)
            bgb = sb("bgb", [128, 512], F32, c)
            tri = sb("tri", [128, 128], F32, c)
            upp = sb("upp", [128, 128], F32, c)
            m64 = sb("m64", [CH, CH * B_HEADS], F32, c)
            gn = sb("glan", [128, 1], F32, c)
            sc.dma("sp", wg_f[:], w_gate[l], writes=["wg_f"])
            sc.op("dve", lambda e: e.tensor_copy(out=wg[:], in_=wg_f[:]), reads=["wg_f"], writes=["wg"])
            sc.dma("sp", bgb[:], b_gate[l].partition_broadcast(128), writes=["bgb"])
            sc.dma("sp", tri[:], tri_in, writes=["tri"])
            sc.dma("sp", upp[:], upp_in, writes=["upp"])
            sc.dma("sp", m64[:], m64_in, writes=["m64"])
            sc.dma("sp", gn[:], gla_norm[l].rearrange("(p o) -> p o", o=1), writes=["gn"], slow=True)
            sc.op("act", lambda e: e.mul(out=gn[:], in_=gn[:], mul=float(np.sqrt(128.0))), reads=["gn"], writes=["gn"])
            state = sb("gstate", [CH, B_HEADS, 128], F32, c)
            state_bf = sb("gstate_bf", [CH, B_HEADS, 128], BF16, c)
            sc.op("pool", lambda e: e.memset(state[:], 0.0), writes=["gstate"])
            sc.op("pool", lambda e: e.memset(state_bf[:], 0.0), writes=["gstate_bf"])
            NB2 = 3
            bgt = [sb("gbg%d" % i, [16, 128], BF16, c) for i in range(NB2)]
            zt = [sb("gz%d" % i, [128, 512], F32, c) for i in range(NB2)]
            qT = [sb("gq%d" % i, [CH, B_HEADS, 128], BF16, c) for i in range(NB2)]
            kT = [sb("gk%d" % i, [CH, B_HEADS, 128], BF16, c) for i in range(NB2)]
            ktok = [sb("gkt%d" % i, [128, 512], BF16, c) for i in range(NB2)]
            v128 = [sb("gv%d" % i, [128, 1024], BF16, c) for i in range(NB2)]
            v64 = [sb("gw%d" % i, [CH, 2, 1024], BF16, c) for i in range(NB2)]
            brt = [sb("gbr%d" % i, [128, B_HEADS, 128], BF16, c) for i in range(NB2)]
            eq = [sb("geq%d" % i, [CH, B_HEADS, 128], F32, c) for i in range(NB2)]
            ek = [sb("gek%d" % i, [CH, B_HEADS, 128], F32, c) for i in range(NB2)]
            qd = [sb("gqd%d" % i, [CH, B_HEADS, 128], BF16, c) for i in range(NB2)]
            ki = [sb("gki%d" % i, [CH, B_HEADS, 128], BF16, c) for i in range(NB2)]
            ko = [sb("gko%d" % i, [128, 512], BF16, c) for i in range(NB2)]
            ed = [sb("ged%d" % i, [128, 512], F32, c) for i in range(NB2)]
            am = [sb("gam%d" % i, [CH, 512], BF16, c) for i in range(2)]
            oraw = [sb("gor%d" % i, [128, 512], F32, c) for i in range(2)]
            osq = [sb("gos%d" % i, [128, 512], F32, c) for i in range(2)]
            orn = [sb("grn%d" % i, [128, 512], F32, c) for i in range(2)]
            ost = [sb("gost%d" % i, [128, B_HEADS, 128], BF16, c) for i in range(2)]
            npb = [0]

            def nextpb():
                pb = npb[0] % 8
                npb[0] += 1
                return pb

            for blk in range(NBLK):
                b = blk % NB2
                t0 = blk * 128
                tsl = slice(t0, t0 + 128)
                sc.dma("sp", bgt[b][:], pr["bgT"][:, tsl], writes=[("gbg", b)])
                sc.dma("sp", qT[b][:], pr["bqT"][:, tsl].rearrange("(h k) s -> k h s", k=CH), writes=[("gq", b)])
                sc.dma("sp", kT[b][:], pr["bkT"][:, tsl].rearrange("(h k) s -> k h s", k=CH), writes=[("gk", b)])
                sc.dma("sp", ktok[b][:], pr["bk"][tsl, :], writes=[("gkt", b)])
                sc.dma("sp", v128[b][:], pr["bv"][tsl, :], writes=[("gv", b)])
                sc.dma("sp", v64[b][:], pr["bv"][tsl, :].rearrange("(c j) v -> j c v", j=CH), writes=[("gw", b)])
                sc.dma("sp", brt[b][:], pr["brT"][:, tsl].rearrange("(h v) s -> v h s", v=128), writes=[("gbr", b)])
                pz = nextpb()
                mm_group(sc, ps[pz][:, :], [(bgt[b][:], wg[:])], reads=[("gbg", b), "wg"], writes=[("ps", pz)])
                sc.op("dve", lambda e, o_=zt[b][:], p_=ps[pz][:, :]: e.tensor_tensor(out=o_, in0=p_, in1=bgb[:], op=ALU.add),
                      reads=[("ps", pz), "bgb"], writes=[("gz", b)])
                sc.op("act", lambda e, o_=zt[b][:]: e.activation(out=o_, in_=o_, func=AF.Exp, scale=-1.0), reads=[("gz", b)], writes=[("gz", b)])
                sc.op("act", lambda e, o_=zt[b][:]: e.activation(out=o_, in_=o_, func=AF.Ln, bias=1.0), reads=[("gz", b)], writes=[("gz", b)])
                pc = [nextpb(), nextpb()]
                for h in range(B_HEADS):
                    mm_group(sc, ps[pc[h // 4]][0:CH, (h % 4) * 128:(h % 4 + 1) * 128], [(zt[b][:, h * CH:(h + 1) * CH], tri[:])],
                             reads=[("gz", b), "tri"], writes=[("ps", pc[h // 4])])
                pd = nextpb()
                mm_group(sc, ps[pd][:, :], [(upp[:], zt[b][:])], reads=[("gz", b), "upp"], writes=[("ps", pd)])
                for hh in range(2):
                    sc.op("act", lambda e, o_=eq[b][:, 4 * hh:4 * hh + 4, :], p_=ps[pc[hh]][0:CH, :].rearrange("k (h i) -> k h i", i=128):
                          e.activation(out=o_, in_=p_, func=AF.Exp, scale=-1.0 / B_TAU), reads=[("ps", pc[hh])], writes=[("geq", b)])
                    sc.op("act", lambda e, o_=ek[b][:, 4 * hh:4 * hh + 4, :], p_=ps[pc[hh]][0:CH, :].rearrange("k (h i) -> k h i", i=128):
                          e.activation(out=o_, in_=p_, func=AF.Exp, scale=1.0 / B_TAU), reads=[("ps", pc[hh])], writes=[("gek", b)])
                sc.op("act", lambda e, o_=ed[b][:], p_=ps[pd][:, :]: e.activation(out=o_, in_=p_, func=AF.Exp, scale=-1.0 / B_TAU),
                      reads=[("ps", pd)], writes=[("ged", b)])
                sc.op("dve", lambda e, o_=qd[b][:], q_=qT[b][:], e_=eq[b][:]:
                      e.scalar_tensor_tensor(out=o_, in0=q_, scalar=float(B_DK) ** -0.5, in1=e_, op0=ALU.mult, op1=ALU.mult),
                      reads=[("gq", b), ("geq", b)], writes=[("gqd", b)])
                sc.op("pool", lambda e, o_=ki[b][:], k_=kT[b][:], e_=ek[b][:]: e.tensor_tensor(out=o_, in0=k_, in1=e_, op=ALU.mult),
                      reads=[("gk", b), ("gek", b)], writes=[("gki", b)])
                sc.op("pool", lambda e, o_=ko[b][:], k_=ktok[b][:], e_=ed[b][:]: e.tensor_tensor(out=o_, in0=k_, in1=e_, op=ALU.mult),
                      reads=[("gkt", b), ("ged", b)], writes=[("gko", b)])
                ob = blk % 2
                po = nextpb()
                po2 = nextpb()
                pos = (po, po2)
                for cc in range(2):
                    a = (2 * blk + cc) % 2
                    isl = slice(cc * CH, (cc + 1) * CH)
                    pa = nextpb()
                    for h in range(B_HEADS):
                        mm_group(sc, ps[pa][0:CH, h * CH:(h + 1) * CH], [(ki[b][:, h, isl], qd[b][:, h, isl])],
                                 reads=[("gki", b), ("gqd", b)], writes=[("ps", pa)])
                    sc.op("dve", lambda e, o_=am[a][:], p_=ps[pa][0:CH, :]: e.tensor_tensor(out=o_, in0=p_, in1=m64[:], op=ALU.mult),
                          reads=[("ps", pa), "m64"], writes=[("gam", a)])
                    for h in range(B_HEADS):
                        mm_group(sc, ps[pos[cc]][:, h * CH:(h + 1) * CH],
                                 [(v64[b][:, cc, h * 128:(h + 1) * 128], am[a][:, h * CH:(h + 1) * CH]),
                                  (state_bf[:, h, :], qd[b][:, h, isl])],
                                 reads=[("gw", b), ("gam", a), "gstate_bf", ("gqd", b)], writes=[("ps", pos[cc])])
                    pk = [nextpb(), nextpb()]
                    for h in range(B_HEADS):
                        mm_group(sc, ps[pk[h // 4]][0:CH, (h % 4) * 128:(h % 4 + 1) * 128],
                                 [(ko[b][cc * CH:(cc + 1) * CH, h * CH:(h + 1) * CH], v128[b][cc * CH:(cc + 1) * CH, h * 128:(h + 1) * 128])],
                                 reads=[("gko", b), ("gv", b)], writes=[("ps", pk[h // 4])])
                    for h in range(B_HEADS):
                        sc.op("dve", lambda e, o_=state[:, h, :], d_=eq[b][:, h, cc * CH + CH - 1:cc * CH + CH],
                              p_=ps[pk[h // 4]][0:CH, (h % 4) * 128:(h % 4 + 1) * 128]:
                              e.scalar_tensor_tensor(out=o_, in0=o_, scalar=d_, in1=p_, op0=ALU.mult, op1=ALU.add),
                              reads=[("geq", b), ("ps", pk[h // 4]), "gstate"], writes=["gstate"])
                    sc.op("act", lambda e: e.copy(out=state_bf[:], in_=state[:]), reads=["gstate"], writes=["gstate_bf"])
                for cc in range(2):
                    a = (2 * blk + cc) % 2
                    sc.op("act", lambda e, o_=oraw[a][:], p_=ps[pos[cc]][:, :]: e.copy(out=o_, in_=p_), reads=[("ps", pos[cc])], writes=[("gor", a)])
                    sc.op("pool", lambda e, o_=osq[a][:], i_=oraw[a][:]: e.tensor_tensor(out=o_, in0=i_, in1=i_, op=ALU.mult),
                          reads=[("gor", a)], writes=[("gos", a)])
                    pn = nextpb()
                    mm_group(sc, ps[pn][:, :], [(ones_f[:], osq[a][:])], reads=[("gos", a), "ones_f"], writes=[("ps", pn)])
                    sc.op("act", lambda e, o_=orn[a][:], p_=ps[pn][:, :]: e.activation(out=o_, in_=p_, func=AF.Ln, bias=128.0 * EPS),
                          reads=[("ps", pn)], writes=[("grn", a)])
                    sc.op("act", lambda e, o_=orn[a][:]: e.activation(out=o_, in_=o_, func=AF.Exp, scale=-0.5), reads=[("grn", a)], writes=[("grn", a)])
                    sc.op("dve", lambda e, o_=oraw[a][:], r_=orn[a][:]: e.tensor_tensor(out=o_, in0=o_, in1=r_, op=ALU.mult),
                          reads=[("gor", a), ("grn", a)], writes=[("gor", a)])
                    sc.op("dve", lambda e, o_=ost[ob][:, :, cc * CH:(cc + 1) * CH], i_=oraw[a][:].rearrange("v (h i) -> v h i", i=CH),
                          g_=brt[b][:, :, cc * CH:(cc + 1) * CH]:
                          e.scalar_tensor_tensor(out=o_, in0=i_, scalar=gn[:, 0:1], in1=g_, op0=ALU.mult, op1=ALU.mult),
                          reads=[("gor", a), "gn", ("gbr", b)], writes=[("gost", ob)])
                sc.dma("pool", OB[:, tsl].rearrange("(h v) s -> v h s", v=128), ost[ob][:], reads=[("gost", ob)])
            sc.emit()
        sc.barrier()

    sc.emit()
    sc.barrier()
    xcur = xT
    for l in range(NL):
        if l == 0:
            phase_norm(xcur, None, None, None, g_in["g_pre_mix"][l], XN)
        phase_dense(XN, KC, Wb_in[l], NIT, 512, h_inproj(l),
                    lambda j, l=l: pieces_of(w_in[l], in_tiles[j][1], in_tiles[j][2], KC))
        phase_attn_A(l)
        phase_gla(l)
        phase_attn_C(l)
        phase_merge(l)
        phase_dense(MT, KC, Wb_out[l], D // 512, 512, h_f32out(YT, 512),
                    lambda j, l=l: pieces_of(w_out[l], 512 * j, 512, KC))
        phase_norm(xcur, YT, g_in["g_post_mix"][l], XA, g_in["g_pre_ffn"][l], XN)
        phase_dense(XN, KC, Wb_up[l], NUT, 512, h_ffn_up(l),
                    lambda j, l=l: pieces_of(w_up[l], 256 * j, 256, KC, 0, 0) + pieces_of(w_up[l], DFF + 256 * j, 256, KC, 0, 256))
        phase_dense(HT, KF, Wb_dn[l], D // 128, 128, h_f32out(YT, 128),
                    lambda j, l=l: pieces_of(w_down[l], 128 * j, 128, KF), nbuf_in=1)
        last = (l == NL - 1)
        xdst = outT if last else XB
        phase_norm(XA, YT, g_in["g_post_ffn"][l], xdst, None if last else g_in["g_pre_mix"][l + 1], None if last else XN)
        xcur = XB
    sc.emit(final=True)
    return P


def make_inputs(cfg, inputs, b):
    bucket, maskc = host_tables()
    rel_bias = np.asarray(inputs["rel_bias"], np.float32)
    biasg = np.empty((24, 128, 256), np.float32)
    for h in range(24):
        kind = (h // 4) if h < 12 else 3
        biasg[h] = rel_bias[bucket[kind], h]
    tri, upp, m64 = gla_masks()
    m = {
        "xT": np.ascontiguousarray(np.asarray(inputs["x"][b], np.float32).T),
        "biasg": biasg, "maskc": maskc, "gla_tri": tri, "gla_upp": upp, "gla_m64": m64,
    }
    for k in ("w_in", "w_gla_gate", "b_gla_gate", "gla_norm", "attn_sinks", "w_br_a", "w_br_b", "w_br_c", "w_out",
              "g_pre_mix", "g_post_mix", "g_pre_ffn", "g_post_ffn", "w_up", "conv_w", "conv_b", "w_down"):
        m[k] = np.ascontiguousarray(np.asarray(inputs[k], np.float32))
    return m


_PROG_CACHE = {}


def run(cfg, inputs):
    key = (cfg.D, cfg.S, cfg.DFF, cfg.NL, cfg.NB, cfg.debug)
    if key not in _PROG_CACHE:
        _PROG_CACHE[key] = build(cfg)
    P = _PROG_CACHE[key]
    in_maps = [make_inputs(cfg, inputs, b) for b in range(cfg.NB)]
    res = run_bass_kernel_spmd(P.nc, in_maps, core_ids=list(range(cfg.NB)))
    return P, res


def kernel(**inputs):
    x = np.asarray(inputs["x"])
    B, S, D = x.shape
    cfg = Cfg(D=D, S=S, DFF=np.asarray(inputs["w_down"]).shape[1], NL=np.asarray(inputs["w_in"]).shape[0], NB=B)
    P, res = run(cfg, inputs)
    out = np.stack([np.ascontiguousarray(res.results[b]["outT"].T) for b in range(B)], axis=0)
    return out.astype(np.float32)
```

```python
import numpy as np
import ml_dtypes
from contextlib import ExitStack
import concourse.bass as bass
import concourse.mybir as mybir
from concourse.bass_utils import run_bass_kernel_spmd

F32 = mybir.dt.float32
BF16 = mybir.dt.bfloat16
ALU = mybir.AluOpType
AF = mybir.ActivationFunctionType

HD = 128
A_HEADS = 12
A_SLOTS = 4
DIL = ((128, 1), (512, 4), (2048, 16))
B_HEADS = 8
B_DK = 64
B_DV = 128
B_RANK = 16
B_TAU = 16.0
CH = 64
C_QH = 12
C_KVH = 3
REL_BUCKETS = 32
REL_MAX_DIST = 2048
EPS = 1e-6
NEG = -30000.0


class Cfg:
    def __init__(self, D=4096, S=8192, DFF=11008, NL=2, NB=2, debug=False):
        self.D, self.S, self.DFF, self.NL, self.NB, self.debug = D, S, DFF, NL, NB, debug
        self.KC = D // 128
        self.widths = (1536, 1536, 1536, 512, 512, 1024, 1024, 16, 1536, 384, 384, D, D, D)
        self.offs = np.concatenate([[0], np.cumsum(self.widths)]).tolist()
        self.NIN = self.offs[-1]
        assert D % 512 == 0 and S % 2048 == 0 and DFF % 256 == 0


class Sched:
    ENG = ("pe", "dve", "act", "pool", "sp")
    NDMA = 12

    def __init__(self, nc, ctx):
        self.nc = nc
        self.sem = {e: ctx.enter_context(nc.semaphore("c_" + e)) for e in self.ENG}
        self.count = {e: 0 for e in self.ENG}
        self.dsem = {q: [ctx.enter_context(nc.semaphore("d_%s%d" % (q, i))) for i in range(self.NDMA)]
                     for q in ("sp", "pool", "act")}
        self.dcnt = {q: 0 for q in self.dsem}
        self.waited = {e: {} for e in self.ENG}
        self.lastw = {}
        self.readers = {}
        self.streams = {e: [] for e in self.ENG}
        self.bar = []
        self.ninst = 0

    def _deps(self, eng, reads, writes):
        deps = list(self.bar)
        for k in reads:
            t = self.lastw.get(k)
            if t is not None:
                deps.append(t)
        for k in writes:
            t = self.lastw.get(k)
            if t is not None:
                deps.append(t)
            deps.extend(self.readers.get(k, ()))
        return deps

    def _reduce(self, eng, deps):
        best = {}
        w = self.waited[eng]
        for (s, v, src) in deps:
            if src == "pe" and eng == "pe":
                continue
            if w.get(id(s), 0) >= v:
                continue
            if best.get(id(s), (None, 0))[1] < v:
                best[id(s)] = (s, v)
        out = []
        for sid, (s, v) in best.items():
            w[sid] = v
            out.append((s, v))
        return out

    def _record(self, tok, reads, writes):
        for k in writes:
            self.lastw[k] = tok
            self.readers[k] = []
        for k in reads:
            if k not in writes:
                self.readers.setdefault(k, []).append(tok)

    def op(self, eng, fn, reads=(), writes=()):
        waits = self._reduce(eng, self._deps(eng, reads, writes))
        self.count[eng] += 1
        tok = (self.sem[eng], self.count[eng], eng)
        self.streams[eng].append((waits, fn, self.sem[eng], 1))
        self._record(tok, reads, writes)
        return tok

    def dma(self, q, out, in_, reads=(), writes=(), slow=False):
        i = self.dcnt[q]
        self.dcnt[q] += 1
        s = self.dsem[q][i % self.NDMA]
        rnd = i // self.NDMA
        deps = self._deps(q, reads, writes)
        if rnd > 0:
            deps.append((s, 16 * rnd, "dma"))
        waits = self._reduce(q, deps)
        tok = (s, 16 * (rnd + 1), "dma")
        if slow:
            self.streams[q].append((waits, lambda e, out=out, in_=in_: e.dma_start(out=out, in_=in_, allow_slow_non_contiguous=True), s, 16))
        else:
            self.streams[q].append((waits, lambda e, out=out, in_=in_: e.dma_start(out=out, in_=in_), s, 16))
        self._record(tok, reads, writes)
        return tok

    def barrier(self):
        toks = [(self.sem[e], self.count[e], e) for e in self.ENG if self.count[e] > 0]
        for q in self.dsem:
            n = self.dcnt[q]
            for j in range(self.NDMA):
                cnt = (n - j + self.NDMA - 1) // self.NDMA if n > j else 0
                if cnt > 0:
                    toks.append((self.dsem[q][j], 16 * cnt, "dma"))
        self.bar = toks
        self.lastw = {}
        self.readers = {}

    def emit(self, final=False):
        if final:
            self.barrier()
            for e in self.ENG:
                waits = self._reduce(e, list(self.bar))
                if waits:
                    self.streams[e].append((waits, None, None, 0))
        nc = self.nc
        streams = self.streams
        total = sum(len(v) for v in streams.values())
        self.ninst += total

        def replay(name, e):
            for (waits, fn, s, inc) in streams[name]:
                for (ws, wv) in waits:
                    e.wait_ge(ws, wv)
                if fn is not None:
                    ins = fn(e)
                    ins.then_inc(s, inc)

        with nc.Block() as block:
            if streams["pe"]:
                @block.tensor
                def _(e):
                    replay("pe", e)
            if streams["dve"]:
                @block.vector
                def _(e):
                    replay("dve", e)
            if streams["act"]:
                @block.scalar
                def _(e):
                    replay("act", e)
            if streams["pool"]:
                @block.gpsimd
                def _(e):
                    replay("pool", e)
            if streams["sp"]:
                @block.sync
                def _(e):
                    replay("sp", e)
        self.streams = {e: [] for e in self.ENG}


def mm_group(sc, out_ap, pairs, reads, writes):
    def fn(e, pairs=pairs, out_ap=out_ap):
        n = len(pairs)
        ins = None
        for i, (l, r) in enumerate(pairs):
            ins = e.matmul(out_ap, l, r, start=(i == 0), stop=(i == n - 1))
        return ins
    return sc.op("pe", fn, reads, writes)


def _t5_bucket(dist):
    dist = np.maximum(dist, 0)
    max_exact = REL_BUCKETS // 2
    far = max_exact + (np.log(np.maximum(dist, 1).astype(np.float32) / max_exact)
                       / np.float32(np.log(REL_MAX_DIST / max_exact))
                       * (REL_BUCKETS - max_exact)).astype(np.int32)
    return np.where(dist < max_exact, dist, np.minimum(far, REL_BUCKETS - 1))


def _rel_kq():
    k = np.arange(128)[:, None]
    q = np.arange(128)[None, :]
    rel_prev = q + 128 - k
    rel_cur = q - k
    return np.concatenate([rel_prev, rel_cur], axis=1)


def host_tables():
    rel = _rel_kq()
    kinds = [(128, d) for (_, d) in DIL] + [(127, 1)]
    bucket = []
    maskc = []
    for (span, d) in kinds:
        bucket.append(_t5_bucket(rel * d))
        ok = (rel >= 0) & (rel <= span)
        maskc.append(np.where(ok, 0.0, NEG).astype(np.float32))
    return bucket, np.stack(maskc)


def gla_masks():
    j = np.arange(128)[:, None]
    i = np.arange(128)[None, :]
    same = (j // CH) == (i // CH)
    tri = (same & (j <= i)).astype(np.float32)
    upp = (same & (j > i)).astype(np.float32)
    jj = np.arange(CH)[:, None]
    ii = np.arange(CH)[None, :]
    m64 = (jj <= ii).astype(np.float32)
    return tri, upp, np.tile(m64, (1, B_HEADS))


class Prog:
    def __init__(self, cfg):
        self.cfg = cfg
        self.nc = bass.Bass(target_bir_lowering=False)
        self.ctx = ExitStack()
        self.sc = Sched(self.nc, self.ctx)
        self.dbg = {}

    def din(self, name, shape, dt=F32):
        return self.nc.dram_tensor(name, list(shape), dt, kind="ExternalInput").ap()

    def dout(self, name, shape, dt=F32):
        return self.nc.dram_tensor(name, list(shape), dt, kind="ExternalOutput").ap()

    def dscr(self, name, shape, dt=BF16, dbg=False):
        if dbg and self.cfg.debug:
            t = self.nc.dram_tensor(name, list(shape), dt, kind="ExternalOutput").ap()
            self.dbg[name] = t
            return t
        return self.nc.dram_tensor(name, list(shape), dt).ap()


def build(cfg):
    P = Prog(cfg)
    nc, sc = P.nc, P.sc
    D, S, DFF, NL, KC = cfg.D, cfg.S, cfg.DFF, cfg.NL, cfg.KC
    KF = DFF // 128
    NT = S // 512
    offs = cfg.offs

    xT = P.din("xT", [D, S])
    outT = P.dout("outT", [D, S])
    w_in = P.din("w_in", [NL, D, cfg.NIN])
    w_gate = P.din("w_gla_gate", [NL, B_RANK, 512])
    b_gate = P.din("b_gla_gate", [NL, 512])
    gla_norm = P.din("gla_norm", [NL, 128])
    sinks = P.din("attn_sinks", [NL, C_QH])
    w_bra = P.din("w_br_a", [NL, 512, D])
    w_brb = P.din("w_br_b", [NL, 1024, D])
    w_brc = P.din("w_br_c", [NL, 1536, D])
    w_out = P.din("w_out", [NL, D, D])
    g_in = {n: P.din(n, [NL, D]) for n in ("g_pre_mix", "g_post_mix", "g_pre_ffn", "g_post_ffn")}
    w_up = P.din("w_up", [NL, D, 2 * DFF])
    conv_w = P.din("conv_w", [NL, 3, 2 * DFF])
    conv_b = P.din("conv_b", [NL, 2 * DFF])
    w_down = P.din("w_down", [NL, DFF, D])
    biasg = P.din("biasg", [24, 128, 256])
    maskc = P.din("maskc", [4, 128, 256])
    tri_in = P.din("gla_tri", [128, 128])
    upp_in = P.din("gla_upp", [128, 128])
    m64_in = P.din("gla_m64", [CH, CH * B_HEADS])

    o = offs
    in_tiles = []
    for i in range(3):
        in_tiles.append(("F", o[0] + 512 * i, 512, "aqT", 512 * i, None))
    for i in range(3):
        in_tiles.append(("F", o[1] + 512 * i, 512, "akT", 512 * i, None))
    for i in range(3):
        in_tiles.append(("T", o[2] + 512 * i, 512, "av", 512 * i, None))
    in_tiles.append(("F", o[3], 512, "bqT", 0, None))
    in_tiles.append(("F", o[4], 512, "bkT", 0, None))
    in_tiles.append(("T", o[4], 512, "bk", 0, None))
    for i in range(2):
        in_tiles.append(("T", o[5] + 512 * i, 512, "bv", 512 * i, None))
    for i in range(2):
        in_tiles.append(("F", o[6] + 512 * i, 512, "brT", 512 * i, AF.Silu))
    in_tiles.append(("F", o[7], 16, "bgT", 0, None))
    for i in range(3):
        in_tiles.append(("F", o[8] + 512 * i, 512, "cqT", 512 * i, None))
    in_tiles.append(("F", o[9], 384, "ckT", 0, None))
    in_tiles.append(("T", o[10], 384, "cv", 0, None))
    for gi, gname in enumerate(("gaT", "gbT", "gcT")):
        for i in range(D // 512):
            in_tiles.append(("F", o[11 + gi] + 512 * i, 512, gname, 512 * i, AF.Sigmoid))
    NIT = len(in_tiles)

    NUT = DFF // 256
    Wb_in = [P.dscr("Wb_in%d" % l, [NIT, 128, KC * 512]) for l in range(NL)]
    Wb_br = [P.dscr("Wb_br%d" % l, [D // 512, 128, 24 * 512]) for l in range(NL)]
    Wb_out = [P.dscr("Wb_out%d" % l, [D // 512, 128, KC * 512]) for l in range(NL)]
    Wb_up = [P.dscr("Wb_up%d" % l, [NUT, 128, KC * 512]) for l in range(NL)]
    Wb_dn = [P.dscr("Wb_dn%d" % l, [D // 128, 128, KF * 128]) for l in range(NL)]

    XN = P.dscr("XN", [D, S], BF16, dbg=True)
    XA = P.dscr("XA", [D, S], F32, dbg=True)
    XB = P.dscr("XB", [D, S], F32)
    YT = P.dscr("YT", [D, S], F32, dbg=True)
    MT = P.dscr("MT", [D, S], BF16, dbg=True)
    HT = P.dscr("HT", [DFF, S], BF16, dbg=True)
    pr = {
        "aqT": P.dscr("aqT", [1536, S], BF16, dbg=True), "akT": P.dscr("akT", [1536, S], BF16),
        "av": P.dscr("av", [S, 1536], BF16, dbg=True),
        "bqT": P.dscr("bqT", [512, S]), "bkT": P.dscr("bkT", [512, S]), "bk": P.dscr("bk", [S, 512]),
        "bv": P.dscr("bv", [S, 1024]), "brT": P.dscr("brT", [1024, S], BF16, dbg=True),
        "bgT": P.dscr("bgT", [16, S]),
        "cqT": P.dscr("cqT", [1536, S]), "ckT": P.dscr("ckT", [384, S]), "cv": P.dscr("cv", [S, 384]),
        "gaT": P.dscr("gaT", [D, S], BF16, dbg=True), "gbT": P.dscr("gbT", [D, S]), "gcT": P.dscr("gcT", [D, S]),
    }
    OA = P.dscr("OA", [512, S], BF16, dbg=True)
    OB = P.dscr("OB", [1024, S], BF16, dbg=True)
    OC = P.dscr("OC", [1536, S], BF16, dbg=True)

    ctx = P.ctx

    uid = [0]

    def sb(name, shape, dt, c=None):
        uid[0] += 1
        return (c or ctx).enter_context(nc.sbuf_tensor("%s_%d" % (name, uid[0]), list(shape), dt))

    def psum_banks(c, n=8):
        uid[0] += 1
        return [c.enter_context(nc.psum_tensor("ps%d_%d" % (i, uid[0]), [128, 512], F32)) for i in range(n)]

    ones_bf = sb("ones_bf", [128, 128], BF16)
    ones_f = sb("ones_f", [128, 128], F32)
    sc.op("pool", lambda e: e.memset(ones_bf[:], 1.0), writes=["ones_bf"])
    sc.op("pool", lambda e: e.memset(ones_f[:], 1.0), writes=["ones_f"])

    def phase_norm(x_src, y_src, gpost, x_dst, gnext, xn_dst):
        TN = 128
        with ExitStack() as c:
            ps = psum_banks(c, 4)
            gp = sb("gp", [128, KC], F32, c)
            gn = sb("gn", [128, KC], F32, c)
            if gpost is not None:
                sc.dma("sp", gp[:], gpost.rearrange("(kc p) -> p kc", p=128), writes=["gp"], slow=True)
            if gnext is not None:
                sc.dma("sp", gn[:], gnext.rearrange("(kc p) -> p kc", p=128), writes=["gn"], slow=True)
            NS = 3
            xt = [sb("nx%d" % i, [128, KC, TN], F32, c) for i in range(NS)]
            yt = [sb("ny%d" % i, [128, KC, TN], F32, c) for i in range(NS)]
            xo = [sb("no%d" % i, [128, KC, TN], BF16, c) for i in range(NS)]
            sq = [sb("nsq%d" % i, [128, KC, TN], F32, c) for i in range(2)]
            red = [sb("nrd%d" % i, [128, TN], F32, c) for i in range(4)]
            rs = [sb("nrs%d" % i, [128, TN], F32, c) for i in range(4)]
            cnt = {"s": 0, "m": 0}

            def stats(src_tile, src_key):
                i = cnt["s"]
                cnt["s"] += 1
                q, r = i % 2, i % 4
                sc.op("act", lambda e, o_=sq[q][:], i_=src_tile[:]: e.activation(out=o_, in_=i_, func=AF.Square),
                      reads=[src_key], writes=[("nsq", q)])
                sc.op("dve", lambda e, o_=red[r][:], i_=sq[q][:].rearrange("p k t -> p t k"):
                      e.reduce_sum(out=o_, in_=i_, axis=mybir.AxisListType.X),
                      reads=[("nsq", q)], writes=[("nrd", r)])
                sc.op("pe", lambda e, o_=ps[r][:, 0:TN], r_=red[r][:]: e.matmul(o_, ones_f[:], r_, start=True, stop=True),
                      reads=[("nrd", r), "ones_f"], writes=[("nps", r)])
                sc.op("act", lambda e, o_=rs[r][:], i_=ps[r][:, 0:TN]:
                      e.activation(out=o_, in_=i_, func=AF.Ln, scale=1.0 / D, bias=EPS),
                      reads=[("nps", r)], writes=[("nrs", r)])
                sc.op("act", lambda e, o_=rs[r][:]: e.activation(out=o_, in_=o_, func=AF.Exp, scale=-0.5),
                      reads=[("nrs", r)], writes=[("nrs", r)])
                return r

            def scale_rows(dst_tile, dst_key, src_tile, src_key, gt, gkey, r):
                for kc in range(KC):
                    eng = "dve"
                    cnt["m"] += 1
                    wk = (dst_key, eng, kc % 2)
                    sc.op(eng, lambda e, o_=dst_tile[:, kc, :], i_=src_tile[:, kc, :], g_=gt[:, kc:kc + 1], r_=rs[r][:]:
                          e.scalar_tensor_tensor(out=o_, in0=i_, scalar=g_, in1=r_, op0=ALU.mult, op1=ALU.mult),
                          reads=[gkey, ("nrs", r)] + ([src_key] if src_key != dst_key else []), writes=[wk])
                return [(dst_key, en, u) for en in ("dve", "pool") for u in (0, 1)]

            for t in range(S // TN):
                b = t % NS
                ts = slice(t * TN, (t + 1) * TN)
                sc.dma("sp", xt[b][:], x_src.rearrange("(kc p) s -> p kc s", p=128)[:, :, ts], writes=[("nx", b), ("nxa", b)])
                if y_src is not None:
                    yk = ("ny", b)
                    ysub = [(yk, en, u) for en in ("dve", "pool") for u in (0, 1)]
                    sc.dma("sp", yt[b][:], y_src.rearrange("(kc p) s -> p kc s", p=128)[:, :, ts], writes=[yk] + ysub)
                    r = stats(yt[b], yk)
                    for kc in range(KC):
                        eng = "dve"
                        cnt["m"] += 1
                        sc.op(eng, lambda e, o_=yt[b][:, kc, :], g_=gp[:, kc:kc + 1], r_=rs[r][:]:
                              e.scalar_tensor_tensor(out=o_, in0=o_, scalar=g_, in1=r_, op0=ALU.mult, op1=ALU.mult),
                              reads=["gp", ("nrs", r), yk], writes=[(yk, eng, kc % 2)])
                    sc.op("dve", lambda e, o_=xt[b][:], y_=yt[b][:]: e.tensor_tensor(out=o_, in0=o_, in1=y_, op=ALU.add),
                          reads=ysub + [("nx", b)], writes=[("nx", b), ("nxa", b), yk])
                    sc.dma("pool", x_dst.rearrange("(kc p) s -> p kc s", p=128)[:, :, ts], xt[b][:], reads=[("nx", b)])
                if gnext is not None:
                    r = stats(xt[b], ("nx", b))
                    ok = ("no", b)
                    osub = [(ok, en, u) for en in ("dve", "pool") for u in (0, 1)]
                    for kc in range(KC):
                        eng = "dve"
                        cnt["m"] += 1
                        sc.op(eng, lambda e, o_=xo[b][:, kc, :], i_=xt[b][:, kc, :], g_=gn[:, kc:kc + 1], r_=rs[r][:]:
                              e.scalar_tensor_tensor(out=o_, in0=i_, scalar=g_, in1=r_, op0=ALU.mult, op1=ALU.mult),
                              reads=["gn", ("nrs", r), ("nxa", b)], writes=[(ok, eng, kc % 2)])
                    sc.dma("pool", xn_dst.rearrange("(kc p) s -> p kc s", p=128)[:, :, ts], xo[b][:], reads=osub)
            sc.emit()
        sc.barrier()

    def make_wfetch(c, cw_buf, kcn, nstg=4):
        stg = [sb("wst%d" % i, [128, 2048], F32, c) for i in range(nstg)]
        st = {"n": 0}
        engs = ("pool", "dve", "pool", "act")

        def fetch(first, wbuf, wkey, wtile_dram, pieces, dkey):
            wkeys = [(wkey, en) for en in ("pool", "dve", "act")]
            if not first:
                sc.dma("sp", wbuf[:], wtile_dram, reads=[dkey], writes=wkeys)
                return
            wv = wbuf[:].rearrange("p (k c) -> p k c", c=cw_buf)
            for (src, k0, kn, c0, cwp) in pieces:
                i = st["n"] % nstg
                st["n"] += 1
                sview = stg[i][:, 0:kn * cwp].rearrange("p (k c) -> p k c", c=cwp)
                sc.dma("sp", sview, src, writes=[("wst", i)])
                eng = engs[st["n"] % 4]
                dstv = wv[:, k0:k0 + kn, c0:c0 + cwp]
                if eng == "act":
                    sc.op("act", lambda e, o_=dstv, i_=sview: e.copy(out=o_, in_=i_), reads=[("wst", i)], writes=[(wkey, eng)])
                else:
                    sc.op(eng, lambda e, o_=dstv, i_=sview: e.tensor_copy(out=o_, in_=i_), reads=[("wst", i)], writes=[(wkey, eng)])
            sc.dma("pool", wtile_dram, wbuf[:], reads=wkeys, writes=[dkey])
        return fetch

    def pieces_of(src2d, col0, ncol, kcn, k_dst0=0, c_dst0=0):
        v = src2d.rearrange("(kc p) n -> p kc n", p=128)
        g = max(1, 2048 // ncol)
        out = []
        k0 = 0
        while k0 < kcn:
            kn = min(g, kcn - k0)
            out.append((v[:, k0:k0 + kn, col0:col0 + ncol], k_dst0 + k0, kn, c_dst0, ncol))
            k0 += kn
        return out

    def phase_dense(inp, kcn, wtiles, ntiles, cw, handler, wsrc, nbuf_in=2):
        with ExitStack() as c:
            ps = psum_banks(c, 8)
            xin = [sb("din%d" % i, [128, kcn, 512], BF16, c) for i in range(nbuf_in)]
            wb = [sb("dw%d" % i, [128, kcn * cw], BF16, c) for i in range(2)]
            fetch = make_wfetch(c, cw, kcn)
            f = handler(c, ps)
            nw = 0
            for tt in range(NT):
                ib = tt % nbuf_in
                sc.dma("sp", xin[ib][:], inp.rearrange("(kc p) s -> p kc s", p=128)[:, :, tt * 512:(tt + 1) * 512],
                       writes=[("din", ib)])
                for j in range(ntiles):
                    b = nw % 2
                    nw += 1
                    fetch(tt == 0, wb[b], ("dw", b), wtiles[j], wsrc(j) if tt == 0 else None, ("wdram", j))
                    f(tt, j, wb[b][:].rearrange("p (k c) -> p k c", c=cw), xin[ib], ("din", ib),
                      [(("dw", b), en) for en in ("pool", "dve", "act")])
            sc.emit()
        sc.barrier()

    def h_inproj(l):
        def mk(c, ps):
            stg = [sb("ist%d" % i, [128, 4, 512], BF16, c) for i in range(3)]
            state = {"n": 0, "pb": 0, "e": 0}

            def f(tt, j, w, xin, kin, kw):
                kind, c0, ncol, dname, doff, act = in_tiles[j]
                q = state["n"] % 3
                state["n"] += 1
                dst = pr[dname]
                if kind == "F":
                    nb = (ncol + 127) // 128
                    for cb in range(nb):
                        m = min(128, ncol - cb * 128)
                        pb = state["pb"] % 8
                        state["pb"] += 1
                        mm_group(sc, ps[pb][0:m, :], [(w[:, kc, cb * 128:cb * 128 + m], xin[:, kc, :]) for kc in range(KC)],
                                 reads=[kin] + kw, writes=[("ps", pb)])
                        if act is not None:
                            sc.op("act", lambda e, o_=stg[q][0:m, cb, :], i_=ps[pb][0:m, :], a_=act: e.activation(out=o_, in_=i_, func=a_),
                                  reads=[("ps", pb)], writes=[("ist", q)])
                        elif state["e"] % 2 == 0:
                            sc.op("act", lambda e, o_=stg[q][0:m, cb, :], i_=ps[pb][0:m, :]: e.copy(out=o_, in_=i_),
                                  reads=[("ps", pb)], writes=[("ist", q)])
                        else:
                            sc.op("dve", lambda e, o_=stg[q][0:m, cb, :], i_=ps[pb][0:m, :]: e.tensor_copy(out=o_, in_=i_),
                                  reads=[("ps", pb)], writes=[("ist", q)])
                        state["e"] += 1
                    if ncol % 128 == 0:
                        sc.dma("pool", dst[doff:doff + ncol, tt * 512:(tt + 1) * 512].rearrange("(cb p) s -> p cb s", p=128),
                               stg[q][:, 0:nb, :], reads=[("ist", q)])
                    else:
                        sc.dma("pool", dst[doff:doff + ncol, tt * 512:(tt + 1) * 512], stg[q][0:ncol, 0, :], reads=[("ist", q)])
                else:
                    for tb in range(4):
                        pb = state["pb"] % 8
                        state["pb"] += 1
                        mm_group(sc, ps[pb][:, 0:ncol], [(xin[:, kc, tb * 128:(tb + 1) * 128], w[:, kc, 0:ncol]) for kc in range(KC)],
                                 reads=[kin] + kw, writes=[("ps", pb)])
                        if state["e"] % 2 == 0:
                            sc.op("act", lambda e, o_=stg[q][:, tb, 0:ncol], i_=ps[pb][:, 0:ncol]: e.copy(out=o_, in_=i_),
                                  reads=[("ps", pb)], writes=[("ist", q)])
                        else:
                            sc.op("dve", lambda e, o_=stg[q][:, tb, 0:ncol], i_=ps[pb][:, 0:ncol]: e.tensor_copy(out=o_, in_=i_),
                                  reads=[("ps", pb)], writes=[("ist", q)])
                        state["e"] += 1
                    sc.dma("pool", dst[tt * 512:(tt + 1) * 512, doff:doff + ncol].rearrange("(tb p) n -> p tb n", p=128),
                           stg[q][:, :, 0:ncol], reads=[("ist", q)])
            return f
        return mk

    def h_f32out(dst, cw):
        def mk(c, ps):
            nb = cw // 128
            stg = [sb("fst%d" % i, [128, nb, 512], F32, c) for i in range(3)]
            state = {"n": 0, "pb": 0}

            def f(tt, j, w, xin, kin, kw):
                kcn = w.shape[1]
                q = state["n"] % 3
                state["n"] += 1
                for cb in range(nb):
                    pb = state["pb"] % 8
                    state["pb"] += 1
                    mm_group(sc, ps[pb][:, :], [(w[:, kc, cb * 128:(cb + 1) * 128], xin[:, kc, :]) for kc in range(kcn)],
                             reads=[kin] + kw, writes=[("ps", pb)])
                    if state["pb"] % 2 == 0:
                        sc.op("act", lambda e, o_=stg[q][:, cb, :], i_=ps[pb][:, :]: e.copy(out=o_, in_=i_),
                              reads=[("ps", pb)], writes=[("fst", q)])
                    else:
                        sc.op("dve", lambda e, o_=stg[q][:, cb, :], i_=ps[pb][:, :]: e.tensor_copy(out=o_, in_=i_),
                              reads=[("ps", pb)], writes=[("fst", q)])
                sc.dma("pool", dst[j * cw:(j + 1) * cw, tt * 512:(tt + 1) * 512].rearrange("(cb p) s -> p cb s", p=128),
                       stg[q][:], reads=[("fst", q)])
            return f
        return mk

    def h_ffn_up(l):
        def mk(c, ps):
            cwt = sb("cwt", [128, 3, 2 * KF], F32, c)
            cbt = sb("cbt", [128, 2 * KF], F32, c)
            sc.dma("sp", cwt[:], conv_w[l].rearrange("t (b p) -> p t b", p=128), writes=["cwt"], slow=True)
            sc.dma("sp", cbt[:], conv_b[l].rearrange("(b p) -> p b", p=128), writes=["cbt"], slow=True)
            carry = sb("carry", [128, 2 * KF, 2], F32, c)
            sc.op("pool", lambda e: e.memset(carry[:], 0.0), writes=["carry"])
            yb = [sb("yb%d" % i, [128, 514], F32, c) for i in range(4)]
            ub = [sb("ub%d" % i, [128, 512], F32, c) for i in range(4)]
            hst = [sb("hst%d" % i, [128, 2, 512], BF16, c) for i in range(3)]
            state = {"n": 0, "pb": 0, "y": 0}

            def f(tt, j, w, xin, kin, kw):
                q = state["n"] % 3
                state["n"] += 1
                for half in range(2):
                    res = []
                    for which in range(2):
                        blk = (KF * which) + 2 * j + half
                        wc = which * 256 + half * 128
                        pb = state["pb"] % 8
                        state["pb"] += 1
                        mm_group(sc, ps[pb][:, :], [(w[:, kc, wc:wc + 128], xin[:, kc, :]) for kc in range(KC)],
                                 reads=[kin] + kw, writes=[("ps", pb)])
                        yi = state["y"] % 4
                        state["y"] += 1
                        ck = ("carry", blk)
                        sc.op("pool", lambda e, o_=yb[yi][:, 0:2], i_=carry[:, blk, :]: e.tensor_copy(out=o_, in_=i_),
                              reads=[ck, "carry"], writes=[("yb", yi)])
                        sc.op("act", lambda e, o_=yb[yi][:, 2:514], i_=ps[pb][:, :]: e.copy(out=o_, in_=i_),
                              reads=[("ps", pb)], writes=[("yb", yi)])
                        sc.op("pool", lambda e, o_=carry[:, blk, :], i_=yb[yi][:, 512:514]: e.tensor_copy(out=o_, in_=i_),
                              reads=[("yb", yi)], writes=[ck])
                        sc.op("act", lambda e, o_=ub[yi][:], i_=yb[yi][:, 2:514], s_=cwt[:, 2, blk:blk + 1], b_=cbt[:, blk:blk + 1]:
                              e.activation(out=o_, in_=i_, func=AF.Identity, scale=s_, bias=b_),
                              reads=[("yb", yi), "cwt", "cbt"], writes=[("ub", yi)])
                        sc.op("dve", lambda e, o_=ub[yi][:], i_=yb[yi][:, 1:513], s_=cwt[:, 1, blk:blk + 1]:
                              e.scalar_tensor_tensor(out=o_, in0=i_, scalar=s_, in1=o_, op0=ALU.mult, op1=ALU.add),
                              reads=[("yb", yi), "cwt", ("ub", yi)], writes=[("ub", yi)])
                        sc.op("dve", lambda e, o_=ub[yi][:], i_=yb[yi][:, 0:512], s_=cwt[:, 0, blk:blk + 1]:
                              e.scalar_tensor_tensor(out=o_, in0=i_, scalar=s_, in1=o_, op0=ALU.mult, op1=ALU.add),
                              reads=[("yb", yi), "cwt", ("ub", yi)], writes=[("ub", yi)])
                        res.append(yi)
                    gi, ui = res
                    sc.op("act", lambda e, o_=yb[gi][:, 0:512], i_=ub[gi][:]: e.activation(out=o_, in_=i_, func=AF.Silu),
                          reads=[("ub", gi)], writes=[("yb", gi)])
                    sc.op("dve", lambda e, o_=hst[q][:, half, :], a_=yb[gi][:, 0:512], b_=ub[ui][:]:
                          e.tensor_tensor(out=o_, in0=a_, in1=b_, op=ALU.mult),
                          reads=[("yb", gi), ("ub", ui)], writes=[("hst", q)])
                sc.dma("pool", HT[256 * j:256 * j + 256, tt * 512:(tt + 1) * 512].rearrange("(cb p) s -> p cb s", p=128),
                       hst[q][:], reads=[("hst", q)])
            return f
        return mk

    def phase_merge(l):
        with ExitStack() as c:
            ps = psum_banks(c, 8)
            xin = [sb("min%d" % i, [128, 24, 512], BF16, c) for i in range(2)]
            wb = [sb("mw%d" % i, [128, 24 * 512], BF16, c) for i in range(2)]
            gt = [sb("mg%d" % i, [128, 3, 4, 512], BF16, c) for i in range(2)]
            t1 = [sb("mt1_%d" % i, [128, 512], F32, c) for i in range(2)]
            t2 = [sb("mt2_%d" % i, [128, 512], F32, c) for i in range(2)]
            t3 = [sb("mt3_%d" % i, [128, 512], F32, c) for i in range(2)]
            stg = [sb("mst%d" % i, [128, 4, 512], BF16, c) for i in range(3)]
            fetch = make_wfetch(c, 512, 24, nstg=2)
            nw = 0
            npb = 0
            nq = 0
            gsrc = (pr["gaT"], pr["gbT"], pr["gcT"])
            for tt in range(NT):
                ib = tt % 2
                tsl = slice(tt * 512, (tt + 1) * 512)
                sc.dma("sp", xin[ib][:, 0:4, :], OA.rearrange("(kc p) s -> p kc s", p=128)[:, :, tsl], writes=[("min", ib)])
                sc.dma("sp", xin[ib][:, 4:12, :], OB.rearrange("(kc p) s -> p kc s", p=128)[:, :, tsl], writes=[("min", ib)])
                sc.dma("sp", xin[ib][:, 12:24, :], OC.rearrange("(kc p) s -> p kc s", p=128)[:, :, tsl], writes=[("min", ib)])
                for j in range(D // 512):
                    b = nw % 2
                    nw += 1
                    pcs = None
                    if tt == 0:
                        pcs = (pieces_of(w_bra[l], 512 * j, 512, 4, 0) + pieces_of(w_brb[l], 512 * j, 512, 8, 4)
                               + pieces_of(w_brc[l], 512 * j, 512, 12, 12))
                    fetch(tt == 0, wb[b], ("mw", b), Wb_br[l][j], pcs, ("wdram", j))
                    mwk = [(("mw", b), en) for en in ("pool", "dve", "act")]
                    for gi in range(3):
                        sc.dma("sp", gt[b][:, gi, :, :],
                               gsrc[gi][512 * j:512 * j + 512, tsl].rearrange("(cb p) s -> p cb s", p=128), writes=[("mg", b)])
                    w = wb[b][:].rearrange("p (k c) -> p k c", c=512)
                    q = nq % 3
                    nq += 1
                    for cb in range(4):
                        pbs = []
                        for (k0, k1) in ((0, 4), (4, 12), (12, 24)):
                            pb = npb % 8
                            npb += 1
                            mm_group(sc, ps[pb][:, :], [(w[:, kc, cb * 128:(cb + 1) * 128], xin[ib][:, kc, :]) for kc in range(k0, k1)],
                                     reads=[("min", ib)] + mwk, writes=[("ps", pb)])
                            pbs.append(pb)
                        u = cb % 2
                        for tbuf, nm, pb, gi in ((t1, "mt1", pbs[0], 0), (t2, "mt2", pbs[1], 1), (t3, "mt3", pbs[2], 2)):
                            sc.op("dve", lambda e, o_=tbuf[u][:], p_=ps[pb][:, :], g_=gt[b][:, gi, cb, :]:
                                  e.tensor_tensor(out=o_, in0=p_, in1=g_, op=ALU.mult),
                                  reads=[("ps", pb), ("mg", b)], writes=[(nm, u)])
                        sc.op("pool", lambda e, o_=t1[u][:], i_=t2[u][:]: e.tensor_tensor(out=o_, in0=o_, in1=i_, op=ALU.add),
                              reads=[("mt1", u), ("mt2", u)], writes=[("mt1", u)])
                        sc.op("pool", lambda e, o_=stg[q][:, cb, :], a_=t1[u][:], i_=t3[u][:]: e.tensor_tensor(out=o_, in0=a_, in1=i_, op=ALU.add),
                              reads=[("mt1", u), ("mt3", u)], writes=[("mst", q)])
                    sc.dma("pool", MT[512 * j:512 * j + 512, tsl].rearrange("(cb p) s -> p cb s", p=128), stg[q][:], reads=[("mst", q)])
            sc.emit()
        sc.barrier()

    def phase_attn_A(l):
        NSUP = S // 2048
        scale = float(HD) ** -0.5
        with ExitStack() as c:
            ps = psum_banks(c, 8)
            bm = sb("bmA", [128, 12, 256], F32, c)
            mk_ = sb("mkA", [128, 3, 256], F32, c)
            sc.dma("sp", bm[:], biasg[0:12].rearrange("h k q -> k h q"), writes=["bm"])
            sc.dma("sp", mk_[:], maskc[0:3].rearrange("g k q -> k g q"), writes=["mk"])
            for h in range(12):
                sc.op("dve", lambda e, o_=bm[:, h, :], m_=mk_[:, h // 4, :]: e.tensor_tensor(out=o_, in0=o_, in1=m_, op=ALU.add),
                      reads=["bm", "mk"], writes=["bm"])
            qt = [sb("aq%d" % i, [128, 2048], BF16, c) for i in range(2)]
            kt = [sb("ak%d" % i, [128, 4096], BF16, c) for i in range(2)]
            vc = [sb("avc%d" % i, [128, 16, 128], BF16, c) for i in range(2)]
            vp = [sb("avp%d" % i, [128, 16, 128], BF16, c) for i in range(2)]
            num = sb("anum", [128, 2048], F32, c)
            den = sb("aden", [128, 2048], F32, c)
            lg = [sb("alg%d" % i, [128, 512], F32, c) for i in range(4)]
            pt = [sb("apt%d" % i, [128, 512], BF16, c) for i in range(4)]
            ost = [sb("aost%d" % i, [128, 2048], BF16, c) for i in range(2)]
            nld = 0
            npb = 0
            nlg = 0
            nun = 0
            for j in range(A_SLOTS):
                for u in range(NSUP):
                    t0 = u * 2048
                    for g, (win, d) in enumerate(DIL):
                        h = 4 * g + j
                        b = nld % 2
                        nld += 1
                        hr = slice(h * 128, (h + 1) * 128)
                        sc.dma("sp", qt[b][:], pr["aqT"][hr, t0:t0 + 2048], writes=[("aq", b)])
                        sc.dma("sp", kt[b][:, 2048:4096], pr["akT"][hr, t0:t0 + 2048], writes=[("ak", b)])
                        back = {1: 128, 4: 512, 16: 2048}[d]
                        if u > 0:
                            sc.dma("sp", kt[b][:, 4096 - 2048 - back:2048], pr["akT"][hr, t0 - back:t0], writes=[("ak", b)])
                        vsrc = pr["av"][t0:t0 + 2048, hr]
                        if d == 1:
                            sc.dma("sp", vc[b][:], vsrc.rearrange("(n i) e -> i n e", i=128), writes=[("avc", b)])
                        elif d == 4:
                            for m in range(4):
                                sc.dma("sp", vc[b][:, 4 * m:4 * m + 4, :],
                                       vsrc[512 * m:512 * m + 512, :].rearrange("(i r) e -> i r e", r=4), writes=[("avc", b)])
                        else:
                            sc.dma("sp", vc[b][:], vsrc.rearrange("(i r) e -> i r e", r=16), writes=[("avc", b)])
                        if u > 0:
                            vps = pr["av"][t0 - back:t0, hr]
                            if d == 1:
                                sc.dma("sp", vp[b][:, 0, :], vps, writes=[("avp", b)])
                            elif d == 4:
                                sc.dma("sp", vp[b][:, 0:4, :], vps.rearrange("(i r) e -> i r e", r=4), writes=[("avp", b)])
                            else:
                                sc.dma("sp", vp[b][:], vps.rearrange("(i r) e -> i r e", r=16), writes=[("avp", b)])
                        blocks = []
                        if d == 1:
                            for n in range(16):
                                blocks.append((n * 128, n, (n - 1) if n > 0 else None, 0))
                        elif d == 4:
                            for m in range(4):
                                for r in range(4):
                                    blocks.append((m * 512 + r, 4 * m + r, (4 * (m - 1) + r) if m > 0 else None, r))
                        else:
                            for r in range(16):
                                blocks.append((r, r, None, r))
                        for bi in range(0, 16, 2):
                            pb_s = npb % 8
                            npb += 1
                            li = nlg % 4
                            nlg += 1
                            info = []
                            for x in range(2):
                                f, cur_idx, prev_in_cur, r = blocks[bi + x]
                                qv = qt[b][:, f:f + 127 * d + 1:d] if d > 1 else qt[b][:, f:f + 128]
                                kc_ = kt[b][:, 2048 + f:2048 + f + 127 * d + 1:d] if d > 1 else kt[b][:, 2048 + f:2048 + f + 128]
                                fp = 2048 + f - 128 * d
                                has_prev = (u > 0) or (f - 128 * d >= 0)
                                col = x * 256
                                if has_prev:
                                    kp_ = kt[b][:, fp:fp + 127 * d + 1:d] if d > 1 else kt[b][:, fp:fp + 128]
                                    mm_group(sc, ps[pb_s][:, col:col + 128], [(kp_, qv)], reads=[("ak", b), ("aq", b)], writes=[("ps", pb_s)])
                                mm_group(sc, ps[pb_s][:, col + 128:col + 256], [(kc_, qv)], reads=[("ak", b), ("aq", b)], writes=[("ps", pb_s)])
                                if prev_in_cur is not None:
                                    vprev = vc[b][:, prev_in_cur, :]
                                elif has_prev:
                                    vprev = vp[b][:, r if d > 1 else 0, :]
                                else:
                                    vprev = None
                                info.append((f, has_prev, vprev, vc[b][:, cur_idx, :], col))
                            for x in range(2):
                                f, has_prev, vprev, vcur, col = info[x]
                                c0 = col if has_prev else col + 128
                                bc0 = 0 if has_prev else 128
                                sc.op("dve", lambda e, o_=lg[li][:, c0:col + 256], p_=ps[pb_s][:, c0:col + 256], b_=bm[:, h, bc0:256]:
                                      e.scalar_tensor_tensor(out=o_, in0=p_, scalar=scale, in1=b_, op0=ALU.mult, op1=ALU.add),
                                      reads=[("ps", pb_s), "bm"], writes=[("alg", li)])
                                sc.op("act", lambda e, o_=pt[li][:, c0:col + 256], i_=lg[li][:, c0:col + 256]: e.activation(out=o_, in_=i_, func=AF.Exp),
                                      reads=[("alg", li)], writes=[("apt", li)])
                            pb_n = npb % 8
                            npb += 1
                            pb_d = npb % 8
                            npb += 1
                            for x in range(2):
                                f, has_prev, vprev, vcur, col = info[x]
                                prs_n = []
                                prs_d = []
                                if has_prev:
                                    prs_n.append((vprev, pt[li][:, col:col + 128]))
                                    prs_d.append((ones_bf[:], pt[li][:, col:col + 128]))
                                prs_n.append((vcur, pt[li][:, col + 128:col + 256]))
                                prs_d.append((ones_bf[:], pt[li][:, col + 128:col + 256]))
                                mm_group(sc, ps[pb_n][:, x * 128:(x + 1) * 128], prs_n, reads=[("apt", li), ("avc", b), ("avp", b)], writes=[("ps", pb_n)])
                                mm_group(sc, ps[pb_d][:, x * 128:(x + 1) * 128], prs_d, reads=[("apt", li), "ones_bf"], writes=[("ps", pb_d)])
                            for x in range(2):
                                f = info[x][0]
                                nv = num[:, f:f + 127 * d + 1:d] if d > 1 else num[:, f:f + 128]
                                dv = den[:, f:f + 127 * d + 1:d] if d > 1 else den[:, f:f + 128]
                                if g == 0:
                                    sc.op("act", lambda e, o_=nv, i_=ps[pb_n][:, x * 128:(x + 1) * 128]: e.copy(out=o_, in_=i_),
                                          reads=[("ps", pb_n)], writes=["anum"])
                                    sc.op("dve", lambda e, o_=dv, i_=ps[pb_d][:, x * 128:(x + 1) * 128]: e.tensor_copy(out=o_, in_=i_),
                                          reads=[("ps", pb_d)], writes=["aden"])
                                else:
                                    sc.op("dve", lambda e, o_=nv, i_=ps[pb_n][:, x * 128:(x + 1) * 128]: e.tensor_tensor(out=o_, in0=o_, in1=i_, op=ALU.add),
                                          reads=[("ps", pb_n), "anum"], writes=["anum"])
                                    sc.op("dve", lambda e, o_=dv, i_=ps[pb_d][:, x * 128:(x + 1) * 128]: e.tensor_tensor(out=o_, in0=o_, in1=i_, op=ALU.add),
                                          reads=[("ps", pb_d), "aden"], writes=["aden"])
                    ob = nun % 2
                    nun += 1
                    sc.op("act", lambda e: e.activation(out=den[:], in_=den[:], func=AF.Ln), reads=["aden"], writes=["aden"])
                    sc.op("act", lambda e: e.activation(out=den[:], in_=den[:], func=AF.Exp, scale=-1.0), reads=["aden"], writes=["aden"])
                    sc.op("dve", lambda e, o_=ost[ob][:]: e.tensor_tensor(out=o_, in0=num[:], in1=den[:], op=ALU.mult),
                          reads=["anum", "aden"], writes=[("aost", ob)])
                    sc.dma("pool", OA[j * 128:(j + 1) * 128, t0:t0 + 2048], ost[ob][:], reads=[("aost", ob)])
            sc.emit()
        sc.barrier()

    def phase_attn_C(l):
        NSUP = S // 2048
        scale = float(HD) ** -0.5
        with ExitStack() as c:
            ps = psum_banks(c, 8)
            bm = sb("bmC", [128, 12, 256], F32, c)
            mk_ = sb("mkC", [128, 256], F32, c)
            es = sb("esC", [128, 12], F32, c)
            sc.dma("sp", bm[:], biasg[12:24].rearrange("h k q -> k h q"), writes=["bm"])
            sc.dma("sp", mk_[:], maskc[3], writes=["mk"])
            sc.dma("sp", es[:], sinks[l].partition_broadcast(128), writes=["es"])
            for h in range(12):
                sc.op("dve", lambda e, o_=bm[:, h, :]: e.tensor_tensor(out=o_, in0=o_, in1=mk_[:], op=ALU.add),
                      reads=["bm", "mk"], writes=["bm"])
            sc.op("act", lambda e: e.activation(out=es[:], in_=es[:], func=AF.Exp), reads=["es"], writes=["es"])
            qt = [sb("cq%d" % i, [128, 2048], BF16, c) for i in range(2)]
            kt = [sb("ck%d" % i, [128, 2048 + 128], BF16, c) for i in range(2)]
            vt = [sb("cvt%d" % i, [128, 17, 128], BF16, c) for i in range(2)]
            lg = [sb("clg%d" % i, [128, 512], F32, c) for i in range(4)]
            pt = [sb("cpt%d" % i, [128, 512], BF16, c) for i in range(4)]
            dn = [sb("cdn%d" % i, [128, 256], F32, c) for i in range(4)]
            ost = [sb("cost%d" % i, [128, 2048], BF16, c) for i in range(2)]
            nld = 0
            nkv = 0
            npb = 0
            nlg = 0
            for kvh in range(C_KVH):
                for u in range(NSUP):
                    t0 = u * 2048
                    kb = nkv % 2
                    nkv += 1
                    kr = slice(kvh * 128, (kvh + 1) * 128)
                    sc.dma("sp", kt[kb][:, 128:], pr["ckT"][kr, t0:t0 + 2048], writes=[("ck", kb)])
                    sc.dma("sp", vt[kb][:, 1:17, :], pr["cv"][t0:t0 + 2048, kr].rearrange("(n i) e -> i n e", i=128), writes=[("cvt", kb)])
                    if u > 0:
                        sc.dma("sp", kt[kb][:, 0:128], pr["ckT"][kr, t0 - 128:t0], writes=[("ck", kb)])
                        sc.dma("sp", vt[kb][:, 0, :], pr["cv"][t0 - 128:t0, kr], writes=[("cvt", kb)])
                    for gq in range(4):
                        h = kvh * 4 + gq
                        b = nld % 2
                        nld += 1
                        sc.dma("sp", qt[b][:], pr["cqT"][h * 128:(h + 1) * 128, t0:t0 + 2048], writes=[("cq", b)])
                        for bi in range(0, 16, 2):
                            pb_s = npb % 8
                            npb += 1
                            li = nlg % 4
                            nlg += 1
                            info = []
                            for x in range(2):
                                n = bi + x
                                has_prev = (u > 0) or (n > 0)
                                col = x * 256
                                qv = qt[b][:, n * 128:(n + 1) * 128]
                                if has_prev:
                                    mm_group(sc, ps[pb_s][:, col:col + 128], [(kt[kb][:, n * 128:(n + 1) * 128], qv)],
                                             reads=[("ck", kb), ("cq", b)], writes=[("ps", pb_s)])
                                mm_group(sc, ps[pb_s][:, col + 128:col + 256], [(kt[kb][:, (n + 1) * 128:(n + 2) * 128], qv)],
                                         reads=[("ck", kb), ("cq", b)], writes=[("ps", pb_s)])
                                info.append((n, has_prev, col))
                            for x in range(2):
                                n, has_prev, col = info[x]
                                c0 = col if has_prev else col + 128
                                bc0 = 0 if has_prev else 128
                                sc.op("dve", lambda e, o_=lg[li][:, c0:col + 256], p_=ps[pb_s][:, c0:col + 256], b_=bm[:, h, bc0:256]:
                                      e.scalar_tensor_tensor(out=o_, in0=p_, scalar=scale, in1=b_, op0=ALU.mult, op1=ALU.add),
                                      reads=[("ps", pb_s), "bm"], writes=[("clg", li)])
                                sc.op("act", lambda e, o_=pt[li][:, c0:col + 256], i_=lg[li][:, c0:col + 256]: e.activation(out=o_, in_=i_, func=AF.Exp),
                                      reads=[("clg", li)], writes=[("cpt", li)])
                            pb_n = npb % 8
                            npb += 1
                            pb_d = npb % 8
                            npb += 1
                            for x in range(2):
                                n, has_prev, col = info[x]
                                prs_n = []
                                prs_d = []
                                if has_prev:
                                    prs_n.append((vt[kb][:, n, :], pt[li][:, col:col + 128]))
                                    prs_d.append((ones_bf[:], pt[li][:, col:col + 128]))
                                prs_n.append((vt[kb][:, n + 1, :], pt[li][:, col + 128:col + 256]))
                                prs_d.append((ones_bf[:], pt[li][:, col + 128:col + 256]))
                                mm_group(sc, ps[pb_n][:, x * 128:(x + 1) * 128], prs_n, reads=[("cpt", li), ("cvt", kb)], writes=[("ps", pb_n)])
                                mm_group(sc, ps[pb_d][:, x * 128:(x + 1) * 128], prs_d, reads=[("cpt", li), "ones_bf"], writes=[("ps", pb_d)])
                            sc.op("act", lambda e, o_=dn[li][:], i_=ps[pb_d][:, 0:256], s_=es[:, h:h + 1]:
                                  e.activation(out=o_, in_=i_, func=AF.Ln, bias=s_, scale=1.0),
                                  reads=[("ps", pb_d), "es"], writes=[("cdn", li)])
                            sc.op("act", lambda e, o_=dn[li][:]: e.activation(out=o_, in_=o_, func=AF.Exp, scale=-1.0),
                                  reads=[("cdn", li)], writes=[("cdn", li)])
                            sc.op("dve", lambda e, o_=ost[b][:, bi * 128:(bi + 2) * 128], n_=ps[pb_n][:, 0:256], d_=dn[li][:]:
                                  e.tensor_tensor(out=o_, in0=n_, in1=d_, op=ALU.mult),
                                  reads=[("ps", pb_n), ("cdn", li)], writes=[("cost", b)])
                        sc.dma("pool", OC[h * 128:(h + 1) * 128, t0:t0 + 2048], ost[b][:], reads=[("cost", b)])
            sc.emit()
        sc.barrier()

    def phase_gla(l):
        NBLK = S // 128
        with ExitStack() as c:
            ps = psum_banks(c, 8)
            wg_f = sb("wg_f", [16, 512], F32, c)
            wg = sb("wg", [16, 512], BF16, c)
            bgb = sb("bgb", [128, 512], F32, c)
            tri = sb("tri", [128, 128], F32, c)
            upp = sb("upp", [128, 128], F32, c)
            m64 = sb("m64", [CH, CH * B_HEADS], F32, c)
            gn = sb("glan", [128, 1], F32, c)
            sc.dma("sp", wg_f[:], w_gate[l], writes=["wg_f"])
            sc.op("dve", lambda e: e.tensor_copy(out=wg[:], in_=wg_f[:]), reads=["wg_f"], writes=["wg"])
            sc.dma("sp", bgb[:], b_gate[l].partition_broadcast(128), writes=["bgb"])
            sc.dma("sp", tri[:], tri_in, writes=["tri"])
            sc.dma("sp", upp[:], upp_in, writes=["upp"])
            sc.dma("sp", m64[:], m64_in, writes=["m64"])
            sc.dma("sp", gn[:], gla_norm[l].rearrange("(p o) -> p o", o=1), writes=["gn"], slow=True)
            sc.op("act", lambda e: e.mul(out=gn[:], in_=gn[:], mul=float(np.sqrt(128.0))), reads=["gn"], writes=["gn"])
            state = sb("gstate", [CH, B_HEADS, 128], F32, c)
            state_bf = sb("gstate_bf", [CH, B_HEADS, 128], BF16, c)
            sc.op("pool", lambda e: e.memset(state[:], 0.0), writes=["gstate"])
            sc.op("pool", lambda e: e.memset(state_bf[:], 0.0), writes=["gstate_bf"])
            NB2 = 3
            bgt = [sb("gbg%d" % i, [16, 128], BF16, c) for i in range(NB2)]
            zt = [sb("gz%d" % i, [128, 512], F32, c) for i in range(NB2)]
            qT = [sb("gq%d" % i, [CH, B_HEADS, 128], BF16, c) for i in range(NB2)]
            kT = [sb("gk%d" % i, [CH, B_HEADS, 128], BF16, c) for i in range(NB2)]
            ktok = [sb("gkt%d" % i, [128, 512], BF16, c) for i in range(NB2)]
            v128 = [sb("gv%d" % i, [128, 1024], BF16, c) for i in range(NB2)]
            v64 = [sb("gw%d" % i, [CH, 2, 1024], BF16, c) for i in range(NB2)]
            brt = [sb("gbr%d" % i, [128, B_HEADS, 128], BF16, c) for i in range(NB2)]
            eq = [sb("geq%d" % i, [CH, B_HEADS, 128], F32, c) for i in range(NB2)]
            ek = [sb("gek%d" % i, [CH, B_HEADS, 128], F32, c) for i in range(NB2)]
            qd = [sb("gqd%d" % i, [CH, B_HEADS, 128], BF16, c) for i in range(NB2)]
            ki = [sb("gki%d" % i, [CH, B_HEADS, 128], BF16, c) for i in range(NB2)]
            ko = [sb("gko%d" % i, [128, 512], BF16, c) for i in range(NB2)]
            ed = [sb("ged%d" % i, [128, 512], F32, c) for i in range(NB2)]
            am = [sb("gam%d" % i, [CH, 512], BF16, c) for i in range(2)]
            oraw = [sb("gor%d" % i, [128, 512], F32, c) for i in range(2)]
            osq = [sb("gos%d" % i, [128, 512], F32, c) for i in range(2)]
            orn = [sb("grn%d" % i, [128, 512], F32, c) for i in range(2)]
            ost = [sb("gost%d" % i, [128, B_HEADS, 128], BF16, c) for i in range(2)]
            npb = [0]

            def nextpb():
                pb = npb[0] % 8
                npb[0] += 1
                return pb

            for blk in range(NBLK):
                b = blk % NB2
                t0 = blk * 128
                tsl = slice(t0, t0 + 128)
                sc.dma("sp", bgt[b][:], pr["bgT"][:, tsl], writes=[("gbg", b)])
                sc.dma("sp", qT[b][:], pr["bqT"][:, tsl].rearrange("(h k) s -> k h s", k=CH), writes=[("gq", b)])
                sc.dma("sp", kT[b][:], pr["bkT"][:, tsl].rearrange("(h k) s -> k h s", k=CH), writes=[("gk", b)])
                sc.dma("sp", ktok[b][:], pr["bk"][tsl, :], writes=[("gkt", b)])
                sc.dma("sp", v128[b][:], pr["bv"][tsl, :], writes=[("gv", b)])
                sc.dma("sp", v64[b][:], pr["bv"][tsl, :].rearrange("(c j) v -> j c v", j=CH), writes=[("gw", b)])
                sc.dma("sp", brt[b][:], pr["brT"][:, tsl].rearrange("(h v) s -> v h s", v=128), writes=[("gbr", b)])
                pz = nextpb()
                mm_group(sc, ps[pz][:, :], [(bgt[b][:], wg[:])], reads=[("gbg", b), "wg"], writes=[("ps", pz)])
                sc.op("dve", lambda e, o_=zt[b][:], p_=ps[pz][:, :]: e.tensor_tensor(out=o_, in0=p_, in1=bgb[:], op=ALU.add),
                      reads=[("ps", pz), "bgb"], writes=[("gz", b)])
                sc.op("act", lambda e, o_=zt[b][:]: e.activation(out=o_, in_=o_, func=AF.Exp, scale=-1.0), reads=[("gz", b)], writes=[("gz", b)])
                sc.op("act", lambda e, o_=zt[b][:]: e.activation(out=o_, in_=o_, func=AF.Ln, bias=1.0), reads=[("gz", b)], writes=[("gz", b)])
                pc = [nextpb(), nextpb()]
                for h in range(B_HEADS):
                    mm_group(sc, ps[pc[h // 4]][0:CH, (h % 4) * 128:(h % 4 + 1) * 128], [(zt[b][:, h * CH:(h + 1) * CH], tri[:])],
                             reads=[("gz", b), "tri"], writes=[("ps", pc[h // 4])])
                pd = nextpb()
                mm_group(sc, ps[pd][:, :], [(upp[:], zt[b][:])], reads=[("gz", b), "upp"], writes=[("ps", pd)])
                for hh in range(2):
                    sc.op("act", lambda e, o_=eq[b][:, 4 * hh:4 * hh + 4, :], p_=ps[pc[hh]][0:CH, :].rearrange("k (h i) -> k h i", i=128):
                          e.activation(out=o_, in_=p_, func=AF.Exp, scale=-1.0 / B_TAU), reads=[("ps", pc[hh])], writes=[("geq", b)])
                    sc.op("act", lambda e, o_=ek[b][:, 4 * hh:4 * hh + 4, :], p_=ps[pc[hh]][0:CH, :].rearrange("k (h i) -> k h i", i=128):
                          e.activation(out=o_, in_=p_, func=AF.Exp, scale=1.0 / B_TAU), reads=[("ps", pc[hh])], writes=[("gek", b)])
                sc.op("act", lambda e, o_=ed[b][:], p_=ps[pd][:, :]: e.activation(out=o_, in_=p_, func=AF.Exp, scale=-1.0 / B_TAU),
                      reads=[("ps", pd)], writes=[("ged", b)])
                sc.op("dve", lambda e, o_=qd[b][:], q_=qT[b][:], e_=eq[b][:]:
                      e.scalar_tensor_tensor(out=o_, in0=q_, scalar=float(B_DK) ** -0.5, in1=e_, op0=ALU.mult, op1=ALU.mult),
                      reads=[("gq", b), ("geq", b)], writes=[("gqd", b)])
                sc.op("pool", lambda e, o_=ki[b][:], k_=kT[b][:], e_=ek[b][:]: e.tensor_tensor(out=o_, in0=k_, in1=e_, op=ALU.mult),
                      reads=[("gk", b), ("gek", b)], writes=[("gki", b)])
                sc.op("pool", lambda e, o_=ko[b][:], k_=ktok[b][:], e_=ed[b][:]: e.tensor_tensor(out=o_, in0=k_, in1=e_, op=ALU.mult),
                      reads=[("gkt", b), ("ged", b)], writes=[("gko", b)])
                ob = blk % 2
                po = nextpb()
                po2 = nextpb()
                pos = (po, po2)
                for cc in range(2):
                    a = (2 * blk + cc) % 2
                    isl = slice(cc * CH, (cc + 1) * CH)
                    pa = nextpb()
                    for h in range(B_HEADS):
                        mm_group(sc, ps[pa][0:CH, h * CH:(h + 1) * CH], [(ki[b][:, h, isl], qd[b][:, h, isl])],
                                 reads=[("gki", b), ("gqd", b)], writes=[("ps", pa)])
                    sc.op("dve", lambda e, o_=am[a][:], p_=ps[pa][0:CH, :]: e.tensor_tensor(out=o_, in0=p_, in1=m64[:], op=ALU.mult),
                          reads=[("ps", pa), "m64"], writes=[("gam", a)])
                    for h in range(B_HEADS):
                        mm_group(sc, ps[pos[cc]][:, h * CH:(h + 1) * CH],
                                 [(v64[b][:, cc, h * 128:(h + 1) * 128], am[a][:, h * CH:(h + 1) * CH]),
                                  (state_bf[:, h, :], qd[b][:, h, isl])],
                                 reads=[("gw", b), ("gam", a), "gstate_bf", ("gqd", b)], writes=[("ps", pos[cc])])
                    pk = [nextpb(), nextpb()]
                    for h in range(B_HEADS):
                        mm_group(sc, ps[pk[h // 4]][0:CH, (h % 4) * 128:(h % 4 + 1) * 128],
                                 [(ko[b][cc * CH:(cc + 1) * CH, h * CH:(h + 1) * CH], v128[b][cc * CH:(cc + 1) * CH, h * 128:(h + 1) * 128])],
                                 reads=[("gko", b), ("gv", b)], writes=[("ps", pk[h // 4])])
                    for h in range(B_HEADS):
                        sc.op("dve", lambda e, o_=state[:, h, :], d_=eq[b][:, h, cc * CH + CH - 1:cc * CH + CH],
                              p_=ps[pk[h // 4]][0:CH, (h % 4) * 128:(h % 4 + 1) * 128]:
                              e.scalar_tensor_tensor(out=o_, in0=o_, scalar=d_, in1=p_, op0=ALU.mult, op1=ALU.add),
                              reads=[("geq", b), ("ps", pk[h // 4]), "gstate"], writes=["gstate"])
                    sc.op("act", lambda e: e.copy(out=state_bf[:], in_=state[:]), reads=["gstate"], writes=["gstate_bf"])
                for cc in range(2):
                    a = (2 * blk + cc) % 2
                    sc.op("act", lambda e, o_=oraw[a][:], p_=ps[pos[cc]][:, :]: e.copy(out=o_, in_=p_), reads=[("ps", pos[cc])], writes=[("gor", a)])
                    sc.op("pool", lambda e, o_=osq[a][:], i_=oraw[a][:]: e.tensor_tensor(out=o_, in0=i_, in1=i_, op=ALU.mult),
                          reads=[("gor", a)], writes=[("gos", a)])
                    pn = nextpb()
                    mm_group(sc, ps[pn][:, :], [(ones_f[:], osq[a][:])], reads=[("gos", a), "ones_f"], writes=[("ps", pn)])
                    sc.op("act", lambda e, o_=orn[a][:], p_=ps[pn][:, :]: e.activation(out=o_, in_=p_, func=AF.Ln, bias=128.0 * EPS),
                          reads=[("ps", pn)], writes=[("grn", a)])
                    sc.op("act", lambda e, o_=orn[a][:]: e.activation(out=o_, in_=o_, func=AF.Exp, scale=-0.5), reads=[("grn", a)], writes=[("grn", a)])
                    sc.op("dve", lambda e, o_=oraw[a][:], r_=orn[a][:]: e.tensor_tensor(out=o_, in0=o_, in1=r_, op=ALU.mult),
                          reads=[("gor", a), ("grn", a)], writes=[("gor", a)])
                    sc.op("dve", lambda e, o_=ost[ob][:, :, cc * CH:(cc + 1) * CH], i_=oraw[a][:].rearrange("v (h i) -> v h i", i=CH),
                          g_=brt[b][:, :, cc * CH:(cc + 1) * CH]:
                          e.scalar_tensor_tensor(out=o_, in0=i_, scalar=gn[:, 0:1], in1=g_, op0=ALU.mult, op1=ALU.mult),
                          reads=[("gor", a), "gn", ("gbr", b)], writes=[("gost", ob)])
                sc.dma("pool", OB[:, tsl].rearrange("(h v) s -> v h s", v=128), ost[ob][:], reads=[("gost", ob)])
            sc.emit()
        sc.barrier()

    sc.emit()
    sc.barrier()
    xcur = xT
    for l in range(NL):
        if l == 0:
            phase_norm(xcur, None, None, None, g_in["g_pre_mix"][l], XN)
        phase_dense(XN, KC, Wb_in[l], NIT, 512, h_inproj(l),
                    lambda j, l=l: pieces_of(w_in[l], in_tiles[j][1], in_tiles[j][2], KC))
        phase_attn_A(l)
        phase_gla(l)
        phase_attn_C(l)
        phase_merge(l)
        phase_dense(MT, KC, Wb_out[l], D // 512, 512, h_f32out(YT, 512),
                    lambda j, l=l: pieces_of(w_out[l], 512 * j, 512, KC))
        phase_norm(xcur, YT, g_in["g_post_mix"][l], XA, g_in["g_pre_ffn"][l], XN)
        phase_dense(XN, KC, Wb_up[l], NUT, 512, h_ffn_up(l),
                    lambda j, l=l: pieces_of(w_up[l], 256 * j, 256, KC, 0, 0) + pieces_of(w_up[l], DFF + 256 * j, 256, KC, 0, 256))
        phase_dense(HT, KF, Wb_dn[l], D // 128, 128, h_f32out(YT, 128),
                    lambda j, l=l: pieces_of(w_down[l], 128 * j, 128, KF), nbuf_in=1)
        last = (l == NL - 1)
        xdst = outT if last else XB
        phase_norm(XA, YT, g_in["g_post_ffn"][l], xdst, None if last else g_in["g_pre_mix"][l + 1], None if last else XN)
        xcur = XB
    sc.emit(final=True)
    return P


def make_inputs(cfg, inputs, b):
    bucket, maskc = host_tables()
    rel_bias = np.asarray(inputs["rel_bias"], np.float32)
    biasg = np.empty((24, 128, 256), np.float32)
    for h in range(24):
        kind = (h // 4) if h < 12 else 3
        biasg[h] = rel_bias[bucket[kind], h]
    tri, upp, m64 = gla_masks()
    m = {
        "xT": np.ascontiguousarray(np.asarray(inputs["x"][b], np.float32).T),
        "biasg": biasg, "maskc": maskc, "gla_tri": tri, "gla_upp": upp, "gla_m64": m64,
    }
    for k in ("w_in", "w_gla_gate", "b_gla_gate", "gla_norm", "attn_sinks", "w_br_a", "w_br_b", "w_br_c", "w_out",
              "g_pre_mix", "g_post_mix", "g_pre_ffn", "g_post_ffn", "w_up", "conv_w", "conv_b", "w_down"):
        m[k] = np.ascontiguousarray(np.asarray(inputs[k], np.float32))
    return m


_PROG_CACHE = {}


def run(cfg, inputs):
    key = (cfg.D, cfg.S, cfg.DFF, cfg.NL, cfg.NB, cfg.debug)
    if key not in _PROG_CACHE:
        _PROG_CACHE[key] = build(cfg)
    P = _PROG_CACHE[key]
    in_maps = [make_inputs(cfg, inputs, b) for b in range(cfg.NB)]
    res = run_bass_kernel_spmd(P.nc, in_maps, core_ids=list(range(cfg.NB)))
    return P, res


def kernel(**inputs):
    x = np.asarray(inputs["x"])
    B, S, D = x.shape
    cfg = Cfg(D=D, S=S, DFF=np.asarray(inputs["w_down"]).shape[1], NL=np.asarray(inputs["w_in"]).shape[0], NB=B)
    P, res = run(cfg, inputs)
    out = np.stack([np.ascontiguousarray(res.results[b]["outT"].T) for b in range(B)], axis=0)
    return out.astype(np.float32)
```

```python
import numpy as np
import ml_dtypes
from contextlib import ExitStack
import concourse.bass as bass
import concourse.mybir as mybir
from concourse.bass_utils import run_bass_kernel_spmd

F32 = mybir.dt.float32
BF16 = mybir.dt.bfloat16
ALU = mybir.AluOpType
AF = mybir.ActivationFunctionType

HD = 128
A_HEADS = 12
A_SLOTS = 4
DIL = ((128, 1), (512, 4), (2048, 16))
B_HEADS = 8
B_DK = 64
B_DV = 128
B_RANK = 16
B_TAU = 16.0
CH = 64
C_QH = 12
C_KVH = 3
REL_BUCKETS = 32
REL_MAX_DIST = 2048
EPS = 1e-6
NEG = -30000.0


class Cfg:
    def __init__(self, D=4096, S=8192, DFF=11008, NL=2, NB=2, debug=False):
        self.D, self.S, self.DFF, self.NL, self.NB, self.debug = D, S, DFF, NL, NB, debug
        self.KC = D // 128
        self.widths = (1536, 1536, 1536, 512, 512, 1024, 1024, 16, 1536, 384, 384, D, D, D)
        self.offs = np.concatenate([[0], np.cumsum(self.widths)]).tolist()
        self.NIN = self.offs[-1]
        assert D % 512 == 0 and S % 2048 == 0 and DFF % 256 == 0


class Sched:
    ENG = ("pe", "dve", "act", "pool", "sp")
    NDMA = 12

    def __init__(self, nc, ctx):
        self.nc = nc
        self.sem = {e: ctx.enter_context(nc.semaphore("c_" + e)) for e in self.ENG}
        self.count = {e: 0 for e in self.ENG}
        self.dsem = {q: [ctx.enter_context(nc.semaphore("d_%s%d" % (q, i))) for i in range(self.NDMA)]
                     for q in ("sp", "pool", "act")}
        self.dcnt = {q: 0 for q in self.dsem}
        self.waited = {e: {} for e in self.ENG}
        self.lastw = {}
        self.readers = {}
        self.streams = {e: [] for e in self.ENG}
        self.bar = []
        self.ninst = 0

    def _deps(self, eng, reads, writes):
        deps = list(self.bar)
        for k in reads:
            t = self.lastw.get(k)
            if t is not None:
                deps.append(t)
        for k in writes:
            t = self.lastw.get(k)
            if t is not None:
                deps.append(t)
            deps.extend(self.readers.get(k, ()))
        return deps

    def _reduce(self, eng, deps):
        best = {}
        w = self.waited[eng]
        for (s, v, src) in deps:
            if src == "pe" and eng == "pe":
                continue
            if w.get(id(s), 0) >= v:
                continue
            if best.get(id(s), (None, 0))[1] < v:
                best[id(s)] = (s, v)
        out = []
        for sid, (s, v) in best.items():
            w[sid] = v
            out.append((s, v))
        return out

    def _record(self, tok, reads, writes):
        for k in writes:
            self.lastw[k] = tok
            self.readers[k] = []
        for k in reads:
            if k not in writes:
                self.readers.setdefault(k, []).append(tok)

    def op(self, eng, fn, reads=(), writes=()):
        waits = self._reduce(eng, self._deps(eng, reads, writes))
        self.count[eng] += 1
        tok = (self.sem[eng], self.count[eng], eng)
        self.streams[eng].append((waits, fn, self.sem[eng], 1))
        self._record(tok, reads, writes)
        return tok

    def dma(self, q, out, in_, reads=(), writes=(), slow=False):
        i = self.dcnt[q]
        self.dcnt[q] += 1
        s = self.dsem[q][i % self.NDMA]
        rnd = i // self.NDMA
        deps = self._deps(q, reads, writes)
        if rnd > 0:
            deps.append((s, 16 * rnd, "dma"))
        waits = self._reduce(q, deps)
        tok = (s, 16 * (rnd + 1), "dma")
        if slow:
            self.streams[q].append((waits, lambda e, out=out, in_=in_: e.dma_start(out=out, in_=in_, allow_slow_non_contiguous=True), s, 16))
        else:
            self.streams[q].append((waits, lambda e, out=out, in_=in_: e.dma_start(out=out, in_=in_), s, 16))
        self._record(tok, reads, writes)
        return tok

    def barrier(self):
        toks = [(self.sem[e], self.count[e], e) for e in self.ENG if self.count[e] > 0]
        for q in self.dsem:
            n = self.dcnt[q]
            for j in range(self.NDMA):
                cnt = (n - j + self.NDMA - 1) // self.NDMA if n > j else 0
                if cnt > 0:
                    toks.append((self.dsem[q][j], 16 * cnt, "dma"))
        self.bar = toks
        self.lastw = {}
        self.readers = {}

    def emit(self, final=False):
        if final:
            self.barrier()
            for e in self.ENG:
                waits = self._reduce(e, list(self.bar))
                if waits:
                    self.streams[e].append((waits, None, None, 0))
        nc = self.nc
        streams = self.streams
        total = sum(len(v) for v in streams.values())
        self.ninst += total

        def replay(name, e):
            for (waits, fn, s, inc) in streams[name]:
                for (ws, wv) in waits:
                    e.wait_ge(ws, wv)
                if fn is not None:
                    ins = fn(e)
                    ins.then_inc(s, inc)

        with nc.Block() as block:
            if streams["pe"]:
                @block.tensor
                def _(e):
                    replay("pe", e)
            if streams["dve"]:
                @block.vector
                def _(e):
                    replay("dve", e)
            if streams["act"]:
                @block.scalar
                def _(e):
                    replay("act", e)
            if streams["pool"]:
                @block.gpsimd
                def _(e):
                    replay("pool", e)
            if streams["sp"]:
                @block.sync
                def _(e):
                    replay("sp", e)
        self.streams = {e: [] for e in self.ENG}


def mm_group(sc, out_ap, pairs, reads, writes):
    def fn(e, pairs=pairs, out_ap=out_ap):
        n = len(pairs)
        ins = None
        for i, (l, r) in enumerate(pairs):
            ins = e.matmul(out_ap, l, r, start=(i == 0), stop=(i == n - 1))
        return ins
    return sc.op("pe", fn, reads, writes)


def _t5_bucket(dist):
    dist = np.maximum(dist, 0)
    max_exact = REL_BUCKETS // 2
    far = max_exact + (np.log(np.maximum(dist, 1).astype(np.float32) / max_exact)
                       / np.float32(np.log(REL_MAX_DIST / max_exact))
                       * (REL_BUCKETS - max_exact)).astype(np.int32)
    return np.where(dist < max_exact, dist, np.minimum(far, REL_BUCKETS - 1))


def _rel_kq():
    k = np.arange(128)[:, None]
    q = np.arange(128)[None, :]
    rel_prev = q + 128 - k
    rel_cur = q - k
    return np.concatenate([rel_prev, rel_cur], axis=1)


def host_tables():
    rel = _rel_kq()
    kinds = [(128, d) for (_, d) in DIL] + [(127, 1)]
    bucket = []
    maskc = []
    for (span, d) in kinds:
        bucket.append(_t5_bucket(rel * d))
        ok = (rel >= 0) & (rel <= span)
        maskc.append(np.where(ok, 0.0, NEG).astype(np.float32))
    return bucket, np.stack(maskc)


def gla_masks():
    j = np.arange(128)[:, None]
    i = np.arange(128)[None, :]
    same = (j // CH) == (i // CH)
    tri = (same & (j <= i)).astype(np.float32)
    upp = (same & (j > i)).astype(np.float32)
    jj = np.arange(CH)[:, None]
    ii = np.arange(CH)[None, :]
    m64 = (jj <= ii).astype(np.float32)
    return tri, upp, np.tile(m64, (1, B_HEADS))


class Prog:
    def __init__(self, cfg):
        self.cfg = cfg
        self.nc = bass.Bass(target_bir_lowering=False)
        self.ctx = ExitStack()
        self.sc = Sched(self.nc, self.ctx)
        self.dbg = {}

    def din(self, name, shape, dt=F32):
        return self.nc.dram_tensor(name, list(shape), dt, kind="ExternalInput").ap()

    def dout(self, name, shape, dt=F32):
        return self.nc.dram_tensor(name, list(shape), dt, kind="ExternalOutput").ap()

    def dscr(self, name, shape, dt=BF16, dbg=False):
        if dbg and self.cfg.debug:
            t = self.nc.dram_tensor(name, list(shape), dt, kind="ExternalOutput").ap()
            self.dbg[name] = t
            return t
        return self.nc.dram_tensor(name, list(shape), dt).ap()


def build(cfg):
    P = Prog(cfg)
    nc, sc = P.nc, P.sc
    D, S, DFF, NL, KC = cfg.D, cfg.S, cfg.DFF, cfg.NL, cfg.KC
    KF = DFF // 128
    NT = S // 512
    offs = cfg.offs

    xT = P.din("xT", [D, S])
    outT = P.dout("outT", [D, S])
    w_in = P.din("w_in", [NL, D, cfg.NIN])
    w_gate = P.din("w_gla_gate", [NL, B_RANK, 512])
    b_gate = P.din("b_gla_gate", [NL, 512])
    gla_norm = P.din("gla_norm", [NL, 128])
    sinks = P.din("attn_sinks", [NL, C_QH])
    w_bra = P.din("w_br_a", [NL, 512, D])
    w_brb = P.din("w_br_b", [NL, 1024, D])
    w_brc = P.din("w_br_c", [NL, 1536, D])
    w_out = P.din("w_out", [NL, D, D])
    g_in = {n: P.din(n, [NL, D]) for n in ("g_pre_mix", "g_post_mix", "g_pre_ffn", "g_post_ffn")}
    w_up = P.din("w_up", [NL, D, 2 * DFF])
    conv_w = P.din("conv_w", [NL, 3, 2 * DFF])
    conv_b = P.din("conv_b", [NL, 2 * DFF])
    w_down = P.din("w_down", [NL, DFF, D])
    biasg = P.din("biasg", [24, 128, 256])
    maskc = P.din("maskc", [4, 128, 256])
    tri_in = P.din("gla_tri", [128, 128])
    upp_in = P.din("gla_upp", [128, 128])
    m64_in = P.din("gla_m64", [CH, CH * B_HEADS])

    o = offs
    in_tiles = []
    for i in range(3):
        in_tiles.append(("F", o[0] + 512 * i, 512, "aqT", 512 * i, None))
    for i in range(3):
        in_tiles.append(("F", o[1] + 512 * i, 512, "akT", 512 * i, None))
    for i in range(3):
        in_tiles.append(("T", o[2] + 512 * i, 512, "av", 512 * i, None))
    in_tiles.append(("F", o[3], 512, "bqT", 0, None))
    in_tiles.append(("F", o[4], 512, "bkT", 0, None))
    in_tiles.append(("T", o[4], 512, "bk", 0, None))
    for i in range(2):
        in_tiles.append(("T", o[5] + 512 * i, 512, "bv", 512 * i, None))
    for i in range(2):
        in_tiles.append(("F", o[6] + 512 * i, 512, "brT", 512 * i, AF.Silu))
    in_tiles.append(("F", o[7], 16, "bgT", 0, None))
    for i in range(3):
        in_tiles.append(("F", o[8] + 512 * i, 512, "cqT", 512 * i, None))
    in_tiles.append(("F", o[9], 384, "ckT", 0, None))
    in_tiles.append(("T", o[10], 384, "cv", 0, None))
    for gi, gname in enumerate(("gaT", "gbT", "gcT")):
        for i in range(D // 512):
            in_tiles.append(("F", o[11 + gi] + 512 * i, 512, gname, 512 * i, AF.Sigmoid))
    NIT = len(in_tiles)

    NUT = DFF // 256
    Wb_in = [P.dscr("Wb_in%d" % l, [NIT, 128, KC * 512]) for l in range(NL)]
    Wb_br = [P.dscr("Wb_br%d" % l, [D // 512, 128, 24 * 512]) for l in range(NL)]
    Wb_out = [P.dscr("Wb_out%d" % l, [D // 512, 128, KC * 512]) for l in range(NL)]
    Wb_up = [P.dscr("Wb_up%d" % l, [NUT, 128, KC * 512]) for l in range(NL)]
    Wb_dn = [P.dscr("Wb_dn%d" % l, [D // 128, 128, KF * 128]) for l in range(NL)]

    XN = P.dscr("XN", [D, S], BF16, dbg=True)
    XA = P.dscr("XA", [D, S], F32, dbg=True)
    XB = P.dscr("XB", [D, S], F32)
    YT = P.dscr("YT", [D, S], F32, dbg=True)
    MT = P.dscr("MT", [D, S], BF16, dbg=True)
    HT = P.dscr("HT", [DFF, S], BF16, dbg=True)
    pr = {
        "aqT": P.dscr("aqT", [1536, S], BF16, dbg=True), "akT": P.dscr("akT", [1536, S], BF16),
        "av": P.dscr("av", [S, 1536], BF16, dbg=True),
        "bqT": P.dscr("bqT", [512, S]), "bkT": P.dscr("bkT", [512, S]), "bk": P.dscr("bk", [S, 512]),
        "bv": P.dscr("bv", [S, 1024]), "brT": P.dscr("brT", [1024, S], BF16, dbg=True),
        "bgT": P.dscr("bgT", [16, S]),
        "cqT": P.dscr("cqT", [1536, S]), "ckT": P.dscr("ckT", [384, S]), "cv": P.dscr("cv", [S, 384]),
        "gaT": P.dscr("gaT", [D, S], BF16, dbg=True), "gbT": P.dscr("gbT", [D, S]), "gcT": P.dscr("gcT", [D, S]),
    }
    OA = P.dscr("OA", [512, S], BF16, dbg=True)
    OB = P.dscr("OB", [1024, S], BF16, dbg=True)
    OC = P.dscr("OC", [1536, S], BF16, dbg=True)

    ctx = P.ctx

    uid = [0]

    def sb(name, shape, dt, c=None):
        uid[0] += 1
        return (c or ctx).enter_context(nc.sbuf_tensor("%s_%d" % (name, uid[0]), list(shape), dt))

    def psum_banks(c, n=8):
        uid[0] += 1
        return [c.enter_context(nc.psum_tensor("ps%d_%d" % (i, uid[0]), [128, 512], F32)) for i in range(n)]

    ones_bf = sb("ones_bf", [128, 128], BF16)
    ones_f = sb("ones_f", [128, 128], F32)
    sc.op("pool", lambda e: e.memset(ones_bf[:], 1.0), writes=["ones_bf"])
    sc.op("pool", lambda e: e.memset(ones_f[:], 1.0), writes=["ones_f"])

    def phase_norm(x_src, y_src, gpost, x_dst, gnext, xn_dst):
        TN = 128
        with ExitStack() as c:
            ps = psum_banks(c, 4)
            gp = sb("gp", [128, KC], F32, c)
            gn = sb("gn", [128, KC], F32, c)
            if gpost is not None:
                sc.dma("sp", gp[:], gpost.rearrange("(kc p) -> p kc", p=128), writes=["gp"], slow=True)
            if gnext is not None:
                sc.dma("sp", gn[:], gnext.rearrange("(kc p) -> p kc", p=128), writes=["gn"], slow=True)
            NS = 3
            xt = [sb("nx%d" % i, [128, KC, TN], F32, c) for i in range(NS)]
            yt = [sb("ny%d" % i, [128, KC, TN], F32, c) for i in range(NS)]
            xo = [sb("no%d" % i, [128, KC, TN], BF16, c) for i in range(NS)]
            sq = [sb("nsq%d" % i, [128, KC, TN], F32, c) for i in range(2)]
            red = [sb("nrd%d" % i, [128, TN], F32, c) for i in range(4)]
            rs = [sb("nrs%d" % i, [128, TN], F32, c) for i in range(4)]
            cnt = {"s": 0, "m": 0}

            def stats(src_tile, src_key):
                i = cnt["s"]
                cnt["s"] += 1
                q, r = i % 2, i % 4
                sc.op("act", lambda e, o_=sq[q][:], i_=src_tile[:]: e.activation(out=o_, in_=i_, func=AF.Square),
                      reads=[src_key], writes=[("nsq", q)])
                sc.op("dve", lambda e, o_=red[r][:], i_=sq[q][:].rearrange("p k t -> p t k"):
                      e.reduce_sum(out=o_, in_=i_, axis=mybir.AxisListType.X),
                      reads=[("nsq", q)], writes=[("nrd", r)])
                sc.op("pe", lambda e, o_=ps[r][:, 0:TN], r_=red[r][:]: e.matmul(o_, ones_f[:], r_, start=True, stop=True),
                      reads=[("nrd", r), "ones_f"], writes=[("nps", r)])
                sc.op("act", lambda e, o_=rs[r][:], i_=ps[r][:, 0:TN]:
                      e.activation(out=o_, in_=i_, func=AF.Ln, scale=1.0 / D, bias=EPS),
                      reads=[("nps", r)], writes=[("nrs", r)])
                sc.op("act", lambda e, o_=rs[r][:]: e.activation(out=o_, in_=o_, func=AF.Exp, scale=-0.5),
                      reads=[("nrs", r)], writes=[("nrs", r)])
                return r

            def scale_rows(dst_tile, dst_key, src_tile, src_key, gt, gkey, r):
                for kc in range(KC):
                    eng = "dve"
                    cnt["m"] += 1
                    wk = (dst_key, eng, kc % 2)
                    sc.op(eng, lambda e, o_=dst_tile[:, kc, :], i_=src_tile[:, kc, :], g_=gt[:, kc:kc + 1], r_=rs[r][:]:
                          e.scalar_tensor_tensor(out=o_, in0=i_, scalar=g_, in1=r_, op0=ALU.mult, op1=ALU.mult),
                          reads=[gkey, ("nrs", r)] + ([src_key] if src_key != dst_key else []), writes=[wk])
                return [(dst_key, en, u) for en in ("dve", "pool") for u in (0, 1)]

            for t in range(S // TN):
                b = t % NS
                ts = slice(t * TN, (t + 1) * TN)
                sc.dma("sp", xt[b][:], x_src.rearrange("(kc p) s -> p kc s", p=128)[:, :, ts], writes=[("nx", b), ("nxa", b)])
                if y_src is not None:
                    yk = ("ny", b)
                    ysub = [(yk, en, u) for en in ("dve", "pool") for u in (0, 1)]
                    sc.dma("sp", yt[b][:], y_src.rearrange("(kc p) s -> p kc s", p=128)[:, :, ts], writes=[yk] + ysub)
                    r = stats(yt[b], yk)
                    for kc in range(KC):
                        eng = "dve"
                        cnt["m"] += 1
                        sc.op(eng, lambda e, o_=yt[b][:, kc, :], g_=gp[:, kc:kc + 1], r_=rs[r][:]:
                              e.scalar_tensor_tensor(out=o_, in0=o_, scalar=g_, in1=r_, op0=ALU.mult, op1=ALU.mult),
                              reads=["gp", ("nrs", r), yk], writes=[(yk, eng, kc % 2)])
                    sc.op("dve", lambda e, o_=xt[b][:], y_=yt[b][:]: e.tensor_tensor(out=o_, in0=o_, in1=y_, op=ALU.add),
                          reads=ysub + [("nx", b)], writes=[("nx", b), ("nxa", b), yk])
                    sc.dma("pool", x_dst.rearrange("(kc p) s -> p kc s", p=128)[:, :, ts], xt[b][:], reads=[("nx", b)])
                if gnext is not None:
                    r = stats(xt[b], ("nx", b))
                    ok = ("no", b)
                    osub = [(ok, en, u) for en in ("dve", "pool") for u in (0, 1)]
                    for kc in range(KC):
                        eng = "dve"
                        cnt["m"] += 1
                        sc.op(eng, lambda e, o_=xo[b][:, kc, :], i_=xt[b][:, kc, :], g_=gn[:, kc:kc + 1], r_=rs[r][:]:
                              e.scalar_tensor_tensor(out=o_, in0=i_, scalar=g_, in1=r_, op0=ALU.mult, op1=ALU.mult),
                              reads=["gn", ("nrs", r), ("nxa", b)], writes=[(ok, eng, kc % 2)])
                    sc.dma("pool", xn_dst.rearrange("(kc p) s -> p kc s", p=128)[:, :, ts], xo[b][:], reads=osub)
            sc.emit()
        sc.barrier()

    def make_wfetch(c, cw_buf, kcn, nstg=4):
        stg = [sb("wst%d" % i, [128, 2048], F32, c) for i in range(nstg)]
        st = {"n": 0}
        engs = ("dve", "pool", "act", "dve")

        def fetch(first, wbuf, wkey, wtile_dram, pieces, dkey):
            wkeys = [(wkey, en) for en in ("pool", "dve", "act")]
            if not first:
                sc.dma("sp", wbuf[:], wtile_dram, reads=[dkey], writes=wkeys)
                return
            wv = wbuf[:].rearrange("p (k c) -> p k c", c=cw_buf)
            for (src, k0, kn, c0, cwp) in pieces:
                i = st["n"] % nstg
                st["n"] += 1
                sview = stg[i][:, 0:kn * cwp].rearrange("p (k c) -> p k c", c=cwp)
                sc.dma("sp", sview, src, writes=[("wst", i)])
                eng = engs[st["n"] % 4]
                dstv = wv[:, k0:k0 + kn, c0:c0 + cwp]
                if eng == "act":
                    sc.op("act", lambda e, o_=dstv, i_=sview: e.copy(out=o_, in_=i_), reads=[("wst", i)], writes=[(wkey, eng)])
                else:
                    sc.op(eng, lambda e, o_=dstv, i_=sview: e.tensor_copy(out=o_, in_=i_), reads=[("wst", i)], writes=[(wkey, eng)])
            sc.dma("pool", wtile_dram, wbuf[:], reads=wkeys, writes=[dkey])
        return fetch

    def pieces_of(src2d, col0, ncol, kcn, k_dst0=0, c_dst0=0):
        v = src2d.rearrange("(kc p) n -> p kc n", p=128)
        g = max(1, 2048 // ncol)
        out = []
        k0 = 0
        while k0 < kcn:
            kn = min(g, kcn - k0)
            out.append((v[:, k0:k0 + kn, col0:col0 + ncol], k_dst0 + k0, kn, c_dst0, ncol))
            k0 += kn
        return out

    def phase_dense(inp, kcn, wtiles, ntiles, cw, handler, wsrc, nbuf_in=2):
        with ExitStack() as c:
            ps = psum_banks(c, 8)
            xin = [sb("din%d" % i, [128, kcn, 512], BF16, c) for i in range(nbuf_in)]
            wb = [sb("dw%d" % i, [128, kcn * cw], BF16, c) for i in range(2)]
            fetch = make_wfetch(c, cw, kcn)
            f = handler(c, ps)
            nw = 0
            for tt in range(NT):
                ib = tt % nbuf_in
                sc.dma("sp", xin[ib][:], inp.rearrange("(kc p) s -> p kc s", p=128)[:, :, tt * 512:(tt + 1) * 512],
                       writes=[("din", ib)])
                for j in range(ntiles):
                    b = nw % 2
                    nw += 1
                    fetch(tt == 0, wb[b], ("dw", b), wtiles[j], wsrc(j) if tt == 0 else None, ("wdram", j))
                    f(tt, j, wb[b][:].rearrange("p (k c) -> p k c", c=cw), xin[ib], ("din", ib),
                      [(("dw", b), en) for en in ("pool", "dve", "act")])
            sc.emit()
        sc.barrier()

    def h_inproj(l):
        def mk(c, ps):
            stg = [sb("ist%d" % i, [128, 4, 512], BF16, c) for i in range(3)]
            state = {"n": 0, "pb": 0, "e": 0}

            def f(tt, j, w, xin, kin, kw):
                kind, c0, ncol, dname, doff, act = in_tiles[j]
                q = state["n"] % 3
                state["n"] += 1
                dst = pr[dname]
                if kind == "F":
                    nb = (ncol + 127) // 128
                    for cb in range(nb):
                        m = min(128, ncol - cb * 128)
                        pb = state["pb"] % 8
                        state["pb"] += 1
                        mm_group(sc, ps[pb][0:m, :], [(w[:, kc, cb * 128:cb * 128 + m], xin[:, kc, :]) for kc in range(KC)],
                                 reads=[kin] + kw, writes=[("ps", pb)])
                        if act is not None:
                            sc.op("act", lambda e, o_=stg[q][0:m, cb, :], i_=ps[pb][0:m, :], a_=act: e.activation(out=o_, in_=i_, func=a_),
                                  reads=[("ps", pb)], writes=[("ist", q)])
                        elif state["e"] % 2 == 0:
                            sc.op("act", lambda e, o_=stg[q][0:m, cb, :], i_=ps[pb][0:m, :]: e.copy(out=o_, in_=i_),
                                  reads=[("ps", pb)], writes=[("ist", q)])
                        else:
                            sc.op("dve", lambda e, o_=stg[q][0:m, cb, :], i_=ps[pb][0:m, :]: e.tensor_copy(out=o_, in_=i_),
                                  reads=[("ps", pb)], writes=[("ist", q)])
                        state["e"] += 1
                    if ncol % 128 == 0:
                        sc.dma("pool", dst[doff:doff + ncol, tt * 512:(tt + 1) * 512].rearrange("(cb p) s -> p cb s", p=128),
                               stg[q][:, 0:nb, :], reads=[("ist", q)])
                    else:
                        sc.dma("pool", dst[doff:doff + ncol, tt * 512:(tt + 1) * 512], stg[q][0:ncol, 0, :], reads=[("ist", q)])
                else:
                    for tb in range(4):
                        pb = state["pb"] % 8
                        state["pb"] += 1
                        mm_group(sc, ps[pb][:, 0:ncol], [(xin[:, kc, tb * 128:(tb + 1) * 128], w[:, kc, 0:ncol]) for kc in range(KC)],
                                 reads=[kin] + kw, writes=[("ps", pb)])
                        if state["e"] % 2 == 0:
                            sc.op("act", lambda e, o_=stg[q][:, tb, 0:ncol], i_=ps[pb][:, 0:ncol]: e.copy(out=o_, in_=i_),
                                  reads=[("ps", pb)], writes=[("ist", q)])
                        else:
                            sc.op("dve", lambda e, o_=stg[q][:, tb, 0:ncol], i_=ps[pb][:, 0:ncol]: e.tensor_copy(out=o_, in_=i_),
                                  reads=[("ps", pb)], writes=[("ist", q)])
                        state["e"] += 1
                    sc.dma("pool", dst[tt * 512:(tt + 1) * 512, doff:doff + ncol].rearrange("(tb p) n -> p tb n", p=128),
                           stg[q][:, :, 0:ncol], reads=[("ist", q)])
            return f
        return mk

    def h_f32out(dst, cw):
        def mk(c, ps):
            nb = cw // 128
            stg = [sb("fst%d" % i, [128, nb, 512], F32, c) for i in range(3)]
            state = {"n": 0, "pb": 0}

            def f(tt, j, w, xin, kin, kw):
                kcn = w.shape[1]
                q = state["n"] % 3
                state["n"] += 1
                for cb in range(nb):
                    pb = state["pb"] % 8
                    state["pb"] += 1
                    mm_group(sc, ps[pb][:, :], [(w[:, kc, cb * 128:(cb + 1) * 128], xin[:, kc, :]) for kc in range(kcn)],
                             reads=[kin] + kw, writes=[("ps", pb)])
                    if state["pb"] % 2 == 0:
                        sc.op("act", lambda e, o_=stg[q][:, cb, :], i_=ps[pb][:, :]: e.copy(out=o_, in_=i_),
                              reads=[("ps", pb)], writes=[("fst", q)])
                    else:
                        sc.op("dve", lambda e, o_=stg[q][:, cb, :], i_=ps[pb][:, :]: e.tensor_copy(out=o_, in_=i_),
                              reads=[("ps", pb)], writes=[("fst", q)])
                sc.dma("pool", dst[j * cw:(j + 1) * cw, tt * 512:(tt + 1) * 512].rearrange("(cb p) s -> p cb s", p=128),
                       stg[q][:], reads=[("fst", q)])
            return f
        return mk

    def h_ffn_up(l):
        def mk(c, ps):
            cwt = sb("cwt", [128, 3, 2 * KF], F32, c)
            cbt = sb("cbt", [128, 2 * KF], F32, c)
            sc.dma("sp", cwt[:], conv_w[l].rearrange("t (b p) -> p t b", p=128), writes=["cwt"], slow=True)
            sc.dma("sp", cbt[:], conv_b[l].rearrange("(b p) -> p b", p=128), writes=["cbt"], slow=True)
            carry = sb("carry", [128, 2 * KF, 2], F32, c)
            sc.op("pool", lambda e: e.memset(carry[:], 0.0), writes=["carry"])
            yb = [sb("yb%d" % i, [128, 514], F32, c) for i in range(4)]
            ub = [sb("ub%d" % i, [128, 512], F32, c) for i in range(4)]
            hst = [sb("hst%d" % i, [128, 2, 512], BF16, c) for i in range(3)]
            state = {"n": 0, "pb": 0, "y": 0}

            def f(tt, j, w, xin, kin, kw):
                q = state["n"] % 3
                state["n"] += 1
                for half in range(2):
                    res = []
                    for which in range(2):
                        blk = (KF * which) + 2 * j + half
                        wc = which * 256 + half * 128
                        pb = state["pb"] % 8
                        state["pb"] += 1
                        mm_group(sc, ps[pb][:, :], [(w[:, kc, wc:wc + 128], xin[:, kc, :]) for kc in range(KC)],
                                 reads=[kin] + kw, writes=[("ps", pb)])
                        yi = state["y"] % 4
                        state["y"] += 1
                        ck = ("carry", blk)
                        sc.op("pool", lambda e, o_=yb[yi][:, 0:2], i_=carry[:, blk, :]: e.tensor_copy(out=o_, in_=i_),
                              reads=[ck, "carry"], writes=[("yb", yi)])
                        sc.op("act", lambda e, o_=yb[yi][:, 2:514], i_=ps[pb][:, :]: e.copy(out=o_, in_=i_),
                              reads=[("ps", pb)], writes=[("yb", yi)])
                        sc.op("pool", lambda e, o_=carry[:, blk, :], i_=yb[yi][:, 512:514]: e.tensor_copy(out=o_, in_=i_),
                              reads=[("yb", yi)], writes=[ck])
                        sc.op("act", lambda e, o_=ub[yi][:], i_=yb[yi][:, 2:514], s_=cwt[:, 2, blk:blk + 1], b_=cbt[:, blk:blk + 1]:
                              e.activation(out=o_, in_=i_, func=AF.Identity, scale=s_, bias=b_),
                              reads=[("yb", yi), "cwt", "cbt"], writes=[("ub", yi)])
                        sc.op("dve", lambda e, o_=ub[yi][:], i_=yb[yi][:, 1:513], s_=cwt[:, 1, blk:blk + 1]:
                              e.scalar_tensor_tensor(out=o_, in0=i_, scalar=s_, in1=o_, op0=ALU.mult, op1=ALU.add),
                              reads=[("yb", yi), "cwt", ("ub", yi)], writes=[("ub", yi)])
                        sc.op("dve", lambda e, o_=ub[yi][:], i_=yb[yi][:, 0:512], s_=cwt[:, 0, blk:blk + 1]:
                              e.scalar_tensor_tensor(out=o_, in0=i_, scalar=s_, in1=o_, op0=ALU.mult, op1=ALU.add),
                              reads=[("yb", yi), "cwt", ("ub", yi)], writes=[("ub", yi)])
                        res.append(yi)
                    gi, ui = res
                    sc.op("act", lambda e, o_=yb[gi][:, 0:512], i_=ub[gi][:]: e.activation(out=o_, in_=i_, func=AF.Silu),
                          reads=[("ub", gi)], writes=[("yb", gi)])
                    sc.op("dve", lambda e, o_=hst[q][:, half, :], a_=yb[gi][:, 0:512], b_=ub[ui][:]:
                          e.tensor_tensor(out=o_, in0=a_, in1=b_, op=ALU.mult),
                          reads=[("yb", gi), ("ub", ui)], writes=[("hst", q)])
                sc.dma("pool", HT[256 * j:256 * j + 256, tt * 512:(tt + 1) * 512].rearrange("(cb p) s -> p cb s", p=128),
                       hst[q][:], reads=[("hst", q)])
            return f
        return mk

    def phase_merge(l):
        with ExitStack() as c:
            ps = psum_banks(c, 8)
            xin = [sb("min%d" % i, [128, 24, 512], BF16, c) for i in range(2)]
            wb = [sb("mw%d" % i, [128, 24 * 512], BF16, c) for i in range(2)]
            gt = [sb("mg%d" % i, [128, 3, 4, 512], BF16, c) for i in range(2)]
            t1 = [sb("mt1_%d" % i, [128, 512], F32, c) for i in range(2)]
            t2 = [sb("mt2_%d" % i, [128, 512], F32, c) for i in range(2)]
            t3 = [sb("mt3_%d" % i, [128, 512], F32, c) for i in range(2)]
            stg = [sb("mst%d" % i, [128, 4, 512], BF16, c) for i in range(3)]
            fetch = make_wfetch(c, 512, 24, nstg=2)
            nw = 0
            npb = 0
            nq = 0
            gsrc = (pr["gaT"], pr["gbT"], pr["gcT"])
            for tt in range(NT):
                ib = tt % 2
                tsl = slice(tt * 512, (tt + 1) * 512)
                sc.dma("sp", xin[ib][:, 0:4, :], OA.rearrange("(kc p) s -> p kc s", p=128)[:, :, tsl], writes=[("min", ib)])
                sc.dma("sp", xin[ib][:, 4:12, :], OB.rearrange("(kc p) s -> p kc s", p=128)[:, :, tsl], writes=[("min", ib)])
                sc.dma("sp", xin[ib][:, 12:24, :], OC.rearrange("(kc p) s -> p kc s", p=128)[:, :, tsl], writes=[("min", ib)])
                for j in range(D // 512):
                    b = nw % 2
                    nw += 1
                    pcs = None
                    if tt == 0:
                        pcs = (pieces_of(w_bra[l], 512 * j, 512, 4, 0) + pieces_of(w_brb[l], 512 * j, 512, 8, 4)
                               + pieces_of(w_brc[l], 512 * j, 512, 12, 12))
                    fetch(tt == 0, wb[b], ("mw", b), Wb_br[l][j], pcs, ("wdram", j))
                    mwk = [(("mw", b), en) for en in ("pool", "dve", "act")]
                    for gi in range(3):
                        sc.dma("sp", gt[b][:, gi, :, :],
                               gsrc[gi][512 * j:512 * j + 512, tsl].rearrange("(cb p) s -> p cb s", p=128), writes=[("mg", b)])
                    w = wb[b][:].rearrange("p (k c) -> p k c", c=512)
                    q = nq % 3
                    nq += 1
                    for cb in range(4):
                        pbs = []
                        for (k0, k1) in ((0, 4), (4, 12), (12, 24)):
                            pb = npb % 8
                            npb += 1
                            mm_group(sc, ps[pb][:, :], [(w[:, kc, cb * 128:(cb + 1) * 128], xin[ib][:, kc, :]) for kc in range(k0, k1)],
                                     reads=[("min", ib)] + mwk, writes=[("ps", pb)])
                            pbs.append(pb)
                        u = cb % 2
                        for tbuf, nm, pb, gi in ((t1, "mt1", pbs[0], 0), (t2, "mt2", pbs[1], 1), (t3, "mt3", pbs[2], 2)):
                            sc.op("dve", lambda e, o_=tbuf[u][:], p_=ps[pb][:, :], g_=gt[b][:, gi, cb, :]:
                                  e.tensor_tensor(out=o_, in0=p_, in1=g_, op=ALU.mult),
                                  reads=[("ps", pb), ("mg", b)], writes=[(nm, u)])
                        sc.op("pool", lambda e, o_=t1[u][:], i_=t2[u][:]: e.tensor_tensor(out=o_, in0=o_, in1=i_, op=ALU.add),
                              reads=[("mt1", u), ("mt2", u)], writes=[("mt1", u)])
                        sc.op("pool", lambda e, o_=stg[q][:, cb, :], a_=t1[u][:], i_=t3[u][:]: e.tensor_tensor(out=o_, in0=a_, in1=i_, op=ALU.add),
                              reads=[("mt1", u), ("mt3", u)], writes=[("mst", q)])
                    sc.dma("pool", MT[512 * j:512 * j + 512, tsl].rearrange("(cb p) s -> p cb s", p=128), stg[q][:], reads=[("mst", q)])
            sc.emit()
        sc.barrier()

    def phase_attn_A(l):
        NSUP = S // 2048
        scale = float(HD) ** -0.5
        with ExitStack() as c:
            ps = psum_banks(c, 8)
            bm = sb("bmA", [128, 12, 256], F32, c)
            mk_ = sb("mkA", [128, 3, 256], F32, c)
            sc.dma("sp", bm[:], biasg[0:12].rearrange("h k q -> k h q"), writes=["bm"])
            sc.dma("sp", mk_[:], maskc[0:3].rearrange("g k q -> k g q"), writes=["mk"])
            for h in range(12):
                sc.op("dve", lambda e, o_=bm[:, h, :], m_=mk_[:, h // 4, :]: e.tensor_tensor(out=o_, in0=o_, in1=m_, op=ALU.add),
                      reads=["bm", "mk"], writes=["bm"])
            qt = [sb("aq%d" % i, [128, 2048], BF16, c) for i in range(2)]
            kt = [sb("ak%d" % i, [128, 4096], BF16, c) for i in range(2)]
            vc = [sb("avc%d" % i, [128, 16, 128], BF16, c) for i in range(2)]
            vp = [sb("avp%d" % i, [128, 16, 128], BF16, c) for i in range(2)]
            num = sb("anum", [128, 2048], F32, c)
            den = sb("aden", [128, 2048], F32, c)
            lg = [sb("alg%d" % i, [128, 512], F32, c) for i in range(4)]
            pt = [sb("apt%d" % i, [128, 512], BF16, c) for i in range(4)]
            ost = [sb("aost%d" % i, [128, 2048], BF16, c) for i in range(2)]
            nld = 0
            npb = 0
            nlg = 0
            nun = 0
            for j in range(A_SLOTS):
                for u in range(NSUP):
                    t0 = u * 2048
                    for g, (win, d) in enumerate(DIL):
                        h = 4 * g + j
                        b = nld % 2
                        nld += 1
                        hr = slice(h * 128, (h + 1) * 128)
                        sc.dma("sp", qt[b][:], pr["aqT"][hr, t0:t0 + 2048], writes=[("aq", b)])
                        sc.dma("sp", kt[b][:, 2048:4096], pr["akT"][hr, t0:t0 + 2048], writes=[("ak", b)])
                        back = {1: 128, 4: 512, 16: 2048}[d]
                        if u > 0:
                            sc.dma("sp", kt[b][:, 4096 - 2048 - back:2048], pr["akT"][hr, t0 - back:t0], writes=[("ak", b)])
                        vsrc = pr["av"][t0:t0 + 2048, hr]
                        if d == 1:
                            sc.dma("sp", vc[b][:], vsrc.rearrange("(n i) e -> i n e", i=128), writes=[("avc", b)])
                        elif d == 4:
                            for m in range(4):
                                sc.dma("sp", vc[b][:, 4 * m:4 * m + 4, :],
                                       vsrc[512 * m:512 * m + 512, :].rearrange("(i r) e -> i r e", r=4), writes=[("avc", b)])
                        else:
                            sc.dma("sp", vc[b][:], vsrc.rearrange("(i r) e -> i r e", r=16), writes=[("avc", b)])
                        if u > 0:
                            vps = pr["av"][t0 - back:t0, hr]
                            if d == 1:
                                sc.dma("sp", vp[b][:, 0, :], vps, writes=[("avp", b)])
                            elif d == 4:
                                sc.dma("sp", vp[b][:, 0:4, :], vps.rearrange("(i r) e -> i r e", r=4), writes=[("avp", b)])
                            else:
                                sc.dma("sp", vp[b][:], vps.rearrange("(i r) e -> i r e", r=16), writes=[("avp", b)])
                        blocks = []
                        if d == 1:
                            for n in range(16):
                                blocks.append((n * 128, n, (n - 1) if n > 0 else None, 0))
                        elif d == 4:
                            for m in range(4):
                                for r in range(4):
                                    blocks.append((m * 512 + r, 4 * m + r, (4 * (m - 1) + r) if m > 0 else None, r))
                        else:
                            for r in range(16):
                                blocks.append((r, r, None, r))
                        for bi in range(0, 16, 2):
                            pb_s = npb % 8
                            npb += 1
                            li = nlg % 4
                            nlg += 1
                            info = []
                            for x in range(2):
                                f, cur_idx, prev_in_cur, r = blocks[bi + x]
                                qv = qt[b][:, f:f + 127 * d + 1:d] if d > 1 else qt[b][:, f:f + 128]
                                kc_ = kt[b][:, 2048 + f:2048 + f + 127 * d + 1:d] if d > 1 else kt[b][:, 2048 + f:2048 + f + 128]
                                fp = 2048 + f - 128 * d
                                has_prev = (u > 0) or (f - 128 * d >= 0)
                                col = x * 256
                                if has_prev:
                                    kp_ = kt[b][:, fp:fp + 127 * d + 1:d] if d > 1 else kt[b][:, fp:fp + 128]
                                    mm_group(sc, ps[pb_s][:, col:col + 128], [(kp_, qv)], reads=[("ak", b), ("aq", b)], writes=[("ps", pb_s)])
                                mm_group(sc, ps[pb_s][:, col + 128:col + 256], [(kc_, qv)], reads=[("ak", b), ("aq", b)], writes=[("ps", pb_s)])
                                if prev_in_cur is not None:
                                    vprev = vc[b][:, prev_in_cur, :]
                                elif has_prev:
                                    vprev = vp[b][:, r if d > 1 else 0, :]
                                else:
                                    vprev = None
                                info.append((f, has_prev, vprev, vc[b][:, cur_idx, :], col))
                            for x in range(2):
                                f, has_prev, vprev, vcur, col = info[x]
                                c0 = col if has_prev else col + 128
                                bc0 = 0 if has_prev else 128
                                sc.op("dve", lambda e, o_=lg[li][:, c0:col + 256], p_=ps[pb_s][:, c0:col + 256], b_=bm[:, h, bc0:256]:
                                      e.scalar_tensor_tensor(out=o_, in0=p_, scalar=scale, in1=b_, op0=ALU.mult, op1=ALU.add),
                                      reads=[("ps", pb_s), "bm"], writes=[("alg", li)])
                                sc.op("act", lambda e, o_=pt[li][:, c0:col + 256], i_=lg[li][:, c0:col + 256]: e.activation(out=o_, in_=i_, func=AF.Exp),
                                      reads=[("alg", li)], writes=[("apt", li)])
                            pb_n = npb % 8
                            npb += 1
                            pb_d = npb % 8
                            npb += 1
                            for x in range(2):
                                f, has_prev, vprev, vcur, col = info[x]
                                prs_n = []
                                prs_d = []
                                if has_prev:
                                    prs_n.append((vprev, pt[li][:, col:col + 128]))
                                    prs_d.append((ones_bf[:], pt[li][:, col:col + 128]))
                                prs_n.append((vcur, pt[li][:, col + 128:col + 256]))
                                prs_d.append((ones_bf[:], pt[li][:, col + 128:col + 256]))
                                mm_group(sc, ps[pb_n][:, x * 128:(x + 1) * 128], prs_n, reads=[("apt", li), ("avc", b), ("avp", b)], writes=[("ps", pb_n)])
                                mm_group(sc, ps[pb_d][:, x * 128:(x + 1) * 128], prs_d, reads=[("apt", li), "ones_bf"], writes=[("ps", pb_d)])
                            for x in range(2):
                                f = info[x][0]
                                nv = num[:, f:f + 127 * d + 1:d] if d > 1 else num[:, f:f + 128]
                                dv = den[:, f:f + 127 * d + 1:d] if d > 1 else den[:, f:f + 128]
                                if g == 0:
                                    sc.op("act", lambda e, o_=nv, i_=ps[pb_n][:, x * 128:(x + 1) * 128]: e.copy(out=o_, in_=i_),
                                          reads=[("ps", pb_n)], writes=["anum"])
                                    sc.op("dve", lambda e, o_=dv, i_=ps[pb_d][:, x * 128:(x + 1) * 128]: e.tensor_copy(out=o_, in_=i_),
                                          reads=[("ps", pb_d)], writes=["aden"])
                                else:
                                    sc.op("dve", lambda e, o_=nv, i_=ps[pb_n][:, x * 128:(x + 1) * 128]: e.tensor_tensor(out=o_, in0=o_, in1=i_, op=ALU.add),
                                          reads=[("ps", pb_n), "anum"], writes=["anum"])
                                    sc.op("dve", lambda e, o_=dv, i_=ps[pb_d][:, x * 128:(x + 1) * 128]: e.tensor_tensor(out=o_, in0=o_, in1=i_, op=ALU.add),
                                          reads=[("ps", pb_d), "aden"], writes=["aden"])
                    ob = nun % 2
                    nun += 1
                    sc.op("act", lambda e: e.activation(out=den[:], in_=den[:], func=AF.Ln), reads=["aden"], writes=["aden"])
                    sc.op("act", lambda e: e.activation(out=den[:], in_=den[:], func=AF.Exp, scale=-1.0), reads=["aden"], writes=["aden"])
                    sc.op("dve", lambda e, o_=ost[ob][:]: e.tensor_tensor(out=o_, in0=num[:], in1=den[:], op=ALU.mult),
                          reads=["anum", "aden"], writes=[("aost", ob)])
                    sc.dma("pool", OA[j * 128:(j + 1) * 128, t0:t0 + 2048], ost[ob][:], reads=[("aost", ob)])
            sc.emit()
        sc.barrier()

    def phase_attn_C(l):
        NSUP = S // 2048
        scale = float(HD) ** -0.5
        with ExitStack() as c:
            ps = psum_banks(c, 8)
            bm = sb("bmC", [128, 12, 256], F32, c)
            mk_ = sb("mkC", [128, 256], F32, c)
            es = sb("esC", [128, 12], F32, c)
            sc.dma("sp", bm[:], biasg[12:24].rearrange("h k q -> k h q"), writes=["bm"])
            sc.dma("sp", mk_[:], maskc[3], writes=["mk"])
            sc.dma("sp", es[:], sinks[l].partition_broadcast(128), writes=["es"])
            for h in range(12):
                sc.op("dve", lambda e, o_=bm[:, h, :]: e.tensor_tensor(out=o_, in0=o_, in1=mk_[:], op=ALU.add),
                      reads=["bm", "mk"], writes=["bm"])
            sc.op("act", lambda e: e.activation(out=es[:], in_=es[:], func=AF.Exp), reads=["es"], writes=["es"])
            qt = [sb("cq%d" % i, [128, 2048], BF16, c) for i in range(2)]
            kt = [sb("ck%d" % i, [128, 2048 + 128], BF16, c) for i in range(2)]
            vt = [sb("cvt%d" % i, [128, 17, 128], BF16, c) for i in range(2)]
            lg = [sb("clg%d" % i, [128, 512], F32, c) for i in range(4)]
            pt = [sb("cpt%d" % i, [128, 512], BF16, c) for i in range(4)]
            dn = [sb("cdn%d" % i, [128, 256], F32, c) for i in range(4)]
            ost = [sb("cost%d" % i, [128, 2048], BF16, c) for i in range(2)]
            nld = 0
            nkv = 0
            npb = 0
            nlg = 0
            for kvh in range(C_KVH):
                for u in range(NSUP):
                    t0 = u * 2048
                    kb = nkv % 2
                    nkv += 1
                    kr = slice(kvh * 128, (kvh + 1) * 128)
                    sc.dma("sp", kt[kb][:, 128:], pr["ckT"][kr, t0:t0 + 2048], writes=[("ck", kb)])
                    sc.dma("sp", vt[kb][:, 1:17, :], pr["cv"][t0:t0 + 2048, kr].rearrange("(n i) e -> i n e", i=128), writes=[("cvt", kb)])
                    if u > 0:
                        sc.dma("sp", kt[kb][:, 0:128], pr["ckT"][kr, t0 - 128:t0], writes=[("ck", kb)])
                        sc.dma("sp", vt[kb][:, 0, :], pr["cv"][t0 - 128:t0, kr], writes=[("cvt", kb)])
                    for gq in range(4):
                        h = kvh * 4 + gq
                        b = nld % 2
                        nld += 1
                        sc.dma("sp", qt[b][:], pr["cqT"][h * 128:(h + 1) * 128, t0:t0 + 2048], writes=[("cq", b)])
                        for bi in range(0, 16, 2):
                            pb_s = npb % 8
                            npb += 1
                            li = nlg % 4
                            nlg += 1
                            info = []
                            for x in range(2):
                                n = bi + x
                                has_prev = (u > 0) or (n > 0)
                                col = x * 256
                                qv = qt[b][:, n * 128:(n + 1) * 128]
                                if has_prev:
                                    mm_group(sc, ps[pb_s][:, col:col + 128], [(kt[kb][:, n * 128:(n + 1) * 128], qv)],
                                             reads=[("ck", kb), ("cq", b)], writes=[("ps", pb_s)])
                                mm_group(sc, ps[pb_s][:, col + 128:col + 256], [(kt[kb][:, (n + 1) * 128:(n + 2) * 128], qv)],
                                         reads=[("ck", kb), ("cq", b)], writes=[("ps", pb_s)])
                                info.append((n, has_prev, col))
                            for x in range(2):
                                n, has_prev, col = info[x]
                                c0 = col if has_prev else col + 128
                                bc0 = 0 if has_prev else 128
                                sc.op("dve", lambda e, o_=lg[li][:, c0:col + 256], p_=ps[pb_s][:, c0:col + 256], b_=bm[:, h, bc0:256]:
                                      e.scalar_tensor_tensor(out=o_, in0=p_, scalar=scale, in1=b_, op0=ALU.mult, op1=ALU.add),
                                      reads=[("ps", pb_s), "bm"], writes=[("clg", li)])
                                sc.op("act", lambda e, o_=pt[li][:, c0:col + 256], i_=lg[li][:, c0:col + 256]: e.activation(out=o_, in_=i_, func=AF.Exp),
                                      reads=[("clg", li)], writes=[("cpt", li)])
                            pb_n = npb % 8
                            npb += 1
                            pb_d = npb % 8
                            npb += 1
                            for x in range(2):
                                n, has_prev, col = info[x]
                                prs_n = []
                                prs_d = []
                                if has_prev:
                                    prs_n.append((vt[kb][:, n, :], pt[li][:, col:col + 128]))
                                    prs_d.append((ones_bf[:], pt[li][:, col:col + 128]))
                                prs_n.append((vt[kb][:, n + 1, :], pt[li][:, col + 128:col + 256]))
                                prs_d.append((ones_bf[:], pt[li][:, col + 128:col + 256]))
                                mm_group(sc, ps[pb_n][:, x * 128:(x + 1) * 128], prs_n, reads=[("cpt", li), ("cvt", kb)], writes=[("ps", pb_n)])
                                mm_group(sc, ps[pb_d][:, x * 128:(x + 1) * 128], prs_d, reads=[("cpt", li), "ones_bf"], writes=[("ps", pb_d)])
                            sc.op("act", lambda e, o_=dn[li][:], i_=ps[pb_d][:, 0:256], s_=es[:, h:h + 1]:
                                  e.activation(out=o_, in_=i_, func=AF.Ln, bias=s_, scale=1.0),
                                  reads=[("ps", pb_d), "es"], writes=[("cdn", li)])
                            sc.op("act", lambda e, o_=dn[li][:]: e.activation(out=o_, in_=o_, func=AF.Exp, scale=-1.0),
                                  reads=[("cdn", li)], writes=[("cdn", li)])
                            sc.op("dve", lambda e, o_=ost[b][:, bi * 128:(bi + 2) * 128], n_=ps[pb_n][:, 0:256], d_=dn[li][:]:
                                  e.tensor_tensor(out=o_, in0=n_, in1=d_, op=ALU.mult),
                                  reads=[("ps", pb_n), ("cdn", li)], writes=[("cost", b)])
                        sc.dma("pool", OC[h * 128:(h + 1) * 128, t0:t0 + 2048], ost[b][:], reads=[("cost", b)])
            sc.emit()
        sc.barrier()

    def phase_gla(l):
        NBLK = S // 128
        with ExitStack() as c:
            ps = psum_banks(c, 8)
            wg_f = sb("wg_f", [16, 512], F32, c)
            wg = sb("wg", [16, 512], BF16, c)
            bgb = sb("bgb", [128, 512], F32, c)
            tri = sb("tri", [128, 128], F32, c)
            upp = sb("upp", [128, 128], F32, c)
            m64 = sb("m64", [CH, CH * B_HEADS], F32, c)
            gn = sb("glan", [128, 1], F32, c)
            sc.dma("sp", wg_f[:], w_gate[l], writes=["wg_f"])
            sc.op("dve", lambda e: e.tensor_copy(out=wg[:], in_=wg_f[:]), reads=["wg_f"], writes=["wg"])
            sc.dma("sp", bgb[:], b_gate[l].partition_broadcast(128), writes=["bgb"])
            sc.dma("sp", tri[:], tri_in, writes=["tri"])
            sc.dma("sp", upp[:], upp_in, writes=["upp"])
            sc.dma("sp", m64[:], m64_in, writes=["m64"])
            sc.dma("sp", gn[:], gla_norm[l].rearrange("(p o) -> p o", o=1), writes=["gn"], slow=True)
            sc.op("act", lambda e: e.mul(out=gn[:], in_=gn[:], mul=float(np.sqrt(128.0))), reads=["gn"], writes=["gn"])
            state = sb("gstate", [CH, B_HEADS, 128], F32, c)
            state_bf = sb("gstate_bf", [CH, B_HEADS, 128], BF16, c)
            sc.op("pool", lambda e: e.memset(state[:], 0.0), writes=[("gstate", hh_) for hh_ in range(B_HEADS)])
            sc.op("pool", lambda e: e.memset(state_bf[:], 0.0), writes=["gstate_bf"])
            NB2 = 3
            bgt = [sb("gbg%d" % i, [16, 128], BF16, c) for i in range(NB2)]
            zt = [sb("gz%d" % i, [128, 512], F32, c) for i in range(NB2)]
            qT = [sb("gq%d" % i, [CH, B_HEADS, 128], BF16, c) for i in range(NB2)]
            kT = [sb("gk%d" % i, [CH, B_HEADS, 128], BF16, c) for i in range(NB2)]
            ktok = [sb("gkt%d" % i, [128, 512], BF16, c) for i in range(NB2)]
            v128 = [sb("gv%d" % i, [128, 1024], BF16, c) for i in range(NB2)]
            v64 = [sb("gw%d" % i, [CH, 2, 1024], BF16, c) for i in range(NB2)]
            brt = [sb("gbr%d" % i, [128, B_HEADS, 128], BF16, c) for i in range(NB2)]
            eq = [sb("geq%d" % i, [CH, B_HEADS, 128], F32, c) for i in range(NB2)]
            ek = [sb("gek%d" % i, [CH, B_HEADS, 128], F32, c) for i in range(NB2)]
            qd = [sb("gqd%d" % i, [CH, B_HEADS, 128], BF16, c) for i in range(NB2)]
            ki = [sb("gki%d" % i, [CH, B_HEADS, 128], BF16, c) for i in range(NB2)]
            ko = [sb("gko%d" % i, [128, 512], BF16, c) for i in range(NB2)]
            ed = [sb("ged%d" % i, [128, 512], F32, c) for i in range(NB2)]
            am = [sb("gam%d" % i, [CH, 512], BF16, c) for i in range(2)]
            oraw = [sb("gor%d" % i, [128, 512], F32, c) for i in range(2)]
            osq = [sb("gos%d" % i, [128, 512], F32, c) for i in range(2)]
            orn = [sb("grn%d" % i, [128, 512], F32, c) for i in range(2)]
            ost = [sb("gost%d" % i, [128, B_HEADS, 128], BF16, c) for i in range(2)]
            npb = [0]

            def nextpb():
                pb = npb[0] % 8
                npb[0] += 1
                return pb

            for blk in range(NBLK):
                b = blk % NB2
                t0 = blk * 128
                tsl = slice(t0, t0 + 128)
                sc.dma("sp", bgt[b][:], pr["bgT"][:, tsl], writes=[("gbg", b)])
                sc.dma("sp", qT[b][:], pr["bqT"][:, tsl].rearrange("(h k) s -> k h s", k=CH), writes=[("gq", b)])
                sc.dma("sp", kT[b][:], pr["bkT"][:, tsl].rearrange("(h k) s -> k h s", k=CH), writes=[("gk", b)])
                sc.dma("sp", ktok[b][:], pr["bk"][tsl, :], writes=[("gkt", b)])
                sc.dma("sp", v128[b][:], pr["bv"][tsl, :], writes=[("gv", b)])
                sc.dma("sp", v64[b][:], pr["bv"][tsl, :].rearrange("(c j) v -> j c v", j=CH), writes=[("gw", b)])
                sc.dma("sp", brt[b][:], pr["brT"][:, tsl].rearrange("(h v) s -> v h s", v=128), writes=[("gbr", b)])
                pz = nextpb()
                mm_group(sc, ps[pz][:, :], [(bgt[b][:], wg[:])], reads=[("gbg", b), "wg"], writes=[("ps", pz)])
                sc.op("dve", lambda e, o_=zt[b][:], p_=ps[pz][:, :]: e.tensor_tensor(out=o_, in0=p_, in1=bgb[:], op=ALU.add),
                      reads=[("ps", pz), "bgb"], writes=[("gz", b)])
                sc.op("act", lambda e, o_=zt[b][:]: e.activation(out=o_, in_=o_, func=AF.Exp, scale=-1.0), reads=[("gz", b)], writes=[("gz", b)])
                sc.op("act", lambda e, o_=zt[b][:]: e.activation(out=o_, in_=o_, func=AF.Ln, bias=1.0), reads=[("gz", b)], writes=[("gz", b)])
                pc = [nextpb(), nextpb()]
                for h in range(B_HEADS):
                    mm_group(sc, ps[pc[h // 4]][0:CH, (h % 4) * 128:(h % 4 + 1) * 128], [(zt[b][:, h * CH:(h + 1) * CH], tri[:])],
                             reads=[("gz", b), "tri"], writes=[("ps", pc[h // 4])])
                pd = nextpb()
                mm_group(sc, ps[pd][:, :], [(upp[:], zt[b][:])], reads=[("gz", b), "upp"], writes=[("ps", pd)])
                for hh in range(2):
                    sc.op("act", lambda e, o_=eq[b][:, 4 * hh:4 * hh + 4, :], p_=ps[pc[hh]][0:CH, :].rearrange("k (h i) -> k h i", i=128):
                          e.activation(out=o_, in_=p_, func=AF.Exp, scale=-1.0 / B_TAU), reads=[("ps", pc[hh])], writes=[("geq", b)])
                    sc.op("act", lambda e, o_=ek[b][:, 4 * hh:4 * hh + 4, :], p_=ps[pc[hh]][0:CH, :].rearrange("k (h i) -> k h i", i=128):
                          e.activation(out=o_, in_=p_, func=AF.Exp, scale=1.0 / B_TAU), reads=[("ps", pc[hh])], writes=[("gek", b)])
                sc.op("act", lambda e, o_=ed[b][:], p_=ps[pd][:, :]: e.activation(out=o_, in_=p_, func=AF.Exp, scale=-1.0 / B_TAU),
                      reads=[("ps", pd)], writes=[("ged", b)])
                sc.op("dve", lambda e, o_=qd[b][:], q_=qT[b][:], e_=eq[b][:]:
                      e.scalar_tensor_tensor(out=o_, in0=q_, scalar=float(B_DK) ** -0.5, in1=e_, op0=ALU.mult, op1=ALU.mult),
                      reads=[("gq", b), ("geq", b)], writes=[("gqd", b)])
                sc.op("pool", lambda e, o_=ki[b][:], k_=kT[b][:], e_=ek[b][:]: e.tensor_tensor(out=o_, in0=k_, in1=e_, op=ALU.mult),
                      reads=[("gk", b), ("gek", b)], writes=[("gki", b)])
                sc.op("pool", lambda e, o_=ko[b][:], k_=ktok[b][:], e_=ed[b][:]: e.tensor_tensor(out=o_, in0=k_, in1=e_, op=ALU.mult),
                      reads=[("gkt", b), ("ged", b)], writes=[("gko", b)])
                ob = blk % 2
                po = nextpb()
                po2 = nextpb()
                pos = (po, po2)
                for cc in range(2):
                    a = (2 * blk + cc) % 2
                    isl = slice(cc * CH, (cc + 1) * CH)
                    pa = nextpb()
                    for h in range(B_HEADS):
                        mm_group(sc, ps[pa][0:CH, h * CH:(h + 1) * CH], [(ki[b][:, h, isl], qd[b][:, h, isl])],
                                 reads=[("gki", b), ("gqd", b)], writes=[("ps", pa)])
                    sc.op("dve", lambda e, o_=am[a][:], p_=ps[pa][0:CH, :]: e.tensor_tensor(out=o_, in0=p_, in1=m64[:], op=ALU.mult),
                          reads=[("ps", pa), "m64"], writes=[("gam", a)])
                    for h in range(B_HEADS):
                        mm_group(sc, ps[pos[cc]][:, h * CH:(h + 1) * CH],
                                 [(v64[b][:, cc, h * 128:(h + 1) * 128], am[a][:, h * CH:(h + 1) * CH]),
                                  (state_bf[:, h, :], qd[b][:, h, isl])],
                                 reads=[("gw", b), ("gam", a), "gstate_bf", ("gqd", b)], writes=[("ps", pos[cc])])
                    pk = [nextpb(), nextpb()]
                    for h in range(B_HEADS):
                        mm_group(sc, ps[pk[h // 4]][0:CH, (h % 4) * 128:(h % 4 + 1) * 128],
                                 [(ko[b][cc * CH:(cc + 1) * CH, h * CH:(h + 1) * CH], v128[b][cc * CH:(cc + 1) * CH, h * 128:(h + 1) * 128])],
                                 reads=[("gko", b), ("gv", b)], writes=[("ps", pk[h // 4])])
                    for h in range(B_HEADS):
                        sc.op("dve", lambda e, o_=state[:, h, :], d_=eq[b][:, h, cc * CH + CH - 1:cc * CH + CH],
                              p_=ps[pk[h // 4]][0:CH, (h % 4) * 128:(h % 4 + 1) * 128]:
                              e.scalar_tensor_tensor(out=o_, in0=o_, scalar=d_, in1=p_, op0=ALU.mult, op1=ALU.add),
                              reads=[("geq", b), ("ps", pk[h // 4])], writes=[("gstate", h)])
                    sc.op("act", lambda e: e.copy(out=state_bf[:], in_=state[:]),
                          reads=[("gstate", hh_) for hh_ in range(B_HEADS)], writes=["gstate_bf"])
                for cc in range(2):
                    a = (2 * blk + cc) % 2
                    sc.op("act", lambda e, o_=oraw[a][:], p_=ps[pos[cc]][:, :]: e.copy(out=o_, in_=p_), reads=[("ps", pos[cc])], writes=[("gor", a)])
                    sc.op("pool", lambda e, o_=osq[a][:], i_=oraw[a][:]: e.tensor_tensor(out=o_, in0=i_, in1=i_, op=ALU.mult),
                          reads=[("gor", a)], writes=[("gos", a)])
                    pn = nextpb()
                    mm_group(sc, ps[pn][:, :], [(ones_f[:], osq[a][:])], reads=[("gos", a), "ones_f"], writes=[("ps", pn)])
                    sc.op("act", lambda e, o_=orn[a][:], p_=ps[pn][:, :]: e.activation(out=o_, in_=p_, func=AF.Ln, bias=128.0 * EPS),
                          reads=[("ps", pn)], writes=[("grn", a)])
                    sc.op("act", lambda e, o_=orn[a][:]: e.activation(out=o_, in_=o_, func=AF.Exp, scale=-0.5), reads=[("grn", a)], writes=[("grn", a)])
                    sc.op("dve", lambda e, o_=oraw[a][:], r_=orn[a][:]: e.tensor_tensor(out=o_, in0=o_, in1=r_, op=ALU.mult),
                          reads=[("gor", a), ("grn", a)], writes=[("gor", a)])
                    sc.op("dve", lambda e, o_=ost[ob][:, :, cc * CH:(cc + 1) * CH], i_=oraw[a][:].rearrange("v (h i) -> v h i", i=CH),
                          g_=brt[b][:, :, cc * CH:(cc + 1) * CH]:
                          e.scalar_tensor_tensor(out=o_, in0=i_, scalar=gn[:, 0:1], in1=g_, op0=ALU.mult, op1=ALU.mult),
                          reads=[("gor", a), "gn", ("gbr", b)], writes=[("gost", ob)])
                sc.dma("pool", OB[:, tsl].rearrange("(h v) s -> v h s", v=128), ost[ob][:], reads=[("gost", ob)])
            sc.emit()
        sc.barrier()

    sc.emit()
    sc.barrier()
    xcur = xT
    for l in range(NL):
        if l == 0:
            phase_norm(xcur, None, None, None, g_in["g_pre_mix"][l], XN)
        phase_dense(XN, KC, Wb_in[l], NIT, 512, h_inproj(l),
                    lambda j, l=l: pieces_of(w_in[l], in_tiles[j][1], in_tiles[j][2], KC))
        phase_attn_A(l)
        phase_gla(l)
        phase_attn_C(l)
        phase_merge(l)
        phase_dense(MT, KC, Wb_out[l], D // 512, 512, h_f32out(YT, 512),
                    lambda j, l=l: pieces_of(w_out[l], 512 * j, 512, KC))
        phase_norm(xcur, YT, g_in["g_post_mix"][l], XA, g_in["g_pre_ffn"][l], XN)
        phase_dense(XN, KC, Wb_up[l], NUT, 512, h_ffn_up(l),
                    lambda j, l=l: pieces_of(w_up[l], 256 * j, 256, KC, 0, 0) + pieces_of(w_up[l], DFF + 256 * j, 256, KC, 0, 256))
        phase_dense(HT, KF, Wb_dn[l], D // 128, 128, h_f32out(YT, 128),
                    lambda j, l=l: pieces_of(w_down[l], 128 * j, 128, KF), nbuf_in=1)
        last = (l == NL - 1)
        xdst = outT if last else XB
        phase_norm(XA, YT, g_in["g_post_ffn"][l], xdst, None if last else g_in["g_pre_mix"][l + 1], None if last else XN)
        xcur = XB
    sc.emit(final=True)
    return P


def make_inputs(cfg, inputs, b):
    bucket, maskc = host_tables()
    rel_bias = np.asarray(inputs["rel_bias"], np.float32)
    biasg = np.empty((24, 128, 256), np.float32)
    for h in range(24):
        kind = (h // 4) if h < 12 else 3
        biasg[h] = rel_bias[bucket[kind], h]
    tri, upp, m64 = gla_masks()
    m = {
        "xT": np.ascontiguousarray(np.asarray(inputs["x"][b], np.float32).T),
        "biasg": biasg, "maskc": maskc, "gla_tri": tri, "gla_upp": upp, "gla_m64": m64,
    }
    for k in ("w_in", "w_gla_gate", "b_gla_gate", "gla_norm", "attn_sinks", "w_br_a", "w_br_b", "w_br_c", "w_out",
              "g_pre_mix", "g_post_mix", "g_pre_ffn", "g_post_ffn", "w_up", "conv_w", "conv_b", "w_down"):
        m[k] = np.ascontiguousarray(np.asarray(inputs[k], np.float32))
    return m


_PROG_CACHE = {}


def run(cfg, inputs):
    key = (cfg.D, cfg.S, cfg.DFF, cfg.NL, cfg.NB, cfg.debug)
    if key not in _PROG_CACHE:
        _PROG_CACHE[key] = build(cfg)
    P = _PROG_CACHE[key]
    in_maps = [make_inputs(cfg, inputs, b) for b in range(cfg.NB)]
    res = run_bass_kernel_spmd(P.nc, in_maps, core_ids=list(range(cfg.NB)))
    return P, res


def kernel(**inputs):
    x = np.asarray(inputs["x"])
    B, S, D = x.shape
    cfg = Cfg(D=D, S=S, DFF=np.asarray(inputs["w_down"]).shape[1], NL=np.asarray(inputs["w_in"]).shape[0], NB=B)
    P, res = run(cfg, inputs)
    out = np.stack([np.ascontiguousarray(res.results[b]["outT"].T) for b in range(B)], axis=0)
    return out.astype(np.float32)
```

```python
import numpy as np
import ml_dtypes
from contextlib import ExitStack
import concourse.bass as bass
import concourse.mybir as mybir
from concourse.bass_utils import run_bass_kernel_spmd

F32 = mybir.dt.float32
BF16 = mybir.dt.bfloat16
ALU = mybir.AluOpType
AF = mybir.ActivationFunctionType

HD = 128
A_HEADS = 12
A_SLOTS = 4
DIL = ((128, 1), (512, 4), (2048, 16))
B_HEADS = 8
B_DK = 64
B_DV = 128
B_RANK = 16
B_TAU = 16.0
CH = 64
C_QH = 12
C_KVH = 3
REL_BUCKETS = 32
REL_MAX_DIST = 2048
EPS = 1e-6
NEG = -30000.0


class Cfg:
    def __init__(self, D=4096, S=8192, DFF=11008, NL=2, NB=2, debug=False):
        self.D, self.S, self.DFF, self.NL, self.NB, self.debug = D, S, DFF, NL, NB, debug
        self.KC = D // 128
        self.widths = (1536, 1536, 1536, 512, 512, 1024, 1024, 16, 1536, 384, 384, D, D, D)
        self.offs = np.concatenate([[0], np.cumsum(self.widths)]).tolist()
        self.NIN = self.offs[-1]
        assert D % 512 == 0 and S % 2048 == 0 and DFF % 256 == 0


class Sched:
    ENG = ("pe", "dve", "act", "pool", "sp")
    NDMA = 12

    def __init__(self, nc, ctx):
        self.nc = nc
        self.sem = {e: ctx.enter_context(nc.semaphore("c_" + e)) for e in self.ENG}
        self.count = {e: 0 for e in self.ENG}
        self.dsem = {q: [ctx.enter_context(nc.semaphore("d_%s%d" % (q, i))) for i in range(self.NDMA)]
                     for q in ("sp", "pool", "act")}
        self.dcnt = {q: 0 for q in self.dsem}
        self.waited = {e: {} for e in self.ENG}
        self.lastw = {}
        self.readers = {}
        self.streams = {e: [] for e in self.ENG}
        self.bar = []
        self.ninst = 0

    def _deps(self, eng, reads, writes):
        deps = list(self.bar)
        for k in reads:
            t = self.lastw.get(k)
            if t is not None:
                deps.append(t)
        for k in writes:
            t = self.lastw.get(k)
            if t is not None:
                deps.append(t)
            deps.extend(self.readers.get(k, ()))
        return deps

    def _reduce(self, eng, deps):
        best = {}
        w = self.waited[eng]
        for (s, v, src) in deps:
            if src == "pe" and eng == "pe":
                continue
            if w.get(id(s), 0) >= v:
                continue
            if best.get(id(s), (None, 0))[1] < v:
                best[id(s)] = (s, v)
        out = []
        for sid, (s, v) in best.items():
            w[sid] = v
            out.append((s, v))
        return out

    def _record(self, tok, reads, writes):
        for k in writes:
            self.lastw[k] = tok
            self.readers[k] = []
        for k in reads:
            if k not in writes:
                self.readers.setdefault(k, []).append(tok)

    def op(self, eng, fn, reads=(), writes=()):
        waits = self._reduce(eng, self._deps(eng, reads, writes))
        self.count[eng] += 1
        tok = (self.sem[eng], self.count[eng], eng)
        self.streams[eng].append((waits, fn, self.sem[eng], 1))
        self._record(tok, reads, writes)
        return tok

    def dma(self, q, out, in_, reads=(), writes=(), slow=False):
        i = self.dcnt[q]
        self.dcnt[q] += 1
        s = self.dsem[q][i % self.NDMA]
        rnd = i // self.NDMA
        deps = self._deps(q, reads, writes)
        if rnd > 0:
            deps.append((s, 16 * rnd, "dma"))
        waits = self._reduce(q, deps)
        tok = (s, 16 * (rnd + 1), "dma")
        if slow:
            self.streams[q].append((waits, lambda e, out=out, in_=in_: e.dma_start(out=out, in_=in_, allow_slow_non_contiguous=True), s, 16))
        else:
            self.streams[q].append((waits, lambda e, out=out, in_=in_: e.dma_start(out=out, in_=in_), s, 16))
        self._record(tok, reads, writes)
        return tok

    def barrier(self):
        toks = [(self.sem[e], self.count[e], e) for e in self.ENG if self.count[e] > 0]
        for q in self.dsem:
            n = self.dcnt[q]
            for j in range(self.NDMA):
                cnt = (n - j + self.NDMA - 1) // self.NDMA if n > j else 0
                if cnt > 0:
                    toks.append((self.dsem[q][j], 16 * cnt, "dma"))
        self.bar = toks
        self.lastw = {}
        self.readers = {}

    def emit(self, final=False):
        if final:
            self.barrier()
            for e in self.ENG:
                waits = self._reduce(e, list(self.bar))
                if waits:
                    self.streams[e].append((waits, None, None, 0))
        nc = self.nc
        streams = self.streams
        total = sum(len(v) for v in streams.values())
        self.ninst += total

        def replay(name, e):
            for (waits, fn, s, inc) in streams[name]:
                for (ws, wv) in waits:
                    e.wait_ge(ws, wv)
                if fn is not None:
                    ins = fn(e)
                    ins.then_inc(s, inc)

        with nc.Block() as block:
            if streams["pe"]:
                @block.tensor
                def _(e):
                    replay("pe", e)
            if streams["dve"]:
                @block.vector
                def _(e):
                    replay("dve", e)
            if streams["act"]:
                @block.scalar
                def _(e):
                    replay("act", e)
            if streams["pool"]:
                @block.gpsimd
                def _(e):
                    replay("pool", e)
            if streams["sp"]:
                @block.sync
                def _(e):
                    replay("sp", e)
        self.streams = {e: [] for e in self.ENG}


def mm_group(sc, out_ap, pairs, reads, writes):
    def fn(e, pairs=pairs, out_ap=out_ap):
        n = len(pairs)
        ins = None
        for i, (l, r) in enumerate(pairs):
            ins = e.matmul(out_ap, l, r, start=(i == 0), stop=(i == n - 1))
        return ins
    return sc.op("pe", fn, reads, writes)


def _t5_bucket(dist):
    dist = np.maximum(dist, 0)
    max_exact = REL_BUCKETS // 2
    far = max_exact + (np.log(np.maximum(dist, 1).astype(np.float32) / max_exact)
                       / np.float32(np.log(REL_MAX_DIST / max_exact))
                       * (REL_BUCKETS - max_exact)).astype(np.int32)
    return np.where(dist < max_exact, dist, np.minimum(far, REL_BUCKETS - 1))


def _rel_kq():
    k = np.arange(128)[:, None]
    q = np.arange(128)[None, :]
    rel_prev = q + 128 - k
    rel_cur = q - k
    return np.concatenate([rel_prev, rel_cur], axis=1)


def host_tables():
    rel = _rel_kq()
    kinds = [(128, d) for (_, d) in DIL] + [(127, 1)]
    bucket = []
    maskc = []
    for (span, d) in kinds:
        bucket.append(_t5_bucket(rel * d))
        ok = (rel >= 0) & (rel <= span)
        maskc.append(np.where(ok, 0.0, NEG).astype(np.float32))
    return bucket, np.stack(maskc)


def gla_masks():
    j = np.arange(128)[:, None]
    i = np.arange(128)[None, :]
    same = (j // CH) == (i // CH)
    tri = (same & (j <= i)).astype(np.float32)
    upp = (same & (j > i)).astype(np.float32)
    jj = np.arange(CH)[:, None]
    ii = np.arange(CH)[None, :]
    m64 = (jj <= ii).astype(np.float32)
    return tri, upp, np.tile(m64, (1, B_HEADS))


class Prog:
    def __init__(self, cfg):
        self.cfg = cfg
        self.nc = bass.Bass(target_bir_lowering=False)
        self.ctx = ExitStack()
        self.sc = Sched(self.nc, self.ctx)
        self.dbg = {}

    def din(self, name, shape, dt=F32):
        return self.nc.dram_tensor(name, list(shape), dt, kind="ExternalInput").ap()

    def dout(self, name, shape, dt=F32):
        return self.nc.dram_tensor(name, list(shape), dt, kind="ExternalOutput").ap()

    def dscr(self, name, shape, dt=BF16, dbg=False):
        if dbg and self.cfg.debug:
            t = self.nc.dram_tensor(name, list(shape), dt, kind="ExternalOutput").ap()
            self.dbg[name] = t
            return t
        return self.nc.dram_tensor(name, list(shape), dt).ap()


def build(cfg):
    P = Prog(cfg)
    nc, sc = P.nc, P.sc
    D, S, DFF, NL, KC = cfg.D, cfg.S, cfg.DFF, cfg.NL, cfg.KC
    KF = DFF // 128
    NT = S // 512
    offs = cfg.offs

    xT = P.din("xT", [D, S])
    outT = P.dout("outT", [D, S])
    w_in = P.din("w_in", [NL, D, cfg.NIN])
    w_gate = P.din("w_gla_gate", [NL, B_RANK, 512])
    b_gate = P.din("b_gla_gate", [NL, 512])
    gla_norm = P.din("gla_norm", [NL, 128])
    sinks = P.din("attn_sinks", [NL, C_QH])
    w_bra = P.din("w_br_a", [NL, 512, D])
    w_brb = P.din("w_br_b", [NL, 1024, D])
    w_brc = P.din("w_br_c", [NL, 1536, D])
    w_out = P.din("w_out", [NL, D, D])
    g_in = {n: P.din(n, [NL, D]) for n in ("g_pre_mix", "g_post_mix", "g_pre_ffn", "g_post_ffn")}
    w_up = P.din("w_up", [NL, D, 2 * DFF])
    conv_w = P.din("conv_w", [NL, 3, 2 * DFF])
    conv_b = P.din("conv_b", [NL, 2 * DFF])
    w_down = P.din("w_down", [NL, DFF, D])
    biasg = P.din("biasg", [24, 128, 256])
    maskc = P.din("maskc", [4, 128, 256])
    tri_in = P.din("gla_tri", [128, 128])
    upp_in = P.din("gla_upp", [128, 128])
    m64_in = P.din("gla_m64", [CH, CH * B_HEADS])

    o = offs
    in_tiles = []
    for i in range(3):
        in_tiles.append(("F", o[0] + 512 * i, 512, "aqT", 512 * i, None))
    for i in range(3):
        in_tiles.append(("F", o[1] + 512 * i, 512, "akT", 512 * i, None))
    for i in range(3):
        in_tiles.append(("T", o[2] + 512 * i, 512, "av", 512 * i, None))
    in_tiles.append(("F", o[3], 512, "bqT", 0, None))
    in_tiles.append(("F", o[4], 512, "bkT", 0, None))
    in_tiles.append(("T", o[4], 512, "bk", 0, None))
    for i in range(2):
        in_tiles.append(("T", o[5] + 512 * i, 512, "bv", 512 * i, None))
    for i in range(2):
        in_tiles.append(("F", o[6] + 512 * i, 512, "brT", 512 * i, AF.Silu))
    in_tiles.append(("F", o[7], 16, "bgT", 0, None))
    for i in range(3):
        in_tiles.append(("F", o[8] + 512 * i, 512, "cqT", 512 * i, None))
    in_tiles.append(("F", o[9], 384, "ckT", 0, None))
    in_tiles.append(("T", o[10], 384, "cv", 0, None))
    for gi, gname in enumerate(("gaT", "gbT", "gcT")):
        for i in range(D // 512):
            in_tiles.append(("F", o[11 + gi] + 512 * i, 512, gname, 512 * i, AF.Sigmoid))
    NIT = len(in_tiles)

    NUT = DFF // 256
    Wb_in = [P.dscr("Wb_in%d" % l, [NIT, 128, KC * 512]) for l in range(NL)]
    Wb_br = [P.dscr("Wb_br%d" % l, [D // 512, 128, 24 * 512]) for l in range(NL)]
    Wb_out = [P.dscr("Wb_out%d" % l, [D // 512, 128, KC * 512]) for l in range(NL)]
    Wb_up = [P.dscr("Wb_up%d" % l, [NUT, 128, KC * 512]) for l in range(NL)]
    Wb_dn = [P.dscr("Wb_dn%d" % l, [D // 128, 128, KF * 128]) for l in range(NL)]

    XN = P.dscr("XN", [D, S], BF16, dbg=True)
    XA = P.dscr("XA", [D, S], F32, dbg=True)
    XB = P.dscr("XB", [D, S], F32)
    YT = P.dscr("YT", [D, S], F32, dbg=True)
    MT = P.dscr("MT", [D, S], BF16, dbg=True)
    HT = P.dscr("HT", [DFF, S], BF16, dbg=True)
    pr = {
        "aqT": P.dscr("aqT", [1536, S], BF16, dbg=True), "akT": P.dscr("akT", [1536, S], BF16),
        "av": P.dscr("av", [S, 1536], BF16, dbg=True),
        "bqT": P.dscr("bqT", [512, S]), "bkT": P.dscr("bkT", [512, S]), "bk": P.dscr("bk", [S, 512]),
        "bv": P.dscr("bv", [S, 1024]), "brT": P.dscr("brT", [1024, S], BF16, dbg=True),
        "bgT": P.dscr("bgT", [16, S]),
        "cqT": P.dscr("cqT", [1536, S]), "ckT": P.dscr("ckT", [384, S]), "cv": P.dscr("cv", [S, 384]),
        "gaT": P.dscr("gaT", [D, S], BF16, dbg=True), "gbT": P.dscr("gbT", [D, S]), "gcT": P.dscr("gcT", [D, S]),
    }
    OA = P.dscr("OA", [512, S], BF16, dbg=True)
    OB = P.dscr("OB", [1024, S], BF16, dbg=True)
    OC = P.dscr("OC", [1536, S], BF16, dbg=True)

    ctx = P.ctx

    uid = [0]

    def sb(name, shape, dt, c=None):
        uid[0] += 1
        return (c or ctx).enter_context(nc.sbuf_tensor("%s_%d" % (name, uid[0]), list(shape), dt))

    def psum_banks(c, n=8):
        uid[0] += 1
        return [c.enter_context(nc.psum_tensor("ps%d_%d" % (i, uid[0]), [128, 512], F32)) for i in range(n)]

    ones_bf = sb("ones_bf", [128, 128], BF16)
    ones_f = sb("ones_f", [128, 128], F32)
    sc.op("pool", lambda e: e.memset(ones_bf[:], 1.0), writes=["ones_bf"])
    sc.op("pool", lambda e: e.memset(ones_f[:], 1.0), writes=["ones_f"])

    def phase_norm(x_src, y_src, gpost, x_dst, gnext, xn_dst):
        TN = 128
        with ExitStack() as c:
            ps = psum_banks(c, 4)
            gp = sb("gp", [128, KC], F32, c)
            gn = sb("gn", [128, KC], F32, c)
            if gpost is not None:
                sc.dma("sp", gp[:], gpost.rearrange("(kc p) -> p kc", p=128), writes=["gp"], slow=True)
            if gnext is not None:
                sc.dma("sp", gn[:], gnext.rearrange("(kc p) -> p kc", p=128), writes=["gn"], slow=True)
            NS = 3
            xt = [sb("nx%d" % i, [128, KC, TN], F32, c) for i in range(NS)]
            yt = [sb("ny%d" % i, [128, KC, TN], F32, c) for i in range(NS)]
            xo = [sb("no%d" % i, [128, KC, TN], BF16, c) for i in range(NS)]
            sq = [sb("nsq%d" % i, [128, KC, TN], F32, c) for i in range(2)]
            red = [sb("nrd%d" % i, [128, TN], F32, c) for i in range(4)]
            rs = [sb("nrs%d" % i, [128, TN], F32, c) for i in range(4)]
            cnt = {"s": 0, "m": 0}

            def stats(src_tile, src_key):
                i = cnt["s"]
                cnt["s"] += 1
                q, r = i % 2, i % 4
                sc.op("act", lambda e, o_=sq[q][:], i_=src_tile[:]: e.activation(out=o_, in_=i_, func=AF.Square),
                      reads=[src_key], writes=[("nsq", q)])
                sc.op("dve", lambda e, o_=red[r][:], i_=sq[q][:].rearrange("p k t -> p t k"):
                      e.reduce_sum(out=o_, in_=i_, axis=mybir.AxisListType.X),
                      reads=[("nsq", q)], writes=[("nrd", r)])
                sc.op("pe", lambda e, o_=ps[r][:, 0:TN], r_=red[r][:]: e.matmul(o_, ones_f[:], r_, start=True, stop=True),
                      reads=[("nrd", r), "ones_f"], writes=[("nps", r)])
                sc.op("act", lambda e, o_=rs[r][:], i_=ps[r][:, 0:TN]:
                      e.activation(out=o_, in_=i_, func=AF.Ln, scale=1.0 / D, bias=EPS),
                      reads=[("nps", r)], writes=[("nrs", r)])
                sc.op("act", lambda e, o_=rs[r][:]: e.activation(out=o_, in_=o_, func=AF.Exp, scale=-0.5),
                      reads=[("nrs", r)], writes=[("nrs", r)])
                return r

            def scale_rows(dst_tile, dst_key, src_tile, src_key, gt, gkey, r):
                for kc in range(KC):
                    eng = "dve"
                    cnt["m"] += 1
                    wk = (dst_key, eng, kc % 2)
                    sc.op(eng, lambda e, o_=dst_tile[:, kc, :], i_=src_tile[:, kc, :], g_=gt[:, kc:kc + 1], r_=rs[r][:]:
                          e.scalar_tensor_tensor(out=o_, in0=i_, scalar=g_, in1=r_, op0=ALU.mult, op1=ALU.mult),
                          reads=[gkey, ("nrs", r)] + ([src_key] if src_key != dst_key else []), writes=[wk])
                return [(dst_key, en, u) for en in ("dve", "pool") for u in (0, 1)]

            for t in range(S // TN):
                b = t % NS
                ts = slice(t * TN, (t + 1) * TN)
                sc.dma("sp", xt[b][:], x_src.rearrange("(kc p) s -> p kc s", p=128)[:, :, ts], writes=[("nx", b), ("nxa", b)])
                if y_src is not None:
                    yk = ("ny", b)
                    ysub = [(yk, en, u) for en in ("dve", "pool") for u in (0, 1)]
                    sc.dma("sp", yt[b][:], y_src.rearrange("(kc p) s -> p kc s", p=128)[:, :, ts], writes=[yk] + ysub)
                    r = stats(yt[b], yk)
                    for kc in range(KC):
                        eng = "dve"
                        cnt["m"] += 1
                        sc.op(eng, lambda e, o_=yt[b][:, kc, :], g_=gp[:, kc:kc + 1], r_=rs[r][:]:
                              e.scalar_tensor_tensor(out=o_, in0=o_, scalar=g_, in1=r_, op0=ALU.mult, op1=ALU.mult),
                              reads=["gp", ("nrs", r), yk], writes=[(yk, eng, kc % 2)])
                    sc.op("dve", lambda e, o_=xt[b][:], y_=yt[b][:]: e.tensor_tensor(out=o_, in0=o_, in1=y_, op=ALU.add),
                          reads=ysub + [("nx", b)], writes=[("nx", b), ("nxa", b), yk])
                    sc.dma("pool", x_dst.rearrange("(kc p) s -> p kc s", p=128)[:, :, ts], xt[b][:], reads=[("nx", b)])
                if gnext is not None:
                    r = stats(xt[b], ("nx", b))
                    ok = ("no", b)
                    osub = [(ok, en, u) for en in ("dve", "pool") for u in (0, 1)]
                    for kc in range(KC):
                        eng = "dve"
                        cnt["m"] += 1
                        sc.op(eng, lambda e, o_=xo[b][:, kc, :], i_=xt[b][:, kc, :], g_=gn[:, kc:kc + 1], r_=rs[r][:]:
                              e.scalar_tensor_tensor(out=o_, in0=i_, scalar=g_, in1=r_, op0=ALU.mult, op1=ALU.mult),
                              reads=["gn", ("nrs", r), ("nxa", b)], writes=[(ok, eng, kc % 2)])
                    sc.dma("pool", xn_dst.rearrange("(kc p) s -> p kc s", p=128)[:, :, ts], xo[b][:], reads=osub)
            sc.emit()
        sc.barrier()

    def make_wfetch(c, cw_buf, kcn, nstg=4):
        stg = [sb("wst%d" % i, [128, 2048], F32, c) for i in range(nstg)]
        st = {"n": 0}
        engs = ("dve", "pool", "act", "dve")

        def fetch(first, wbuf, wkey, wtile_dram, pieces, dkey):
            wkeys = [(wkey, en) for en in ("pool", "dve", "act")]
            if not first:
                sc.dma("sp", wbuf[:], wtile_dram, reads=[dkey], writes=wkeys)
                return
            wv = wbuf[:].rearrange("p (k c) -> p k c", c=cw_buf)
            for (src, k0, kn, c0, cwp) in pieces:
                i = st["n"] % nstg
                st["n"] += 1
                sview = stg[i][:, 0:kn * cwp].rearrange("p (k c) -> p k c", c=cwp)
                sc.dma("sp", sview, src, writes=[("wst", i)])
                eng = engs[st["n"] % 4]
                dstv = wv[:, k0:k0 + kn, c0:c0 + cwp]
                if eng == "act":
                    sc.op("act", lambda e, o_=dstv, i_=sview: e.copy(out=o_, in_=i_), reads=[("wst", i)], writes=[(wkey, eng)])
                else:
                    sc.op(eng, lambda e, o_=dstv, i_=sview: e.tensor_copy(out=o_, in_=i_), reads=[("wst", i)], writes=[(wkey, eng)])
            sc.dma("pool", wtile_dram, wbuf[:], reads=wkeys, writes=[dkey])
        return fetch

    def pieces_of(src2d, col0, ncol, kcn, k_dst0=0, c_dst0=0):
        v = src2d.rearrange("(kc p) n -> p kc n", p=128)
        g = max(1, 2048 // ncol)
        out = []
        k0 = 0
        while k0 < kcn:
            kn = min(g, kcn - k0)
            out.append((v[:, k0:k0 + kn, col0:col0 + ncol], k_dst0 + k0, kn, c_dst0, ncol))
            k0 += kn
        return out

    def phase_dense(inp, kcn, wtiles, ntiles, cw, handler, wsrc, nbuf_in=2):
        with ExitStack() as c:
            ps = psum_banks(c, 8)
            xin = [sb("din%d" % i, [128, kcn, 512], BF16, c) for i in range(nbuf_in)]
            wb = [sb("dw%d" % i, [128, kcn * cw], BF16, c) for i in range(2)]
            fetch = make_wfetch(c, cw, kcn)
            f = handler(c, ps)
            nw = 0
            for tt in range(NT):
                ib = tt % nbuf_in
                sc.dma("sp", xin[ib][:], inp.rearrange("(kc p) s -> p kc s", p=128)[:, :, tt * 512:(tt + 1) * 512],
                       writes=[("din", ib)])
                for j in range(ntiles):
                    b = nw % 2
                    nw += 1
                    fetch(tt == 0, wb[b], ("dw", b), wtiles[j], wsrc(j) if tt == 0 else None, ("wdram", j))
                    f(tt, j, wb[b][:].rearrange("p (k c) -> p k c", c=cw), xin[ib], ("din", ib),
                      [(("dw", b), en) for en in ("pool", "dve", "act")])
            sc.emit()
        sc.barrier()

    def h_inproj(l):
        def mk(c, ps):
            stg = [sb("ist%d" % i, [128, 4, 512], BF16, c) for i in range(3)]
            state = {"n": 0, "pb": 0, "e": 0}

            def f(tt, j, w, xin, kin, kw):
                kind, c0, ncol, dname, doff, act = in_tiles[j]
                q = state["n"] % 3
                state["n"] += 1
                dst = pr[dname]
                if kind == "F":
                    nb = (ncol + 127) // 128
                    for cb in range(nb):
                        m = min(128, ncol - cb * 128)
                        pb = state["pb"] % 8
                        state["pb"] += 1
                        mm_group(sc, ps[pb][0:m, :], [(w[:, kc, cb * 128:cb * 128 + m], xin[:, kc, :]) for kc in range(KC)],
                                 reads=[kin] + kw, writes=[("ps", pb)])
                        if act is not None:
                            sc.op("act", lambda e, o_=stg[q][0:m, cb, :], i_=ps[pb][0:m, :], a_=act: e.activation(out=o_, in_=i_, func=a_),
                                  reads=[("ps", pb)], writes=[("ist", q)])
                        elif state["e"] % 2 == 0:
                            sc.op("act", lambda e, o_=stg[q][0:m, cb, :], i_=ps[pb][0:m, :]: e.copy(out=o_, in_=i_),
                                  reads=[("ps", pb)], writes=[("ist", q)])
                        else:
                            sc.op("dve", lambda e, o_=stg[q][0:m, cb, :], i_=ps[pb][0:m, :]: e.tensor_copy(out=o_, in_=i_),
                                  reads=[("ps", pb)], writes=[("ist", q)])
                        state["e"] += 1
                    if ncol % 128 == 0:
                        sc.dma("pool", dst[doff:doff + ncol, tt * 512:(tt + 1) * 512].rearrange("(cb p) s -> p cb s", p=128),
                               stg[q][:, 0:nb, :], reads=[("ist", q)])
                    else:
                        sc.dma("pool", dst[doff:doff + ncol, tt * 512:(tt + 1) * 512], stg[q][0:ncol, 0, :], reads=[("ist", q)])
                else:
                    for tb in range(4):
                        pb = state["pb"] % 8
                        state["pb"] += 1
                        mm_group(sc, ps[pb][:, 0:ncol], [(xin[:, kc, tb * 128:(tb + 1) * 128], w[:, kc, 0:ncol]) for kc in range(KC)],
                                 reads=[kin] + kw, writes=[("ps", pb)])
                        if state["e"] % 2 == 0:
                            sc.op("act", lambda e, o_=stg[q][:, tb, 0:ncol], i_=ps[pb][:, 0:ncol]: e.copy(out=o_, in_=i_),
                                  reads=[("ps", pb)], writes=[("ist", q)])
                        else:
                            sc.op("dve", lambda e, o_=stg[q][:, tb, 0:ncol], i_=ps[pb][:, 0:ncol]: e.tensor_copy(out=o_, in_=i_),
                                  reads=[("ps", pb)], writes=[("ist", q)])
                        state["e"] += 1
                    sc.dma("pool", dst[tt * 512:(tt + 1) * 512, doff:doff + ncol].rearrange("(tb p) n -> p tb n", p=128),
                           stg[q][:, :, 0:ncol], reads=[("ist", q)])
            return f
        return mk

    def h_f32out(dst, cw):
        def mk(c, ps):
            nb = cw // 128
            stg = [sb("fst%d" % i, [128, nb, 512], F32, c) for i in range(3)]
            state = {"n": 0, "pb": 0}

            def f(tt, j, w, xin, kin, kw):
                kcn = w.shape[1]
                q = state["n"] % 3
                state["n"] += 1
                for cb in range(nb):
                    pb = state["pb"] % 8
                    state["pb"] += 1
                    mm_group(sc, ps[pb][:, :], [(w[:, kc, cb * 128:(cb + 1) * 128], xin[:, kc, :]) for kc in range(kcn)],
                             reads=[kin] + kw, writes=[("ps", pb)])
                    if state["pb"] % 2 == 0:
                        sc.op("act", lambda e, o_=stg[q][:, cb, :], i_=ps[pb][:, :]: e.copy(out=o_, in_=i_),
                              reads=[("ps", pb)], writes=[("fst", q)])
                    else:
                        sc.op("dve", lambda e, o_=stg[q][:, cb, :], i_=ps[pb][:, :]: e.tensor_copy(out=o_, in_=i_),
                              reads=[("ps", pb)], writes=[("fst", q)])
                sc.dma("pool", dst[j * cw:(j + 1) * cw, tt * 512:(tt + 1) * 512].rearrange("(cb p) s -> p cb s", p=128),
                       stg[q][:], reads=[("fst", q)])
            return f
        return mk

    def h_ffn_up(l):
        def mk(c, ps):
            cwt = sb("cwt", [128, 3, 2 * KF], F32, c)
            cbt = sb("cbt", [128, 2 * KF], F32, c)
            sc.dma("sp", cwt[:], conv_w[l].rearrange("t (b p) -> p t b", p=128), writes=["cwt"], slow=True)
            sc.dma("sp", cbt[:], conv_b[l].rearrange("(b p) -> p b", p=128), writes=["cbt"], slow=True)
            carry = sb("carry", [128, 2 * KF, 2], F32, c)
            sc.op("pool", lambda e: e.memset(carry[:], 0.0), writes=["carry"])
            yb = [sb("yb%d" % i, [128, 514], F32, c) for i in range(4)]
            ub = [sb("ub%d" % i, [128, 512], F32, c) for i in range(4)]
            hst = [sb("hst%d" % i, [128, 2, 512], BF16, c) for i in range(3)]
            state = {"n": 0, "pb": 0, "y": 0}

            def f(tt, j, w, xin, kin, kw):
                q = state["n"] % 3
                state["n"] += 1
                for half in range(2):
                    res = []
                    for which in range(2):
                        blk = (KF * which) + 2 * j + half
                        wc = which * 256 + half * 128
                        pb = state["pb"] % 8
                        state["pb"] += 1
                        mm_group(sc, ps[pb][:, :], [(w[:, kc, wc:wc + 128], xin[:, kc, :]) for kc in range(KC)],
                                 reads=[kin] + kw, writes=[("ps", pb)])
                        yi = state["y"] % 4
                        state["y"] += 1
                        ck = ("carry", blk)
                        sc.op("pool", lambda e, o_=yb[yi][:, 0:2], i_=carry[:, blk, :]: e.tensor_copy(out=o_, in_=i_),
                              reads=[ck, "carry"], writes=[("yb", yi)])
                        sc.op("act", lambda e, o_=yb[yi][:, 2:514], i_=ps[pb][:, :]: e.copy(out=o_, in_=i_),
                              reads=[("ps", pb)], writes=[("yb", yi)])
                        sc.op("pool", lambda e, o_=carry[:, blk, :], i_=yb[yi][:, 512:514]: e.tensor_copy(out=o_, in_=i_),
                              reads=[("yb", yi)], writes=[ck])
                        sc.op("act", lambda e, o_=ub[yi][:], i_=yb[yi][:, 2:514], s_=cwt[:, 2, blk:blk + 1], b_=cbt[:, blk:blk + 1]:
                              e.activation(out=o_, in_=i_, func=AF.Identity, scale=s_, bias=b_),
                              reads=[("yb", yi), "cwt", "cbt"], writes=[("ub", yi)])
                        sc.op("dve", lambda e, o_=ub[yi][:], i_=yb[yi][:, 1:513], s_=cwt[:, 1, blk:blk + 1]:
                              e.scalar_tensor_tensor(out=o_, in0=i_, scalar=s_, in1=o_, op0=ALU.mult, op1=ALU.add),
                              reads=[("yb", yi), "cwt", ("ub", yi)], writes=[("ub", yi)])
                        sc.op("dve", lambda e, o_=ub[yi][:], i_=yb[yi][:, 0:512], s_=cwt[:, 0, blk:blk + 1]:
                              e.scalar_tensor_tensor(out=o_, in0=i_, scalar=s_, in1=o_, op0=ALU.mult, op1=ALU.add),
                              reads=[("yb", yi), "cwt", ("ub", yi)], writes=[("ub", yi)])
                        res.append(yi)
                    gi, ui = res
                    sc.op("act", lambda e, o_=yb[gi][:, 0:512], i_=ub[gi][:]: e.activation(out=o_, in_=i_, func=AF.Silu),
                          reads=[("ub", gi)], writes=[("yb", gi)])
                    sc.op("dve", lambda e, o_=hst[q][:, half, :], a_=yb[gi][:, 0:512], b_=ub[ui][:]:
                          e.tensor_tensor(out=o_, in0=a_, in1=b_, op=ALU.mult),
                          reads=[("yb", gi), ("ub", ui)], writes=[("hst", q)])
                sc.dma("pool", HT[256 * j:256 * j + 256, tt * 512:(tt + 1) * 512].rearrange("(cb p) s -> p cb s", p=128),
                       hst[q][:], reads=[("hst", q)])
            return f
        return mk

    def phase_merge(l):
        with ExitStack() as c:
            ps = psum_banks(c, 8)
            xin = [sb("min%d" % i, [128, 24, 512], BF16, c) for i in range(2)]
            wb = [sb("mw%d" % i, [128, 24 * 512], BF16, c) for i in range(2)]
            gt = [sb("mg%d" % i, [128, 3, 4, 512], BF16, c) for i in range(2)]
            t1 = [sb("mt1_%d" % i, [128, 512], F32, c) for i in range(2)]
            t2 = [sb("mt2_%d" % i, [128, 512], F32, c) for i in range(2)]
            t3 = [sb("mt3_%d" % i, [128, 512], F32, c) for i in range(2)]
            stg = [sb("mst%d" % i, [128, 4, 512], BF16, c) for i in range(3)]
            fetch = make_wfetch(c, 512, 24, nstg=2)
            nw = 0
            npb = 0
            nq = 0
            gsrc = (pr["gaT"], pr["gbT"], pr["gcT"])
            for tt in range(NT):
                ib = tt % 2
                tsl = slice(tt * 512, (tt + 1) * 512)
                sc.dma("sp", xin[ib][:, 0:4, :], OA.rearrange("(kc p) s -> p kc s", p=128)[:, :, tsl], writes=[("min", ib)])
                sc.dma("sp", xin[ib][:, 4:12, :], OB.rearrange("(kc p) s -> p kc s", p=128)[:, :, tsl], writes=[("min", ib)])
                sc.dma("sp", xin[ib][:, 12:24, :], OC.rearrange("(kc p) s -> p kc s", p=128)[:, :, tsl], writes=[("min", ib)])
                for j in range(D // 512):
                    b = nw % 2
                    nw += 1
                    pcs = None
                    if tt == 0:
                        pcs = (pieces_of(w_bra[l], 512 * j, 512, 4, 0) + pieces_of(w_brb[l], 512 * j, 512, 8, 4)
                               + pieces_of(w_brc[l], 512 * j, 512, 12, 12))
                    fetch(tt == 0, wb[b], ("mw", b), Wb_br[l][j], pcs, ("wdram", j))
                    mwk = [(("mw", b), en) for en in ("pool", "dve", "act")]
                    for gi in range(3):
                        sc.dma("sp", gt[b][:, gi, :, :],
                               gsrc[gi][512 * j:512 * j + 512, tsl].rearrange("(cb p) s -> p cb s", p=128), writes=[("mg", b)])
                    w = wb[b][:].rearrange("p (k c) -> p k c", c=512)
                    q = nq % 3
                    nq += 1
                    for cb in range(4):
                        pbs = []
                        for (k0, k1) in ((0, 4), (4, 12), (12, 24)):
                            pb = npb % 8
                            npb += 1
                            mm_group(sc, ps[pb][:, :], [(w[:, kc, cb * 128:(cb + 1) * 128], xin[ib][:, kc, :]) for kc in range(k0, k1)],
                                     reads=[("min", ib)] + mwk, writes=[("ps", pb)])
                            pbs.append(pb)
                        u = cb % 2
                        for tbuf, nm, pb, gi in ((t1, "mt1", pbs[0], 0), (t2, "mt2", pbs[1], 1), (t3, "mt3", pbs[2], 2)):
                            sc.op("dve", lambda e, o_=tbuf[u][:], p_=ps[pb][:, :], g_=gt[b][:, gi, cb, :]:
                                  e.tensor_tensor(out=o_, in0=p_, in1=g_, op=ALU.mult),
                                  reads=[("ps", pb), ("mg", b)], writes=[(nm, u)])
                        sc.op("pool", lambda e, o_=t1[u][:], i_=t2[u][:]: e.tensor_tensor(out=o_, in0=o_, in1=i_, op=ALU.add),
                              reads=[("mt1", u), ("mt2", u)], writes=[("mt1", u)])
                        sc.op("pool", lambda e, o_=stg[q][:, cb, :], a_=t1[u][:], i_=t3[u][:]: e.tensor_tensor(out=o_, in0=a_, in1=i_, op=ALU.add),
                              reads=[("mt1", u), ("mt3", u)], writes=[("mst", q)])
                    sc.dma("pool", MT[512 * j:512 * j + 512, tsl].rearrange("(cb p) s -> p cb s", p=128), stg[q][:], reads=[("mst", q)])
            sc.emit()
        sc.barrier()

    def phase_attn_A(l):
        NSUP = S // 2048
        scale = float(HD) ** -0.5
        with ExitStack() as c:
            ps = psum_banks(c, 8)
            bm = sb("bmA", [128, 12, 256], F32, c)
            mk_ = sb("mkA", [128, 3, 256], F32, c)
            sc.dma("sp", bm[:], biasg[0:12].rearrange("h k q -> k h q"), writes=["bm"])
            sc.dma("sp", mk_[:], maskc[0:3].rearrange("g k q -> k g q"), writes=["mk"])
            for h in range(12):
                sc.op("dve", lambda e, o_=bm[:, h, :], m_=mk_[:, h // 4, :]: e.tensor_tensor(out=o_, in0=o_, in1=m_, op=ALU.add),
                      reads=["bm", "mk"], writes=["bm"])
            qt = [sb("aq%d" % i, [128, 2048], BF16, c) for i in range(2)]
            kt = [sb("ak%d" % i, [128, 4096], BF16, c) for i in range(2)]
            vc = [sb("avc%d" % i, [128, 16, 128], BF16, c) for i in range(2)]
            vp = [sb("avp%d" % i, [128, 16, 128], BF16, c) for i in range(2)]
            num = sb("anum", [128, 2048], F32, c)
            den = sb("aden", [128, 2048], F32, c)
            lg = [sb("alg%d" % i, [128, 512], F32, c) for i in range(4)]
            pt = [sb("apt%d" % i, [128, 512], BF16, c) for i in range(4)]
            ost = [sb("aost%d" % i, [128, 2048], BF16, c) for i in range(2)]
            nld = 0
            npb = 0
            nlg = 0
            nun = 0
            for j in range(A_SLOTS):
                for u in range(NSUP):
                    t0 = u * 2048
                    for g, (win, d) in enumerate(DIL):
                        h = 4 * g + j
                        b = nld % 2
                        nld += 1
                        hr = slice(h * 128, (h + 1) * 128)
                        sc.dma("sp", qt[b][:], pr["aqT"][hr, t0:t0 + 2048], writes=[("aq", b)])
                        sc.dma("sp", kt[b][:, 2048:4096], pr["akT"][hr, t0:t0 + 2048], writes=[("ak", b)])
                        back = {1: 128, 4: 512, 16: 2048}[d]
                        if u > 0:
                            sc.dma("sp", kt[b][:, 4096 - 2048 - back:2048], pr["akT"][hr, t0 - back:t0], writes=[("ak", b)])
                        vsrc = pr["av"][t0:t0 + 2048, hr]
                        if d == 1:
                            sc.dma("sp", vc[b][:], vsrc.rearrange("(n i) e -> i n e", i=128), writes=[("avc", b)])
                        elif d == 4:
                            for m in range(4):
                                sc.dma("sp", vc[b][:, 4 * m:4 * m + 4, :],
                                       vsrc[512 * m:512 * m + 512, :].rearrange("(i r) e -> i r e", r=4), writes=[("avc", b)])
                        else:
                            sc.dma("sp", vc[b][:], vsrc.rearrange("(i r) e -> i r e", r=16), writes=[("avc", b)])
                        if u > 0:
                            vps = pr["av"][t0 - back:t0, hr]
                            if d == 1:
                                sc.dma("sp", vp[b][:, 0, :], vps, writes=[("avp", b)])
                            elif d == 4:
                                sc.dma("sp", vp[b][:, 0:4, :], vps.rearrange("(i r) e -> i r e", r=4), writes=[("avp", b)])
                            else:
                                sc.dma("sp", vp[b][:], vps.rearrange("(i r) e -> i r e", r=16), writes=[("avp", b)])
                        blocks = []
                        if d == 1:
                            for n in range(16):
                                blocks.append((n * 128, n, (n - 1) if n > 0 else None, 0))
                        elif d == 4:
                            for m in range(4):
                                for r in range(4):
                                    blocks.append((m * 512 + r, 4 * m + r, (4 * (m - 1) + r) if m > 0 else None, r))
                        else:
                            for r in range(16):
                                blocks.append((r, r, None, r))
                        for bi in range(0, 16, 2):
                            pb_s = npb % 8
                            npb += 1
                            li = nlg % 4
                            nlg += 1
                            info = []
                            for x in range(2):
                                f, cur_idx, prev_in_cur, r = blocks[bi + x]
                                qv = qt[b][:, f:f + 127 * d + 1:d] if d > 1 else qt[b][:, f:f + 128]
                                kc_ = kt[b][:, 2048 + f:2048 + f + 127 * d + 1:d] if d > 1 else kt[b][:, 2048 + f:2048 + f + 128]
                                fp = 2048 + f - 128 * d
                                has_prev = (u > 0) or (f - 128 * d >= 0)
                                col = x * 256
                                if has_prev:
                                    kp_ = kt[b][:, fp:fp + 127 * d + 1:d] if d > 1 else kt[b][:, fp:fp + 128]
                                    mm_group(sc, ps[pb_s][:, col:col + 128], [(kp_, qv)], reads=[("ak", b), ("aq", b)], writes=[("ps", pb_s)])
                                mm_group(sc, ps[pb_s][:, col + 128:col + 256], [(kc_, qv)], reads=[("ak", b), ("aq", b)], writes=[("ps", pb_s)])
                                if prev_in_cur is not None:
                                    vprev = vc[b][:, prev_in_cur, :]
                                elif has_prev:
                                    vprev = vp[b][:, r if d > 1 else 0, :]
                                else:
                                    vprev = None
                                info.append((f, has_prev, vprev, vc[b][:, cur_idx, :], col))
                            for x in range(2):
                                f, has_prev, vprev, vcur, col = info[x]
                                c0 = col if has_prev else col + 128
                                bc0 = 0 if has_prev else 128
                                sc.op("dve", lambda e, o_=lg[li][:, c0:col + 256], p_=ps[pb_s][:, c0:col + 256], b_=bm[:, h, bc0:256]:
                                      e.scalar_tensor_tensor(out=o_, in0=p_, scalar=scale, in1=b_, op0=ALU.mult, op1=ALU.add),
                                      reads=[("ps", pb_s), "bm"], writes=[("alg", li)])
                                sc.op("act", lambda e, o_=pt[li][:, c0:col + 256], i_=lg[li][:, c0:col + 256]: e.activation(out=o_, in_=i_, func=AF.Exp),
                                      reads=[("alg", li)], writes=[("apt", li)])
                            pb_n = npb % 8
                            npb += 1
                            pb_d = npb % 8
                            npb += 1
                            for x in range(2):
                                f, has_prev, vprev, vcur, col = info[x]
                                prs_n = []
                                prs_d = []
                                if has_prev:
                                    prs_n.append((vprev, pt[li][:, col:col + 128]))
                                    prs_d.append((ones_bf[:], pt[li][:, col:col + 128]))
                                prs_n.append((vcur, pt[li][:, col + 128:col + 256]))
                                prs_d.append((ones_bf[:], pt[li][:, col + 128:col + 256]))
                                mm_group(sc, ps[pb_n][:, x * 128:(x + 1) * 128], prs_n, reads=[("apt", li), ("avc", b), ("avp", b)], writes=[("ps", pb_n)])
                                mm_group(sc, ps[pb_d][:, x * 128:(x + 1) * 128], prs_d, reads=[("apt", li), "ones_bf"], writes=[("ps", pb_d)])
                            for x in range(2):
                                f = info[x][0]
                                nv = num[:, f:f + 127 * d + 1:d] if d > 1 else num[:, f:f + 128]
                                dv = den[:, f:f + 127 * d + 1:d] if d > 1 else den[:, f:f + 128]
                                if g == 0:
                                    sc.op("act", lambda e, o_=nv, i_=ps[pb_n][:, x * 128:(x + 1) * 128]: e.copy(out=o_, in_=i_),
                                          reads=[("ps", pb_n)], writes=["anum"])
                                    sc.op("dve", lambda e, o_=dv, i_=ps[pb_d][:, x * 128:(x + 1) * 128]: e.tensor_copy(out=o_, in_=i_),
                                          reads=[("ps", pb_d)], writes=["aden"])
                                else:
                                    sc.op("dve", lambda e, o_=nv, i_=ps[pb_n][:, x * 128:(x + 1) * 128]: e.tensor_tensor(out=o_, in0=o_, in1=i_, op=ALU.add),
                                          reads=[("ps", pb_n), "anum"], writes=["anum"])
                                    sc.op("dve", lambda e, o_=dv, i_=ps[pb_d][:, x * 128:(x + 1) * 128]: e.tensor_tensor(out=o_, in0=o_, in1=i_, op=ALU.add),
                                          reads=[("ps", pb_d), "aden"], writes=["aden"])
                    ob = nun % 2
                    nun += 1
                    sc.op("act", lambda e: e.activation(out=den[:], in_=den[:], func=AF.Ln), reads=["aden"], writes=["aden"])
                    sc.op("act", lambda e: e.activation(out=den[:], in_=den[:], func=AF.Exp, scale=-1.0), reads=["aden"], writes=["aden"])
                    sc.op("dve", lambda e, o_=ost[ob][:]: e.tensor_tensor(out=o_, in0=num[:], in1=den[:], op=ALU.mult),
                          reads=["anum", "aden"], writes=[("aost", ob)])
                    sc.dma("pool", OA[j * 128:(j + 1) * 128, t0:t0 + 2048], ost[ob][:], reads=[("aost", ob)])
            sc.emit()
        sc.barrier()

    def phase_attn_C(l):
        NSUP = S // 2048
        scale = float(HD) ** -0.5
        with ExitStack() as c:
            ps = psum_banks(c, 8)
            bm = sb("bmC", [128, 12, 256], F32, c)
            mk_ = sb("mkC", [128, 256], F32, c)
            es = sb("esC", [128, 12], F32, c)
            sc.dma("sp", bm[:], biasg[12:24].rearrange("h k q -> k h q"), writes=["bm"])
            sc.dma("sp", mk_[:], maskc[3], writes=["mk"])
            sc.dma("sp", es[:], sinks[l].partition_broadcast(128), writes=["es"])
            for h in range(12):
                sc.op("dve", lambda e, o_=bm[:, h, :]: e.tensor_tensor(out=o_, in0=o_, in1=mk_[:], op=ALU.add),
                      reads=["bm", "mk"], writes=["bm"])
            sc.op("act", lambda e: e.activation(out=es[:], in_=es[:], func=AF.Exp), reads=["es"], writes=["es"])
            qt = [sb("cq%d" % i, [128, 2048], BF16, c) for i in range(2)]
            kt = [sb("ck%d" % i, [128, 2048 + 128], BF16, c) for i in range(2)]
            vt = [sb("cvt%d" % i, [128, 17, 128], BF16, c) for i in range(2)]
            lg = [sb("clg%d" % i, [128, 512], F32, c) for i in range(4)]
            pt = [sb("cpt%d" % i, [128, 512], BF16, c) for i in range(4)]
            dn = [sb("cdn%d" % i, [128, 256], F32, c) for i in range(4)]
            ost = [sb("cost%d" % i, [128, 2048], BF16, c) for i in range(2)]
            nld = 0
            nkv = 0
            npb = 0
            nlg = 0
            for kvh in range(C_KVH):
                for u in range(NSUP):
                    t0 = u * 2048
                    kb = nkv % 2
                    nkv += 1
                    kr = slice(kvh * 128, (kvh + 1) * 128)
                    sc.dma("sp", kt[kb][:, 128:], pr["ckT"][kr, t0:t0 + 2048], writes=[("ck", kb)])
                    sc.dma("sp", vt[kb][:, 1:17, :], pr["cv"][t0:t0 + 2048, kr].rearrange("(n i) e -> i n e", i=128), writes=[("cvt", kb)])
                    if u > 0:
                        sc.dma("sp", kt[kb][:, 0:128], pr["ckT"][kr, t0 - 128:t0], writes=[("ck", kb)])
                        sc.dma("sp", vt[kb][:, 0, :], pr["cv"][t0 - 128:t0, kr], writes=[("cvt", kb)])
                    for gq in range(4):
                        h = kvh * 4 + gq
                        b = nld % 2
                        nld += 1
                        sc.dma("sp", qt[b][:], pr["cqT"][h * 128:(h + 1) * 128, t0:t0 + 2048], writes=[("cq", b)])
                        for bi in range(0, 16, 2):
                            pb_s = npb % 8
                            npb += 1
                            li = nlg % 4
                            nlg += 1
                            info = []
                            for x in range(2):
                                n = bi + x
                                has_prev = (u > 0) or (n > 0)
                                col = x * 256
                                qv = qt[b][:, n * 128:(n + 1) * 128]
                                if has_prev:
                                    mm_group(sc, ps[pb_s][:, col:col + 128], [(kt[kb][:, n * 128:(n + 1) * 128], qv)],
                                             reads=[("ck", kb), ("cq", b)], writes=[("ps", pb_s)])
                                mm_group(sc, ps[pb_s][:, col + 128:col + 256], [(kt[kb][:, (n + 1) * 128:(n + 2) * 128], qv)],
                                         reads=[("ck", kb), ("cq", b)], writes=[("ps", pb_s)])
                                info.append((n, has_prev, col))
                            for x in range(2):
                                n, has_prev, col = info[x]
                                c0 = col if has_prev else col + 128
                                bc0 = 0 if has_prev else 128
                                sc.op("dve", lambda e, o_=lg[li][:, c0:col + 256], p_=ps[pb_s][:, c0:col + 256], b_=bm[:, h, bc0:256]:
                                      e.scalar_tensor_tensor(out=o_, in0=p_, scalar=scale, in1=b_, op0=ALU.mult, op1=ALU.add),
                                      reads=[("ps", pb_s), "bm"], writes=[("clg", li)])
                                sc.op("act", lambda e, o_=pt[li][:, c0:col + 256], i_=lg[li][:, c0:col + 256]: e.activation(out=o_, in_=i_, func=AF.Exp),
                                      reads=[("clg", li)], writes=[("cpt", li)])
                            pb_n = npb % 8
                            npb += 1
                            pb_d = npb % 8
                            npb += 1
                            for x in range(2):
                                n, has_prev, col = info[x]
                                prs_n = []
                                prs_d = []
                                if has_prev:
                                    prs_n.append((vt[kb][:, n, :], pt[li][:, col:col + 128]))
                                    prs_d.append((ones_bf[:], pt[li][:, col:col + 128]))
                                prs_n.append((vt[kb][:, n + 1, :], pt[li][:, col + 128:col + 256]))
                                prs_d.append((ones_bf[:], pt[li][:, col + 128:col + 256]))
                                mm_group(sc, ps[pb_n][:, x * 128:(x + 1) * 128], prs_n, reads=[("cpt", li), ("cvt", kb)], writes=[("ps", pb_n)])
                                mm_group(sc, ps[pb_d][:, x * 128:(x + 1) * 128], prs_d, reads=[("cpt", li), "ones_bf"], writes=[("ps", pb_d)])
                            sc.op("act", lambda e, o_=dn[li][:], i_=ps[pb_d][:, 0:256], s_=es[:, h:h + 1]:
                                  e.activation(out=o_, in_=i_, func=AF.Ln, bias=s_, scale=1.0),
                                  reads=[("ps", pb_d), "es"], writes=[("cdn", li)])
                            sc.op("act", lambda e, o_=dn[li][:]: e.activation(out=o_, in_=o_, func=AF.Exp, scale=-1.0),
                                  reads=[("cdn", li)], writes=[("cdn", li)])
                            sc.op("dve", lambda e, o_=ost[b][:, bi * 128:(bi + 2) * 128], n_=ps[pb_n][:, 0:256], d_=dn[li][:]:
                                  e.tensor_tensor(out=o_, in0=n_, in1=d_, op=ALU.mult),
                                  reads=[("ps", pb_n), ("cdn", li)], writes=[("cost", b)])
                        sc.dma("pool", OC[h * 128:(h + 1) * 128, t0:t0 + 2048], ost[b][:], reads=[("cost", b)])
            sc.emit()
        sc.barrier()

    def phase_gla(l):
        NBLK = S // 128
        with ExitStack() as c:
            ps = psum_banks(c, 8)
            wg_f = sb("wg_f", [16, 512], F32, c)
            wg = sb("wg", [16, 512], BF16, c)
            bgb = sb("bgb", [128, 512], F32, c)
            tri = sb("tri", [128, 128], F32, c)
            upp = sb("upp", [128, 128], F32, c)
            m64 = sb("m64", [CH, CH * B_HEADS], F32, c)
            gn = sb("glan", [128, 1], F32, c)
            sc.dma("sp", wg_f[:], w_gate[l], writes=["wg_f"])
            sc.op("dve", lambda e: e.tensor_copy(out=wg[:], in_=wg_f[:]), reads=["wg_f"], writes=["wg"])
            sc.dma("sp", bgb[:], b_gate[l].partition_broadcast(128), writes=["bgb"])
            sc.dma("sp", tri[:], tri_in, writes=["tri"])
            sc.dma("sp", upp[:], upp_in, writes=["upp"])
            sc.dma("sp", m64[:], m64_in, writes=["m64"])
            sc.dma("sp", gn[:], gla_norm[l].rearrange("(p o) -> p o", o=1), writes=["gn"], slow=True)
            sc.op("act", lambda e: e.mul(out=gn[:], in_=gn[:], mul=float(np.sqrt(128.0))), reads=["gn"], writes=["gn"])
            state = sb("gstate", [CH, B_HEADS, 128], F32, c)
            state_bf = sb("gstate_bf", [CH, B_HEADS, 128], BF16, c)
            sc.op("pool", lambda e: e.memset(state[:], 0.0), writes=[("gstate", hh_) for hh_ in range(B_HEADS)])
            sc.op("pool", lambda e: e.memset(state_bf[:], 0.0), writes=["gstate_bf"])
            NB2 = 3
            bgt = [sb("gbg%d" % i, [16, 128], BF16, c) for i in range(NB2)]
            zt = [sb("gz%d" % i, [128, 512], F32, c) for i in range(NB2)]
            qT = [sb("gq%d" % i, [CH, B_HEADS, 128], BF16, c) for i in range(NB2)]
            kT = [sb("gk%d" % i, [CH, B_HEADS, 128], BF16, c) for i in range(NB2)]
            ktok = [sb("gkt%d" % i, [128, 512], BF16, c) for i in range(NB2)]
            v128 = [sb("gv%d" % i, [128, 1024], BF16, c) for i in range(NB2)]
            v64 = [sb("gw%d" % i, [CH, 2, 1024], BF16, c) for i in range(NB2)]
            brt = [sb("gbr%d" % i, [128, B_HEADS, 128], BF16, c) for i in range(NB2)]
            eq = [sb("geq%d" % i, [CH, B_HEADS, 128], F32, c) for i in range(NB2)]
            ek = [sb("gek%d" % i, [CH, B_HEADS, 128], F32, c) for i in range(NB2)]
            qd = [sb("gqd%d" % i, [CH, B_HEADS, 128], BF16, c) for i in range(NB2)]
            ki = [sb("gki%d" % i, [CH, B_HEADS, 128], BF16, c) for i in range(NB2)]
            ko = [sb("gko%d" % i, [128, 512], BF16, c) for i in range(NB2)]
            ed = [sb("ged%d" % i, [128, 512], F32, c) for i in range(NB2)]
            am = [sb("gam%d" % i, [CH, 512], BF16, c) for i in range(2)]
            oraw = [sb("gor%d" % i, [128, 512], F32, c) for i in range(2)]
            osq = [sb("gos%d" % i, [128, 512], F32, c) for i in range(2)]
            orn = [sb("grn%d" % i, [128, 512], F32, c) for i in range(2)]
            ost = [sb("gost%d" % i, [128, B_HEADS, 128], BF16, c) for i in range(2)]
            npb = [0]

            def nextpb():
                pb = npb[0] % 8
                npb[0] += 1
                return pb

            for blk in range(NBLK):
                b = blk % NB2
                t0 = blk * 128
                tsl = slice(t0, t0 + 128)
                sc.dma("sp", bgt[b][:], pr["bgT"][:, tsl], writes=[("gbg", b)])
                sc.dma("sp", qT[b][:], pr["bqT"][:, tsl].rearrange("(h k) s -> k h s", k=CH), writes=[("gq", b)])
                sc.dma("sp", kT[b][:], pr["bkT"][:, tsl].rearrange("(h k) s -> k h s", k=CH), writes=[("gk", b)])
                sc.dma("sp", ktok[b][:], pr["bk"][tsl, :], writes=[("gkt", b)])
                sc.dma("sp", v128[b][:], pr["bv"][tsl, :], writes=[("gv", b)])
                sc.dma("sp", v64[b][:], pr["bv"][tsl, :].rearrange("(c j) v -> j c v", j=CH), writes=[("gw", b)])
                sc.dma("sp", brt[b][:], pr["brT"][:, tsl].rearrange("(h v) s -> v h s", v=128), writes=[("gbr", b)])
                pz = nextpb()
                mm_group(sc, ps[pz][:, :], [(bgt[b][:], wg[:])], reads=[("gbg", b), "wg"], writes=[("ps", pz)])
                sc.op("dve", lambda e, o_=zt[b][:], p_=ps[pz][:, :]: e.tensor_tensor(out=o_, in0=p_, in1=bgb[:], op=ALU.add),
                      reads=[("ps", pz), "bgb"], writes=[("gz", b)])
                sc.op("act", lambda e, o_=zt[b][:]: e.activation(out=o_, in_=o_, func=AF.Exp, scale=-1.0), reads=[("gz", b)], writes=[("gz", b)])
                sc.op("act", lambda e, o_=zt[b][:]: e.activation(out=o_, in_=o_, func=AF.Ln, bias=1.0), reads=[("gz", b)], writes=[("gz", b)])
                pc = [nextpb(), nextpb()]
                for h in range(B_HEADS):
                    mm_group(sc, ps[pc[h // 4]][0:CH, (h % 4) * 128:(h % 4 + 1) * 128], [(zt[b][:, h * CH:(h + 1) * CH], tri[:])],
                             reads=[("gz", b), "tri"], writes=[("ps", pc[h // 4])])
                pd = nextpb()
                mm_group(sc, ps[pd][:, :], [(upp[:], zt[b][:])], reads=[("gz", b), "upp"], writes=[("ps", pd)])
                for hh in range(2):
                    sc.op("act", lambda e, o_=eq[b][:, 4 * hh:4 * hh + 4, :], p_=ps[pc[hh]][0:CH, :].rearrange("k (h i) -> k h i", i=128):
                          e.activation(out=o_, in_=p_, func=AF.Exp, scale=-1.0 / B_TAU), reads=[("ps", pc[hh])], writes=[("geq", b)])
                    sc.op("act", lambda e, o_=ek[b][:, 4 * hh:4 * hh + 4, :], p_=ps[pc[hh]][0:CH, :].rearrange("k (h i) -> k h i", i=128):
                          e.activation(out=o_, in_=p_, func=AF.Exp, scale=1.0 / B_TAU), reads=[("ps", pc[hh])], writes=[("gek", b)])
                sc.op("act", lambda e, o_=ed[b][:], p_=ps[pd][:, :]: e.activation(out=o_, in_=p_, func=AF.Exp, scale=-1.0 / B_TAU),
                      reads=[("ps", pd)], writes=[("ged", b)])
                sc.op("dve", lambda e, o_=qd[b][:], q_=qT[b][:], e_=eq[b][:]:
                      e.scalar_tensor_tensor(out=o_, in0=q_, scalar=float(B_DK) ** -0.5, in1=e_, op0=ALU.mult, op1=ALU.mult),
                      reads=[("gq", b), ("geq", b)], writes=[("gqd", b)])
                sc.op("dve", lambda e, o_=ki[b][:], k_=kT[b][:], e_=ek[b][:]: e.tensor_tensor(out=o_, in0=k_, in1=e_, op=ALU.mult),
                      reads=[("gk", b), ("gek", b)], writes=[("gki", b)])
                sc.op("dve", lambda e, o_=ko[b][:], k_=ktok[b][:], e_=ed[b][:]: e.tensor_tensor(out=o_, in0=k_, in1=e_, op=ALU.mult),
                      reads=[("gkt", b), ("ged", b)], writes=[("gko", b)])
                ob = blk % 2
                po = nextpb()
                po2 = nextpb()
                pos = (po, po2)
                for cc in range(2):
                    a = (2 * blk + cc) % 2
                    isl = slice(cc * CH, (cc + 1) * CH)
                    pa = nextpb()
                    for h in range(B_HEADS):
                        mm_group(sc, ps[pa][0:CH, h * CH:(h + 1) * CH], [(ki[b][:, h, isl], qd[b][:, h, isl])],
                                 reads=[("gki", b), ("gqd", b)], writes=[("ps", pa)])
                    sc.op("dve", lambda e, o_=am[a][:], p_=ps[pa][0:CH, :]: e.tensor_tensor(out=o_, in0=p_, in1=m64[:], op=ALU.mult),
                          reads=[("ps", pa), "m64"], writes=[("gam", a)])
                    for h in range(B_HEADS):
                        mm_group(sc, ps[pos[cc]][:, h * CH:(h + 1) * CH],
                                 [(v64[b][:, cc, h * 128:(h + 1) * 128], am[a][:, h * CH:(h + 1) * CH]),
                                  (state_bf[:, h, :], qd[b][:, h, isl])],
                                 reads=[("gw", b), ("gam", a), "gstate_bf", ("gqd", b)], writes=[("ps", pos[cc])])
                    pk = [nextpb(), nextpb()]
                    for h in range(B_HEADS):
                        mm_group(sc, ps[pk[h // 4]][0:CH, (h % 4) * 128:(h % 4 + 1) * 128],
                                 [(ko[b][cc * CH:(cc + 1) * CH, h * CH:(h + 1) * CH], v128[b][cc * CH:(cc + 1) * CH, h * 128:(h + 1) * 128])],
                                 reads=[("gko", b), ("gv", b)], writes=[("ps", pk[h // 4])])
                    for h in range(B_HEADS):
                        sc.op("dve", lambda e, o_=state[:, h, :], d_=eq[b][:, h, cc * CH + CH - 1:cc * CH + CH],
                              p_=ps[pk[h // 4]][0:CH, (h % 4) * 128:(h % 4 + 1) * 128]:
                              e.scalar_tensor_tensor(out=o_, in0=o_, scalar=d_, in1=p_, op0=ALU.mult, op1=ALU.add),
                              reads=[("geq", b), ("ps", pk[h // 4])], writes=[("gstate", h)])
                    sc.op("act", lambda e: e.copy(out=state_bf[:], in_=state[:]),
                          reads=[("gstate", hh_) for hh_ in range(B_HEADS)], writes=["gstate_bf"])
                for cc in range(2):
                    a = (2 * blk + cc) % 2
                    sc.op("act", lambda e, o_=oraw[a][:], p_=ps[pos[cc]][:, :]: e.copy(out=o_, in_=p_), reads=[("ps", pos[cc])], writes=[("gor", a)])
                    sc.op("pool", lambda e, o_=osq[a][:], i_=oraw[a][:]: e.tensor_tensor(out=o_, in0=i_, in1=i_, op=ALU.mult),
                          reads=[("gor", a)], writes=[("gos", a)])
                    pn = nextpb()
                    mm_group(sc, ps[pn][:, :], [(ones_f[:], osq[a][:])], reads=[("gos", a), "ones_f"], writes=[("ps", pn)])
                    sc.op("act", lambda e, o_=orn[a][:], p_=ps[pn][:, :]: e.activation(out=o_, in_=p_, func=AF.Ln, bias=128.0 * EPS),
                          reads=[("ps", pn)], writes=[("grn", a)])
                    sc.op("act", lambda e, o_=orn[a][:]: e.activation(out=o_, in_=o_, func=AF.Exp, scale=-0.5), reads=[("grn", a)], writes=[("grn", a)])
                    sc.op("dve", lambda e, o_=oraw[a][:], r_=orn[a][:]: e.tensor_tensor(out=o_, in0=o_, in1=r_, op=ALU.mult),
                          reads=[("gor", a), ("grn", a)], writes=[("gor", a)])
                    sc.op("dve", lambda e, o_=ost[ob][:, :, cc * CH:(cc + 1) * CH], i_=oraw[a][:].rearrange("v (h i) -> v h i", i=CH),
                          g_=brt[b][:, :, cc * CH:(cc + 1) * CH]:
                          e.scalar_tensor_tensor(out=o_, in0=i_, scalar=gn[:, 0:1], in1=g_, op0=ALU.mult, op1=ALU.mult),
                          reads=[("gor", a), "gn", ("gbr", b)], writes=[("gost", ob)])
                sc.dma("pool", OB[:, tsl].rearrange("(h v) s -> v h s", v=128), ost[ob][:], reads=[("gost", ob)])
            sc.emit()
        sc.barrier()

    sc.emit()
    sc.barrier()
    xcur = xT
    for l in range(NL):
        if l == 0:
            phase_norm(xcur, None, None, None, g_in["g_pre_mix"][l], XN)
        phase_dense(XN, KC, Wb_in[l], NIT, 512, h_inproj(l),
                    lambda j, l=l: pieces_of(w_in[l], in_tiles[j][1], in_tiles[j][2], KC))
        phase_attn_A(l)
        phase_gla(l)
        phase_attn_C(l)
        phase_merge(l)
        phase_dense(MT, KC, Wb_out[l], D // 512, 512, h_f32out(YT, 512),
                    lambda j, l=l: pieces_of(w_out[l], 512 * j, 512, KC))
        phase_norm(xcur, YT, g_in["g_post_mix"][l], XA, g_in["g_pre_ffn"][l], XN)
        phase_dense(XN, KC, Wb_up[l], NUT, 512, h_ffn_up(l),
                    lambda j, l=l: pieces_of(w_up[l], 256 * j, 256, KC, 0, 0) + pieces_of(w_up[l], DFF + 256 * j, 256, KC, 0, 256))
        phase_dense(HT, KF, Wb_dn[l], D // 128, 128, h_f32out(YT, 128),
                    lambda j, l=l: pieces_of(w_down[l], 128 * j, 128, KF), nbuf_in=1)
        last = (l == NL - 1)
        xdst = outT if last else XB
        phase_norm(XA, YT, g_in["g_post_ffn"][l], xdst, None if last else g_in["g_pre_mix"][l + 1], None if last else XN)
        xcur = XB
    sc.emit(final=True)
    return P


def make_inputs(cfg, inputs, b):
    bucket, maskc = host_tables()
    rel_bias = np.asarray(inputs["rel_bias"], np.float32)
    biasg = np.empty((24, 128, 256), np.float32)
    for h in range(24):
        kind = (h // 4) if h < 12 else 3
        biasg[h] = rel_bias[bucket[kind], h]
    tri, upp, m64 = gla_masks()
    m = {
        "xT": np.ascontiguousarray(np.asarray(inputs["x"][b], np.float32).T),
        "biasg": biasg, "maskc": maskc, "gla_tri": tri, "gla_upp": upp, "gla_m64": m64,
    }
    for k in ("w_in", "w_gla_gate", "b_gla_gate", "gla_norm", "attn_sinks", "w_br_a", "w_br_b", "w_br_c", "w_out",
              "g_pre_mix", "g_post_mix", "g_pre_ffn", "g_post_ffn", "w_up", "conv_w", "conv_b", "w_down"):
        m[k] = np.ascontiguousarray(np.asarray(inputs[k], np.float32))
    return m


_PROG_CACHE = {}


def run(cfg, inputs):
    key = (cfg.D, cfg.S, cfg.DFF, cfg.NL, cfg.NB, cfg.debug)
    if key not in _PROG_CACHE:
        _PROG_CACHE[key] = build(cfg)
    P = _PROG_CACHE[key]
    in_maps = [make_inputs(cfg, inputs, b) for b in range(cfg.NB)]
    res = run_bass_kernel_spmd(P.nc, in_maps, core_ids=list(range(cfg.NB)))
    return P, res


def kernel(**inputs):
    x = np.asarray(inputs["x"])
    B, S, D = x.shape
    cfg = Cfg(D=D, S=S, DFF=np.asarray(inputs["w_down"]).shape[1], NL=np.asarray(inputs["w_in"]).shape[0], NB=B)
    P, res = run(cfg, inputs)
    out = np.stack([np.ascontiguousarray(res.results[b]["outT"].T) for b in range(B)], axis=0)
    return out.astype(np.float32)
```
